# Optimizing a Trainium2 kernel written in Bass

```python
import math
import jax, jax.numpy as jnp
from jax import lax
import numpy as np

D_MODEL = 1024
BATCH = 4
SEQ = 8192
DEPTH = 1
DEC_BATCH = 2
DEC_SEQ = 8192
PAST_LEN = 128

HEAD_DIM = 64
ATTN_GROUPS = ((128, 1), (512, 4), (2048, 16))
N_ATTN_GROUPS = len(ATTN_GROUPS)
HEADS_PER_GROUP = D_MODEL // 128
D_ATTN = HEADS_PER_GROUP * HEAD_DIM
QKV_W = N_ATTN_GROUPS * HEADS_PER_GROUP * HEAD_DIM
BLOCK_Q = 128
ROPE_THETA = 10000.0
D_SSM = D_MODEL // 2
SSM_GROUP_CH = 16
SSM_GROUPS = D_SSM // SSM_GROUP_CH
SSM_STATE = 64
N_DIR = 2
IN_COLS = 3 * QKV_W + D_ATTN + 2 * D_SSM + 2 * D_MODEL
EPS = 1e-6
NEG = -1e30

kernel_name = "hybrid_dilated_attn_s5_encoder"


def rms_norm(x, g):
    xf = x.astype(jnp.float32)
    xf = xf * lax.rsqrt(jnp.mean(xf * xf, axis=-1, keepdims=True) + EPS)
    return (xf * g.astype(jnp.float32)).astype(x.dtype)


def rope(t, pos):
    half = HEAD_DIM // 2
    inv = jnp.float32(ROPE_THETA) ** (-jnp.arange(half, dtype=jnp.float32) / half)
    ang = pos[:, None] * inv[None, :]
    cos = jnp.cos(ang)[None, :, None, None, :]
    sin = jnp.sin(ang)[None, :, None, None, :]
    tf = t.astype(jnp.float32)
    t1, t2 = tf[..., :half], tf[..., half:]
    return jnp.concatenate([t1 * cos - t2 * sin, t2 * cos + t1 * sin], axis=-1).astype(t.dtype)


def dilated_attention(q, k, v):
    B_, L = q.shape[0], q.shape[1]
    dt = q.dtype
    scale = HEAD_DIM ** -0.5
    padded = []
    for g, (w, d) in enumerate(ATTN_GROUPS):
        half = w // (2 * d)
        pad = ((0, 0), (half * d, half * d), (0, 0), (0, 0))
        padded.append((jnp.pad(k[:, :, g], pad), jnp.pad(v[:, :, g], pad)))

    def block(s):
        qb = lax.dynamic_slice_in_dim(q, s, BLOCK_Q, axis=1)
        qpos = s + jnp.arange(BLOCK_Q)
        outs, lses = [], []
        for g, (w, d) in enumerate(ATTN_GROUPS):
            half = w // (2 * d)
            nk = 2 * half + 1
            span = BLOCK_Q + 2 * half * d
            kp, vp = padded[g]
            ks = lax.dynamic_slice_in_dim(kp, s, span, axis=1)
            vs = lax.dynamic_slice_in_dim(vp, s, span, axis=1)
            idx = jnp.arange(BLOCK_Q)[:, None] + d * jnp.arange(nk)[None, :]
            kg = ks[:, idx]
            vg = vs[:, idx]
            kpos = qpos[:, None] + d * (jnp.arange(nk)[None, :] - half)
            valid = (kpos >= 0) & (kpos < L)
            logits = jnp.einsum('bqhd,bqkhd->bhqk', qb[:, :, g].astype(jnp.float32),
                                kg.astype(jnp.float32)) * scale
            logits = jnp.where(valid[None, None], logits, NEG)
            m = jnp.max(logits, axis=-1, keepdims=True)
            p = jnp.exp(logits - m)
            den = jnp.sum(p, axis=-1)
            o = jnp.einsum('bhqk,bqkhd->bqhd', p, vg.astype(jnp.float32))
            o = o / jnp.transpose(den, (0, 2, 1))[..., None]
            outs.append(o)
            lses.append(m[..., 0] + jnp.log(den))
        alpha = jax.nn.softmax(jnp.stack(lses, axis=0), axis=0)
        alpha = jnp.transpose(alpha, (0, 1, 3, 2))[..., None]
        return jnp.sum(alpha * jnp.stack(outs, axis=0), axis=0).astype(dt)

    starts = jnp.arange(L // BLOCK_Q, dtype=jnp.int32) * BLOCK_Q
    res = lax.map(block, starts)
    return jnp.transpose(res, (1, 0, 2, 3, 4)).reshape(B_, L, D_ATTN)


def _scan_op(e1, e2):
    a1, b1 = e1
    a2, b2 = e2
    return a1 * a2, a2 * b1 + b2


def s5_bidirectional(u, lam_re, lam_im, log_step, b_re, b_im, c_re, c_im, d_skip):
    B_, L, _ = u.shape
    uf = u.astype(jnp.float32)
    ug = uf.reshape(B_, L, SSM_GROUPS, SSM_GROUP_CH).astype(jnp.complex64)
    y = (d_skip.astype(jnp.float32) * uf).reshape(B_, L, SSM_GROUPS, SSM_GROUP_CH)
    for r in range(N_DIR):
        lam = lax.complex(lam_re[r].astype(jnp.float32), lam_im[r].astype(jnp.float32))
        step = jnp.exp(log_step[r].astype(jnp.float32))[:, None]
        lam_bar = jnp.exp(lam * step)
        bmat = lax.complex(b_re[r].astype(jnp.float32), b_im[r].astype(jnp.float32))
        b_bar = ((lam_bar - 1.0) / lam)[..., None] * bmat
        bu = jnp.einsum('blgc,gnc->lbgn', ug, b_bar)
        a = jnp.broadcast_to(lam_bar[None, None], (L, 1, SSM_GROUPS, SSM_STATE))
        _, h = lax.associative_scan(_scan_op, (a, bu), reverse=(r == 1), axis=0)
        cmat = lax.complex(c_re[r].astype(jnp.float32), c_im[r].astype(jnp.float32))
        y = y + jnp.real(jnp.einsum('lbgn,gcn->blgc', h, cmat))
    return y.reshape(B_, L, D_SSM).astype(u.dtype)


def encoder_layer(x, norm_g, w_in, lam_re, lam_im, log_step, b_re, b_im, c_re, c_im,
                  d_skip, w_glu, w_branch_a, w_branch_b, w_out):
    B_, L, _ = x.shape
    dt = x.dtype
    h = rms_norm(x, norm_g)
    proj = h @ w_in
    offs = np.cumsum([QKV_W, QKV_W, QKV_W, D_ATTN, D_SSM, D_SSM, D_MODEL]).tolist()
    q, k, v, z_a, u, z_b, g_a, g_b = jnp.split(proj, offs, axis=-1)
    shp = (B_, L, N_ATTN_GROUPS, HEADS_PER_GROUP, HEAD_DIM)
    pos = jnp.arange(L, dtype=jnp.float32)
    q = rope(q.reshape(shp), pos)
    k = rope(k.reshape(shp), pos)
    v = v.reshape(shp)
    attn = dilated_attention(q, k, v)
    a = (attn * jax.nn.silu(z_a)) @ w_branch_a
    s = jax.nn.gelu(s5_bidirectional(u, lam_re, lam_im, log_step, b_re, b_im, c_re, c_im, d_skip))
    s = s * jax.nn.sigmoid(s @ w_glu)
    b = (s * jax.nn.silu(z_b)) @ w_branch_b
    merged = jax.nn.sigmoid(g_a) * a + jax.nn.sigmoid(g_b) * b
    return (x + merged @ w_out).astype(dt)


def trunk(x, norm_g, w_in, lam_re, lam_im, log_step, b_re, b_im, c_re, c_im,
          d_skip, w_glu, w_branch_a, w_branch_b, w_out, final_g):
    for i in range(DEPTH):
        x = encoder_layer(x, norm_g[i], w_in[i], lam_re[i], lam_im[i], log_step[i],
                          b_re[i], b_im[i], c_re[i], c_im[i], d_skip[i], w_glu[i],
                          w_branch_a[i], w_branch_b[i], w_out[i])
    return rms_norm(x, final_g)


def setup_inputs(seed: int = 0) -> dict:
    key = jax.random.key(seed)
    ks = jax.random.split(key, 20)
    f32 = jnp.float32
    G, N, C = SSM_GROUPS, SSM_STATE, SSM_GROUP_CH
    nrm = lambda k, s, sc: jax.random.normal(k, s, f32) * sc
    x_prompt = jax.random.normal(ks[0], (BATCH, SEQ, D_MODEL), f32)
    x_sample = jax.random.normal(ks[1], (DEC_BATCH, DEC_SEQ, D_MODEL), f32)
    norm_g = 1.0 + nrm(ks[2], (DEPTH, D_MODEL), 0.02)
    w_in = nrm(ks[3], (DEPTH, D_MODEL, IN_COLS), D_MODEL ** -0.5)
    lam_re = -0.5 + nrm(ks[4], (DEPTH, N_DIR, G, N), 0.01)
    lam_im = math.pi * jnp.arange(N, dtype=f32) + nrm(ks[5], (DEPTH, N_DIR, G, N), 0.01)
    log_step = jax.random.uniform(ks[6], (DEPTH, N_DIR, G), f32,
                                  math.log(1e-3), math.log(1e-1))
    b_re = nrm(ks[7], (DEPTH, N_DIR, G, N, C), (2.0 * C) ** -0.5)
    b_im = nrm(ks[8], (DEPTH, N_DIR, G, N, C), (2.0 * C) ** -0.5)
    c_re = nrm(ks[9], (DEPTH, N_DIR, G, C, N), (2.0 * N) ** -0.5 * 4.0)
    c_im = nrm(ks[10], (DEPTH, N_DIR, G, C, N), (2.0 * N) ** -0.5 * 4.0)
    d_skip = nrm(ks[11], (DEPTH, D_SSM), 1.0)
    w_glu = nrm(ks[12], (DEPTH, D_SSM, D_SSM), D_SSM ** -0.5)
    w_branch_a = nrm(ks[13], (DEPTH, D_ATTN, D_MODEL), D_ATTN ** -0.5)
    w_branch_b = nrm(ks[14], (DEPTH, D_SSM, D_MODEL), D_SSM ** -0.5)
    w_out = nrm(ks[15], (DEPTH, D_MODEL, D_MODEL), D_MODEL ** -0.5)
    final_g = 1.0 + nrm(ks[16], (D_MODEL,), 0.02)
    return {"x_prompt": x_prompt, "x_sample": x_sample, "norm_g": norm_g, "w_in": w_in,
            "lam_re": lam_re, "lam_im": lam_im, "log_step": log_step,
            "b_re": b_re, "b_im": b_im, "c_re": c_re, "c_im": c_im,
            "d_skip": d_skip, "w_glu": w_glu, "w_branch_a": w_branch_a,
            "w_branch_b": w_branch_b, "w_out": w_out, "final_g": final_g}


def reference(x_prompt, x_sample, norm_g, w_in, lam_re, lam_im, log_step, b_re, b_im,
              c_re, c_im, d_skip, w_glu, w_branch_a, w_branch_b, w_out, final_g):
    y_prompt = trunk(x_prompt, norm_g, w_in, lam_re, lam_im, log_step, b_re, b_im, c_re, c_im,
                     d_skip, w_glu, w_branch_a, w_branch_b, w_out, final_g)
    y_sample = trunk(x_sample, norm_g, w_in, lam_re, lam_im, log_step, b_re, b_im, c_re, c_im,
                     d_skip, w_glu, w_branch_a, w_branch_b, w_out, final_g)
    return (y_prompt, y_sample)
```

```python
import numpy as np
import ml_dtypes
import concourse.bass as bass
import concourse.mybir as mybir
from concourse.bass_utils import run_bass_kernel_spmd

F32 = mybir.dt.float32
BF16 = mybir.dt.bfloat16
ALU = mybir.AluOpType
AF = mybir.ActivationFunctionType

L = 8192
D = 1024
NCORES = 8
EPS = 1e-6
EPOCH = 30000


class Tk:
    __slots__ = ("w", "rs", "dsem", "dcnt", "name")

    def __init__(self, name=""):
        self.w = {}
        self.rs = {}
        self.dsem = None
        self.dcnt = 0
        self.name = name


class Prog:
    ENG = ("pe", "act", "dve", "pool", "sp")

    def __init__(self, nc):
        self.nc = nc
        self.ops = {e: [] for e in self.ENG}
        self.cnt = {e: 0 for e in self.ENG}
        self.esems = {e: [] for e in self.ENG}
        self.waited = {e: {} for e in self.ENG}
        self.nsem = 0
        self.dtiles = []

    _uid = [0]

    def newsem(self, name):
        self.nsem += 1
        Prog._uid[0] += 1
        return self.nc.alloc_semaphore("%s_%d" % (name, Prog._uid[0]))

    def _esem(self, e, idx):
        lst = self.esems[e]
        while len(lst) <= idx:
            lst.append(self.newsem("es_%s_%d" % (e, len(lst))))
        return lst[idx]

    def _collect(self, e, reads, writes):
        need = {}
        for t in reads:
            for s, v in t.w.items():
                if need.get(s, 0) < v:
                    need[s] = v
        for t in writes:
            for s, v in t.w.items():
                if need.get(s, 0) < v:
                    need[s] = v
            for s, v in t.rs.items():
                if need.get(s, 0) < v:
                    need[s] = v
        waits = []
        wd = self.waited[e]
        own = set(id(s) for s in self.esems[e]) if e == "pe" else ()
        for s, v in need.items():
            if id(s) in own:
                continue
            if wd.get(s, 0) >= v:
                continue
            wd[s] = v
            waits.append((s, v))
        return waits

    def op(self, e, fn, reads=(), writes=()):
        waits = self._collect(e, reads, writes)
        seq = self.cnt[e]
        self.cnt[e] += 1
        sem = self._esem(e, seq // EPOCH)
        val = seq % EPOCH + 1
        self.ops[e].append((waits, fn, sem, 1))
        for t in writes:
            t.w = {sem: val}
            t.rs = {}
        for t in reads:
            if t.rs.get(sem, 0) < val:
                t.rs[sem] = val

    def dma(self, e, out, in_, sb, reads=(), writes=()):
        waits = self._collect(e, reads, writes)
        if sb.dsem is None:
            sb.dsem = self.newsem("ds_" + sb.name)
            self.dtiles.append(sb)
        sb.dcnt += 16
        sem, val = sb.dsem, sb.dcnt
        self.ops[e].append((waits, lambda eng: eng.dma_start(out=out, in_=in_), sem, 16))
        for t in writes:
            if t is sb:
                t.w = {sem: val}
                t.rs = {}
            else:
                t.w[sem] = val
        for t in reads:
            if t.rs.get(sem, 0) < val:
                t.rs[sem] = val

    def barrier(self):
        ev = {}
        for e in self.ENG:
            n = self.cnt[e]
            if n > 0:
                ev[self.esems[e][(n - 1) // EPOCH]] = (n - 1) % EPOCH + 1
        for t in self.dtiles:
            ev[t.dsem] = t.dcnt
        for e in self.ENG:
            wd = self.waited[e]
            own = set(id(x) for x in self.esems[e])
            waits = []
            for sm, v in ev.items():
                if id(sm) in own or wd.get(sm, 0) >= v:
                    continue
                wd[sm] = v
                waits.append((sm, v))
            if waits:
                self.ops[e].append((waits, None, None, 0))

    def emit(self, final_tiles):
        nc = self.nc
        fin = {}
        for t in final_tiles:
            for s, v in list(t.w.items()) + list(t.rs.items()):
                if fin.get(s, 0) < v:
                    fin[s] = v
        with nc.Block() as block:
            def body(e):
                def run(eng):
                    for waits, fn, sem, inc in self.ops[e]:
                        for s, v in waits:
                            eng.wait_ge(s, v)
                        if fn is not None:
                            fn(eng).then_inc(sem, inc)
                    if e == "sp":
                        for s, v in fin.items():
                            eng.wait_ge(s, v)
                return run
            block.tensor(body("pe"))
            block.scalar(body("act"))
            block.vector(body("dve"))
            block.gpsimd(body("pool"))
            block.sync(body("sp"))


def ssl(start, count, step):
    return slice(start, start + (count - 1) * step + 1, step)


def build(debug=None):
    nc = bass.Bass("TRN2", target_bir_lowering=False)
    P = Prog(nc)

    def din(name, shape, dt=F32):
        return nc.dram_tensor(name, list(shape), dt, kind="ExternalInput").ap()

    dbg_outs = {}

    def dscr(name, shape, dt=BF16):
        kind = "ExternalOutput" if (debug and name in debug) else "Internal"
        if debug and ("in_" + name) in debug:
            kind = "ExternalInput"
        t = nc.dram_tensor(name, list(shape), dt, kind=kind).ap()
        if kind == "ExternalOutput":
            dbg_outs[name] = t
        return t

    dbg_tiles = []

    def dump(name, ap, tk, dt):
        if not (debug and "dumps" in debug):
            return
        t = nc.dram_tensor("dbg_" + name, list(ap.shape), dt, kind="ExternalOutput").ap()
        dbg_outs["dbg_" + name] = t
        dt_ = Tk("dbgd_" + name)
        dbg_tiles.append(dt_)
        P.dma("sp", t, ap, tk, reads=[tk], writes=[dt_])

    x = din("x", [L, D])
    w_in = din("w_in", [D, 8192])
    norm_g = din("norm_g", [128, 8])
    ident_in = din("ident", [128, 128])
    cos_in = din("cosT", [128, L])
    sin_in = din("sinT", [128, L])
    mask_in = din("mask01", [128, 256])
    jm_in = din("jm", [128, 128])
    jp_in = din("jp", [128, 128])
    lamr_in = din("lamr_q", [128, 64])
    lami_in = din("lami_q", [128, 64])
    lstep_in = din("lstep_q", [128, 64])
    ex_in = din("ex_tab", [128, 2, 32])
    sg_in = din("sg_tab", [128, 2])
    ba_in = din("ba_q", [128, 64, 16])
    bb_in = din("bb_q", [128, 64, 16])
    ca_in = din("ca_q", [128, 64, 16])
    cb_in = din("cb_q", [128, 64, 16])
    dcol_in = din("dcol", [128, 32])
    cmask_in = din("cmask", [128, 2, 128])
    wa_in = din("w_a", [512, 1024])
    wb_in = din("w_b", [512, 1024])
    wglu_in = din("w_glu", [512, 512])
    wout_in = din("w_out", [1024, 1024])
    fg_in = din("final_g", [1024])
    y = nc.dram_tensor("y", [L, D], F32, kind="ExternalOutput").ap()

    QT = dscr("QT", [2, 24, 32, L])
    KT = dscr("KT", [2, 24, 32, L])
    Vd = dscr("Vd", [L, 1536])
    ZA = dscr("ZA", [512, L])
    Ud = dscr("Ud", [4, 8, 128, 1024])
    ZB = dscr("ZB", [512, L])
    GA = dscr("GA", [1024, L])
    GB = dscr("GB", [1024, L])
    tQT, tKT, tVd, tZA, tUd, tZB, tGA, tGB = [Tk(n) for n in
                                              ("QT", "KT", "Vd", "ZA", "Ud", "ZB", "GA", "GB")]

    def sb(name, shape, dt):
        return nc.alloc_sbuf_tensor(name, list(shape), dt).ap()

    def ps(name, shape, dt=F32):
        return nc.alloc_psum_tensor(name, list(shape), dt).ap()

    ident_f = sb("ident_f", [128, 128], F32)
    ident_b = sb("ident_b", [128, 128], BF16)
    gcol = sb("gcol", [128, 8], F32)
    t_ident_f, t_ident_b, t_gcol = Tk("identf"), Tk("identb"), Tk("gcol")
    def load_consts():
        P.dma("sp", ident_f, ident_in, t_ident_f, writes=[t_ident_f])
        P.dma("sp", gcol, norm_g, t_gcol, writes=[t_gcol])
        P.op("dve", lambda e: e.tensor_copy(out=ident_b, in_=ident_f), reads=[t_ident_f], writes=[t_ident_b])
    load_consts()

    psb = [ps("psb%d" % i, [128, 512]) for i in range(7)]
    tps = [Tk("psb%d" % i) for i in range(7)]
    pst = ps("pst", [128, 1024], BF16)
    tpst = Tk("pst")

    from contextlib import ExitStack

    def sbs(st, name, shape, dt):
        h = st.enter_context(nc.sbuf_tensor(name, list(shape), dt))
        return h.ap() if hasattr(h, "ap") else h[:]

    NTT = L // 128
    NB = L // 512
    w_in_v = w_in.rearrange("(k p) c -> p k c", p=128)
    st12 = ExitStack()
    hT = sbs(st12, "hT", [128, 8, L], BF16)
    thT = [Tk("hT%d" % i) for i in range(NB)]

    with ExitStack() as st:
        xt = [sbs(st, "xt%d" % i, [128, D], F32) for i in range(2)]
        txt = [Tk("xt%d" % i) for i in range(2)]
        xn = [sbs(st, "xn%d" % i, [128, D], BF16) for i in range(2)]
        txn = [Tk("xn%d" % i) for i in range(2)]
        junk = sbs(st, "junk", [128, D], BF16)
        tjunk = Tk("junk")
        ss = [sbs(st, "ss%d" % i, [128, 1], F32) for i in range(2)]
        tss = [Tk("ss%d" % i) for i in range(2)]
        rs_ = [sbs(st, "rs%d" % i, [128, 1], F32) for i in range(2)]
        trs = [Tk("rs%d" % i) for i in range(2)]
        for tt in range(NTT):
            b = tt % 2
            P.dma("sp", xt[b], x[tt * 128:(tt + 1) * 128, :], txt[b], writes=[txt[b]])
            P.op("act", lambda e, b=b: e.activation(out=junk, in_=xt[b], func=AF.Square, accum_out=ss[b]),
                 reads=[txt[b]], writes=[tjunk, tss[b]])
            P.op("dve", lambda e, b=b: e.tensor_scalar(out=rs_[b], in0=ss[b], scalar1=1.0 / D, scalar2=EPS,
                                                       op0=ALU.mult, op1=ALU.add),
                 reads=[tss[b]], writes=[trs[b]])
            P.op("act", lambda e, b=b: e.activation(out=rs_[b], in_=rs_[b], func=AF.Ln),
                 reads=[trs[b]], writes=[trs[b]])
            P.op("act", lambda e, b=b: e.activation(out=rs_[b], in_=rs_[b], func=AF.Exp, scale=-0.5),
                 reads=[trs[b]], writes=[trs[b]])
            P.op("act", lambda e, b=b: e.activation(out=xn[b], in_=xt[b], func=AF.Copy, scale=rs_[b]),
                 reads=[txt[b], trs[b]], writes=[txn[b]])
            for k in range(8):
                P.op("pe", lambda e, b=b, k=k: e.transpose(out=pst[:, k * 128:(k + 1) * 128],
                                                          in_=xn[b][:, k * 128:(k + 1) * 128], identity=ident_b),
                     reads=[txn[b], t_ident_b], writes=[tpst])
            P.op("dve", lambda e, tt=tt: e.tensor_copy(out=hT[:, :, tt * 128:(tt + 1) * 128],
                                                       in_=pst.rearrange("p (k t) -> p k t", k=8)),
                 reads=[tpst], writes=[thT[tt // 4]])
        P.barrier()

    with ExitStack() as st:
        wst = [sbs(st, "wst%d" % i, [128, 8, 256], F32) for i in range(1)]
        twst = [Tk("wst%d" % i) for i in range(1)]
        wcnt = [0]

        def load_w(dst_ap, tdst, c0, n, perm=False):
            for j in range(n // 256):
                b = wcnt[0] % len(wst)
                wcnt[0] += 1
                P.dma("sp", wst[b], w_in_v[:, :, c0 + j * 256:c0 + (j + 1) * 256], twst[b], writes=[twst[b]])
                for k in range(8):
                    if perm:
                        o_ap = dst_ap[:, k, :].rearrange("p (two h i) -> p h two i", two=2, i=32)[:, 4 * j:4 * j + 4]
                        i_ap = wst[b][:, k, :].rearrange("p (h two i) -> p h two i", two=2, i=32)
                    else:
                        o_ap = dst_ap[:, k, j * 256:(j + 1) * 256]
                        i_ap = wst[b][:, k, :]
                    P.op("dve", lambda e, o_ap=o_ap, i_ap=i_ap, k=k: e.tensor_scalar(
                        out=o_ap, in0=i_ap, scalar1=gcol[:, k:k + 1], scalar2=None, op0=ALU.mult),
                        reads=[twst[b], t_gcol], writes=[tdst])

        wbig = sbs(st, "wbig", [128, 8, 1536], BF16)
        twbig = Tk("wbig")
        wg = [sbs(st, "wg%d" % i, [128, 8, 512], BF16) for i in range(2)]
        twg = [Tk("wg%d" % i) for i in range(2)]
        stg = [sbs(st, "stg%d" % i, [128, 512], BF16) for i in range(3)]
        tstg = [Tk("stg%d" % i) for i in range(3)]

        def zug_gen():
            jobs = [(4608, "za", 0), (5632, "zb", 0), (5120, "u", 0), (6144, "ga", 0), (6656, "ga", 512),
                    (7168, "gb", 0), (7680, "gb", 512)]
            ZBK = (4, 5, 6)
            sc = 0
            zc = 0
            for ji, (c0, kind, roff) in enumerate(jobs):
                wb_ = ji % 2
                load_w(wg[wb_], twg[wb_], c0, 512)
                for sub in range(4):
                    for tb in range(NB):
                        pi = ZBK[zc % 3]
                        zc += 1
                        tsl = slice(tb * 512, (tb + 1) * 512)
                        for k in range(8):
                            P.op("pe", lambda e, pi=pi, k=k, wb_=wb_, sub=sub, tsl=tsl: e.matmul(
                                psb[pi], lhsT=wg[wb_][:, k, sub * 128:(sub + 1) * 128], rhs=hT[:, k, tsl],
                                start=(k == 0), stop=(k == 7)),
                                reads=[twg[wb_], thT[tb]], writes=[tps[pi]])
                        s_ = sc % 3
                        sc += 1
                        rows = slice(roff + sub * 128, roff + (sub + 1) * 128)
                        if kind == "u":
                            P.op("act", lambda e, pi=pi, s_=s_: e.activation(
                                out=stg[s_].rearrange("p (t c) -> p t c", t=8),
                                in_=psb[pi].rearrange("p (c t) -> p t c", t=8), func=AF.Copy),
                                reads=[tps[pi]], writes=[tstg[s_]])
                            P.dma("act", Ud[sub, :, :, tb * 64:(tb + 1) * 64].rearrange("t p c -> p t c"),
                                  stg[s_].rearrange("p (t c) -> p t c", t=8),
                                  tstg[s_], reads=[tstg[s_]], writes=[tUd])
                        else:
                            fn = AF.Silu if kind in ("za", "zb") else AF.Sigmoid
                            dd, td = {"za": (ZA, tZA), "zb": (ZB, tZB), "ga": (GA, tGA), "gb": (GB, tGB)}[kind]
                            P.op("act", lambda e, pi=pi, s_=s_, fn=fn: e.activation(out=stg[s_], in_=psb[pi], func=fn),
                                 reads=[tps[pi]], writes=[tstg[s_]])
                            P.dma("act", dd[rows, tsl], stg[s_], tstg[s_], reads=[tstg[s_]], writes=[td])
                        yield

        zg = zug_gen()
        st2a = ExitStack()
        cs = [sbs(st2a, "cs%d" % i, [128, 2, 512], F32) for i in range(2)]
        tcs = [Tk("cs%d" % i) for i in range(2)]
        rtmp = [sbs(st2a, "rtmp%d" % i, [128, 512], F32) for i in range(4)]
        trtmp = [Tk("rtmp%d" % i) for i in range(4)]
        ro = [sbs(st2a, "ro%d" % i, [128, 2, 512], BF16) for i in range(3)]
        tro = [Tk("ro%d" % i) for i in range(3)]
        rcnt = 0
        ccnt = 0
        for qk in range(2):
            load_w(wbig, twbig, qk * 1536, 1536, perm=True)
            dst, tdst = (QT, tQT) if qk == 0 else (KT, tKT)
            for tb in range(NB):
                cb = ccnt % 2
                ccnt += 1
                tsl = slice(tb * 512, (tb + 1) * 512)
                P.dma("sp", cs[cb][:, 0, :], cos_in[:, tsl], tcs[cb], writes=[tcs[cb]])
                P.dma("sp", cs[cb][:, 1, :], sin_in[:, tsl], tcs[cb], writes=[tcs[cb]])
                for ht in range(6):
                    ia = 2 * (ht % 2)
                    pa, pb = psb[ia], psb[ia + 1]
                    ta, tb_ = tps[ia], tps[ia + 1]
                    for half, (pp, tp) in enumerate(((pa, ta), (pb, tb_))):
                        for k in range(8):
                            P.op("pe", lambda e, pp=pp, k=k, ht=ht, half=half, tsl=tsl: e.matmul(
                                pp, lhsT=wbig[:, k, half * 768 + ht * 128:half * 768 + (ht + 1) * 128], rhs=hT[:, k, tsl],
                                start=(k == 0), stop=(k == 7)),
                                reads=[twbig, thT[tb]], writes=[tp])
                    r = rcnt % 3
                    rcnt += 1
                    C, S = cs[cb][:, 0, :], cs[cb][:, 1, :]
                    P.op("dve", lambda e, pa=pa, C=C: e.tensor_tensor(out=rtmp[0], in0=pa, in1=C, op=ALU.mult),
                         reads=[ta, tcs[cb]], writes=[trtmp[0]])
                    P.op("dve", lambda e, pb=pb, S=S: e.tensor_tensor(out=rtmp[1], in0=pb, in1=S, op=ALU.mult),
                         reads=[tb_, tcs[cb]], writes=[trtmp[1]])
                    P.op("dve", lambda e, r=r: e.tensor_tensor(out=ro[r][:, 0, :], in0=rtmp[0], in1=rtmp[1],
                                                               op=ALU.subtract),
                         reads=[trtmp[0], trtmp[1]], writes=[tro[r]])
                    P.op("dve", lambda e, pb=pb, C=C: e.tensor_tensor(out=rtmp[2], in0=pb, in1=C, op=ALU.mult),
                         reads=[tb_, tcs[cb]], writes=[trtmp[2]])
                    P.op("dve", lambda e, pa=pa, S=S: e.tensor_tensor(out=rtmp[3], in0=pa, in1=S, op=ALU.mult),
                         reads=[ta, tcs[cb]], writes=[trtmp[3]])
                    P.op("dve", lambda e, r=r: e.tensor_tensor(out=ro[r][:, 1, :], in0=rtmp[2], in1=rtmp[3],
                                                               op=ALU.add),
                         reads=[trtmp[2], trtmp[3]], writes=[tro[r]])
                    h0 = ht * 4
                    for half in range(2):
                        P.dma("sp", dst[half, h0:h0 + 4, :, tsl].rearrange("h i t -> (h i) t"),
                              ro[r][:, half, :], tro[r], reads=[tro[r]], writes=[tdst])
                    for _ in range(3):
                        next(zg, None)
        for _ in zg:
            pass
        P.barrier()
        st2a.close()
        load_w(wbig, twbig, 3072, 1536)
        vst = [sbs(st, "vst%d" % i, [128, 1536], BF16) for i in range(2)]
        tvst = [Tk("vst%d" % i) for i in range(2)]
        pc = 0
        for tt in range(NTT):
            vb = tt % 2
            for cb3 in range(3):
                pi = pc % 6
                pc += 1
                for k in range(8):
                    P.op("pe", lambda e, pi=pi, k=k, tt=tt, cb3=cb3: e.matmul(
                        psb[pi], lhsT=hT[:, k, tt * 128:(tt + 1) * 128], rhs=wbig[:, k, cb3 * 512:(cb3 + 1) * 512],
                        start=(k == 0), stop=(k == 7)),
                        reads=[twbig, thT[tt // 4]], writes=[tps[pi]])
                P.op("act", lambda e, pi=pi, vb=vb, cb3=cb3: e.activation(
                    out=vst[vb][:, cb3 * 512:(cb3 + 1) * 512], in_=psb[pi], func=AF.Copy),
                    reads=[tps[pi]], writes=[tvst[vb]])
            P.dma("act", Vd[tt * 128:(tt + 1) * 128, :], vst[vb], tvst[vb], reads=[tvst[vb]], writes=[tVd])
        P.barrier()
    st12.close()
    if debug and "stop2" in debug:
        P.emit([tQT, tKT, tVd, tZA, tUd, tZB, tGA, tGB])
        return nc, dbg_outs

    ATT = dscr("ATT", [512, L])
    tATT = Tk("ATT")
    PADK = 1024
    with ExitStack() as st:
        QTh = [sbs(st, "QTh%d" % i, [64, L], BF16) for i in range(2)]
        tQTh = [Tk("QTh%d" % i) for i in range(2)]
        KTh = [sbs(st, "KTh%d" % i, [64, L + 2 * PADK], BF16) for i in range(2)]
        tKTh = [Tk("KTh%d" % i) for i in range(2)]
        NVT = 80
        Vt = [sbs(st, "Vt%d" % i, [128, NVT, 128], BF16) for i in range(2)]
        tVt = [Tk("Vt%d" % i) for i in range(2)]
        ACC = sbs(st, "ACC", [128, L], F32)
        jsel = sbs(st, "jsel", [128, 128], F32)
        tjsel = Tk("jsel")
        P.dma("sp", jsel, jp_in, tjsel, writes=[tjsel])
        tACC = Tk("ACC")
        pT = [sbs(st, "pT%d" % i, [128, 256], BF16) for i in range(3)]
        tpT = [Tk("pT%d" % i) for i in range(3)]
        onesk = sbs(st, "onesk", [128, 3, 64], BF16)
        tonesk = Tk("onesk")
        mask_f = sbs(st, "mask_f", [128, 256], F32)
        mask_b = sbs(st, "mask_b", [128, 256], BF16)
        tmask_f, tmask_b = Tk("maskf"), Tk("maskb")
        za_t = sbs(st, "za_t", [64, 2048], BF16)
        tza = Tk("za_t")
        dv = sbs(st, "dv", [64, 2048], F32)
        tdv = Tk("dv")
        ao = [sbs(st, "ao%d" % i, [64, 2048], BF16) for i in range(2)]
        tao = [Tk("ao%d" % i) for i in range(2)]

        P.dma("sp", mask_f, mask_in, tmask_f, writes=[tmask_f])
        P.op("dve", lambda e: e.tensor_copy(out=mask_b, in_=mask_f), reads=[tmask_f], writes=[tmask_b])
        P.op("dve", lambda e: e.memset(onesk, 1.0), writes=[tonesk])
        P.op("dve", lambda e: e.memset(onesk[0:64, 1, :], 0.0), writes=[tonesk])
        P.op("dve", lambda e: e.memset(onesk[64:128, 2, :], 0.0), writes=[tonesk])
        for i in range(2):
            P.op("dve", lambda e, i=i: e.memset(KTh[i][:, 0:PADK], 0.0), writes=[tKTh[i]])
            P.op("dve", lambda e, i=i: e.memset(KTh[i][:, PADK + L:], 0.0), writes=[tKTh[i]])

        DIL = (1, 4, 16)
        slot = 0
        sidx = 0
        oidx = 0
        pidx = 0
        for h in range(8):
            for g in range(3):
                d = DIL[g]
                hg = g * 8 + h
                Ls = L // d
                nqb = Ls // 128
                nt = nqb + 1
                sl = slot % 2
                slot += 1
                for half in range(2):
                    P.dma("sp", QTh[sl][half * 32:(half + 1) * 32, :], QT[half, hg, :, :], tQTh[sl],
                          writes=[tQTh[sl]])
                    P.dma("sp", KTh[sl][half * 32:(half + 1) * 32, PADK:PADK + L], KT[half, hg, :, :], tKTh[sl],
                          writes=[tKTh[sl]])
                P.op("dve", lambda e, sl=sl: e.memset(Vt[sl], 0.0), writes=[tVt[sl]])
                P.op("dve", lambda e, sl=sl, d=d, nt=nt: e.memset(Vt[sl][:, 0:d * nt, 64:128], 1.0), writes=[tVt[sl]])
                P.op("dve", lambda e, sl=sl, d=d, nt=nt: e.memset(Vt[sl][0:64, 0:d * nt:nt, 64:128], 0.0),
                     writes=[tVt[sl]])
                P.op("dve", lambda e, sl=sl, d=d, nt=nt: e.memset(Vt[sl][64:128, nt - 1:d * nt:nt, 64:128], 0.0),
                     writes=[tVt[sl]])
                vcols = slice(hg * 64, (hg + 1) * 64)
                for r in range(d):
                    base = r * nt
                    P.dma("sp", Vt[sl][64:128, base, 0:64], Vd[ssl(r, 64, d), vcols], tVt[sl], reads=[tVd],
                          writes=[tVt[sl]])
                    j0 = 64
                    nmid = nt - 2
                    srcv = Vd[ssl(r + d * j0, 128 * nmid, d), vcols].rearrange("(tau p) c -> p tau c", p=128)
                    P.dma("sp", Vt[sl][:, base + 1:base + 1 + nmid, 0:64], srcv, tVt[sl], reads=[tVd],
                          writes=[tVt[sl]])
                    jl = Ls - 64
                    P.dma("sp", Vt[sl][0:64, base + nt - 1, 0:64], Vd[ssl(r + d * jl, 64, d), vcols], tVt[sl],
                          reads=[tVd], writes=[tVt[sl]])
                SBK = (0, 1, 6)
                tl = [(r, tau) for r in range(d) for tau in range(nt)]
                info = {}
                ocur = {}

                def stS(ix):
                    r, tau = tl[ix]
                    qb_lo = max(tau - 1, 0)
                    qb_hi = min(tau, nqb - 1)
                    nq = (qb_hi - qb_lo + 1) * 128
                    mcol0 = (qb_lo - (tau - 1)) * 128
                    kstart = PADK + r + d * (128 * tau - 64)
                    kap = KTh[sl][:, ssl(kstart, 128, d)]
                    qstart = r + d * 128 * qb_lo
                    qap = QTh[sl][:, ssl(qstart, nq, d)]
                    sp_, tsp = psb[SBK[ix % 3]], tps[SBK[ix % 3]]
                    P.op("pe", lambda e, sp_=sp_, kap=kap, qap=qap, nq=nq: e.matmul(
                        sp_[:, 0:nq], lhsT=kap, rhs=qap, start=True, stop=True),
                        reads=[tKTh[sl], tQTh[sl]], writes=[tsp])
                    info[ix] = (r, tau, qb_lo, qb_hi, nq, mcol0, sp_, tsp)

                def stE(ix):
                    r, tau, qb_lo, qb_hi, nq, mcol0, sp_, tsp = info[ix]
                    pi_ = ix % 3
                    P.op("act", lambda e, sp_=sp_, pi_=pi_, nq=nq: e.activation(
                        out=pT[pi_][:, 0:nq], in_=sp_[:, 0:nq], func=AF.Exp, scale=0.125),
                        reads=[tsp], writes=[tpT[pi_]])
                    P.op("dve", lambda e, pi_=pi_, nq=nq, mcol0=mcol0: e.tensor_tensor(
                        out=pT[pi_][:, 0:nq], in0=pT[pi_][:, 0:nq], in1=mask_b[:, mcol0:mcol0 + nq],
                        op=ALU.mult),
                        reads=[tpT[pi_], tmask_b], writes=[tpT[pi_]])

                def stPV(ix, oidx_box):
                    r, tau, qb_lo, qb_hi, nq, mcol0, sp_, tsp = info.pop(ix)
                    pi_ = ix % 3
                    ok = 1 if tau == 0 else (2 if tau == nt - 1 else 0)
                    for qb in range(qb_lo, qb_hi + 1):
                        first = (qb == tau)
                        if first:
                            ocur[(r, qb)] = 2 + (oidx_box[0] % 4)
                            oidx_box[0] += 1
                        oi = ocur[(r, qb)]
                        op_ = psb[oi]
                        c0 = (qb - qb_lo) * 128
                        P.op("pe", lambda e, op_=op_, pi_=pi_, c0=c0, tile=r * nt + tau, first=first, sl=sl: e.matmul(
                            op_[:, 0:128], lhsT=Vt[sl][:, tile, :], rhs=pT[pi_][:, c0:c0 + 128],
                            start=first, stop=(not first)),
                            reads=[tVt[sl], tpT[pi_]], writes=[tps[oi]])
                        if not first:
                            t0 = r + d * 128 * qb
                            acc_ap = ACC[:, ssl(t0, 128, d)]
                            src = op_[:, 0:128]
                            if g == 0:
                                P.op("dve", lambda e, acc_ap=acc_ap, src=src: e.tensor_copy(out=acc_ap, in_=src),
                                     reads=[tps[oi]], writes=[tACC])
                            else:
                                P.op("dve", lambda e, acc_ap=acc_ap, src=src: e.tensor_tensor(
                                    out=acc_ap, in0=src, in1=acc_ap, op=ALU.add),
                                    reads=[tps[oi], tACC], writes=[tACC])

                n_t = len(tl)
                obox = [oidx]
                for i0 in range(min(3, n_t)):
                    stS(i0)
                stE(0)
                for ix in range(n_t):
                    if ix + 1 < n_t:
                        stE(ix + 1)
                    if ix + 3 < n_t:
                        stS(ix + 3)
                    stPV(ix, obox)
                oidx = obox[0]
            for c4 in range(4):
                csl = slice(c4 * 2048, (c4 + 1) * 2048)
                a_ = (h * 4 + c4) % 2
                P.dma("sp", za_t, ZA[h * 64:(h + 1) * 64, csl], tza, reads=[tZA], writes=[tza])
                for j4 in range(4):
                    cj = slice(c4 * 2048 + j4 * 512, c4 * 2048 + (j4 + 1) * 512)
                    eb = SBK[(c4 * 4 + j4) % 3]
                    P.op("pe", lambda e, eb=eb, cj=cj: e.matmul(psb[eb][0:64, :], lhsT=jsel[:, 0:64], rhs=ACC[:, cj],
                                                               start=True, stop=True),
                         reads=[tjsel, tACC], writes=[tps[eb]])
                    P.op("act", lambda e, eb=eb, j4=j4: e.activation(out=dv[:, j4 * 512:(j4 + 1) * 512],
                                                                     in_=psb[eb][0:64, :], func=AF.Ln),
                         reads=[tps[eb]], writes=[tdv])
                P.op("act", lambda e: e.activation(out=dv, in_=dv, func=AF.Exp, scale=-1.0), reads=[tdv], writes=[tdv])
                P.op("dve", lambda e, csl=csl: e.tensor_tensor(out=dv, in0=dv, in1=ACC[0:64, csl], op=ALU.mult),
                     reads=[tACC, tdv], writes=[tdv])
                P.op("dve", lambda e, a_=a_: e.tensor_tensor(out=ao[a_], in0=dv, in1=za_t, op=ALU.mult),
                     reads=[tdv, tza], writes=[tao[a_]])
                P.dma("act", ATT[h * 64:(h + 1) * 64, csl], ao[a_], tao[a_], reads=[tao[a_]], writes=[tATT])
        P.barrier()
    if debug and "stop3" in debug:
        P.emit([tATT, tQT, tKT, tVd, tZA, tUd, tZB, tGA, tGB])
        return nc, dbg_outs

    if debug and "only4" in debug:
        P = Prog(nc)
        for t_ in (t_ident_f, t_ident_b, t_gcol, tUd, tATT) + tuple(tps) + (tpst,):
            t_.w, t_.rs, t_.dsem, t_.dcnt = {}, {}, None, 0
        load_consts()
    Yd = dscr("Yd", [32, 128, 1024])
    tYd = Tk("Yd")
    PI = float(np.pi)
    stM = ExitStack()
    MATS = sbs(stM, "MATS", [128, 64, 4, 128], BF16)
    tMATS = [Tk("MATS%d" % i) for i in range(64)]
    a64 = sbs(stM, "a64", [128, 64], F32)
    b64 = sbs(stM, "b64", [128, 64], F32)
    nb64 = sbs(stM, "nb64", [128, 64], F32)
    tab64 = Tk("ab64")
    Jm_f = sbs(stM, "Jm_f", [128, 128], F32)
    tJm = Tk("Jm")
    P.dma("sp", Jm_f, jm_in, tJm, writes=[tJm])
    with ExitStack() as st:
        def small(name, shape, dt=F32):
            return sbs(st, "s4_" + name, shape, dt), Tk(name)
        lamr, tlamr = small("lamr", [128, 64])
        lami, tlami = small("lami", [128, 64])
        stp, tstp = small("stp", [128, 64])
        ar, tar = small("ar", [128, 64])
        ai, tai = small("ai", [128, 64])
        EX, tEX = small("EX", [128, 2, 32])
        sg, tsg = small("sg", [128, 2])
        BA, tBA = small("BA", [128, 64, 16])
        BB, tBB = small("BB", [128, 64, 16])
        CA, tCA = small("CA", [128, 64, 16])
        CB, tCB = small("CB", [128, 64, 16])
        dcol, tdcol = small("dcol", [128, 32])
        Jp, tJp = small("Jp", [128, 128])
        cmask, tcmask = small("cmask", [128, 2, 128])
        for dst, t_, src in ((lamr, tlamr, lamr_in), (lami, tlami, lami_in), (stp, tstp, lstep_in), (EX, tEX, ex_in),
                             (sg, tsg, sg_in), (BA, tBA, ba_in), (BB, tBB, bb_in), (CA, tCA, ca_in), (CB, tCB, cb_in),
                             (dcol, tdcol, dcol_in), (Jp, tJp, jp_in), (cmask, tcmask, cmask_in)):
            P.dma("sp", dst, src, t_, writes=[t_])
        P.op("act", lambda e: e.activation(out=stp, in_=stp, func=AF.Exp), reads=[tstp], writes=[tstp])
        P.op("dve", lambda e: e.tensor_tensor(out=ar, in0=lamr, in1=stp, op=ALU.mult), reads=[tlamr, tstp], writes=[tar])
        P.op("dve", lambda e: e.tensor_tensor(out=ai, in0=lami, in1=stp, op=ALU.mult), reads=[tlami, tstp], writes=[tai])
        ang, tang = small("ang", [128, 2, 32, 32])
        ang2, tang2 = small("ang2", [128, 2, 32, 32])
        mgl, tmgl = small("mgl", [128, 2, 32, 32])
        mc, tmc = small("mc", [128, 2, 32, 32])
        ms, tms = small("ms", [128, 2, 32, 32])
        mc1, tmc1 = small("mc1", [128, 2, 32, 32])
        ms2, tms2 = small("ms2", [128, 2, 32, 32])
        for r in range(2):
            aib = ai[:, r * 32:(r + 1) * 32].unsqueeze(2).to_broadcast([128, 32, 32])
            arb = ar[:, r * 32:(r + 1) * 32].unsqueeze(2).to_broadcast([128, 32, 32])
            exb = EX[:, r, :].unsqueeze(1).to_broadcast([128, 32, 32])
            P.op("dve", lambda e, r=r, aib=aib, exb=exb: e.tensor_tensor(out=ang[:, r], in0=aib, in1=exb, op=ALU.mult),
                 reads=[tai, tEX], writes=[tang])
            P.op("dve", lambda e, r=r, arb=arb, exb=exb: e.tensor_tensor(out=mgl[:, r], in0=arb, in1=exb, op=ALU.mult),
                 reads=[tar, tEX], writes=[tmgl])
        angf = ang.rearrange("p a b c -> p (a b c)")
        ang2f = ang2.rearrange("p a b c -> p (a b c)")
        mglf = mgl.rearrange("p a b c -> p (a b c)")
        mcf = mc.rearrange("p a b c -> p (a b c)")
        msf = ms.rearrange("p a b c -> p (a b c)")
        mc1f = mc1.rearrange("p a b c -> p (a b c)")
        ms2f = ms2.rearrange("p a b c -> p (a b c)")
        OFF = 64.0 * PI
        INV2PI = 1.0 / (2 * PI)
        kint, tkint = small("kint", [128, 2048], mybir.dt.int32)
        kf, tkf = small("kf", [128, 2048])
        P.op("dve", lambda e: e.tensor_scalar(out=ang2f, in0=angf, scalar1=OFF + PI / 2, scalar2=INV2PI, op0=ALU.add,
                                              op1=ALU.mult), reads=[tang], writes=[tang2])
        P.op("dve", lambda e: e.tensor_scalar(out=angf, in0=angf, scalar1=OFF, scalar2=INV2PI, op0=ALU.add,
                                              op1=ALU.mult), reads=[tang], writes=[tang])
        for af_, taf_ in ((ang2f, tang2), (angf, tang)):
            P.op("dve", lambda e, af_=af_: e.tensor_copy(out=kint, in_=af_), reads=[taf_], writes=[tkint])
            P.op("dve", lambda e: e.tensor_copy(out=kf, in_=kint), reads=[tkint], writes=[tkf])
            P.op("dve", lambda e, af_=af_: e.tensor_tensor(out=af_, in0=af_, in1=kf, op=ALU.subtract), reads=[taf_, tkf],
                 writes=[taf_])
            P.op("dve", lambda e, af_=af_: e.tensor_scalar(out=kf, in0=af_, scalar1=0.5, scalar2=None, op0=ALU.is_gt),
                 reads=[taf_], writes=[tkf])
            P.op("dve", lambda e, af_=af_: e.tensor_tensor(out=af_, in0=af_, in1=kf, op=ALU.subtract), reads=[taf_, tkf],
                 writes=[taf_])
            P.op("dve", lambda e, af_=af_: e.tensor_scalar(out=kf, in0=af_, scalar1=-0.5, scalar2=None, op0=ALU.is_lt),
                 reads=[taf_], writes=[tkf])
            P.op("dve", lambda e, af_=af_: e.tensor_tensor(out=af_, in0=af_, in1=kf, op=ALU.add), reads=[taf_, tkf],
                 writes=[taf_])
        P.op("act", lambda e: e.activation(out=ang2f, in_=ang2f, func=AF.Sin, scale=2 * PI), reads=[tang2], writes=[tang2])
        P.op("act", lambda e: e.activation(out=angf, in_=angf, func=AF.Sin, scale=2 * PI), reads=[tang], writes=[tang])
        P.op("act", lambda e: e.activation(out=mglf, in_=mglf, func=AF.Exp), reads=[tmgl], writes=[tmgl])
        P.op("dve", lambda e: e.tensor_tensor(out=mcf, in0=mglf, in1=ang2f, op=ALU.mult), reads=[tmgl, tang2], writes=[tmc])
        P.op("dve", lambda e: e.tensor_tensor(out=msf, in0=mglf, in1=angf, op=ALU.mult), reads=[tmgl, tang], writes=[tms])
        P.op("dve", lambda e: e.tensor_scalar(out=mc1f, in0=mcf, scalar1=sg[:, 0:1], scalar2=None, op0=ALU.mult),
             reads=[tmc, tsg], writes=[tmc1])
        P.op("dve", lambda e: e.tensor_scalar(out=ms2f, in0=msf, scalar1=sg[:, 1:2], scalar2=None, op0=ALU.mult),
             reads=[tms, tsg], writes=[tms2])
        def pw(tab, j):
            return tab[:, :, :, 24 + j]
        l1r, tl1r = small("l1r", [128, 2, 32])
        nr_, tnr = small("nr_", [128, 2, 32])
        den, tden = small("den", [128, 2, 32])
        tmpa, ttmpa = small("tmpa", [128, 2, 32])
        tmpb, ttmpb = small("tmpb", [128, 2, 32])
        wr, twr = small("wr", [128, 2, 32])
        wi, twi = small("wi", [128, 2, 32])
        wi1, twi1 = small("wi1", [128, 2, 32])
        wi2, twi2 = small("wi2", [128, 2, 32])
        lr3 = lamr.rearrange("p (r g) -> p r g", r=2)
        li3 = lami.rearrange("p (r g) -> p r g", r=2)
        P.op("dve", lambda e: e.tensor_scalar(out=nr_, in0=pw(mc, 0), scalar1=-1.0, scalar2=None, op0=ALU.add),
             reads=[tmc], writes=[tnr])
        P.op("dve", lambda e: e.tensor_tensor(out=den, in0=lr3, in1=lr3, op=ALU.mult), reads=[tlamr], writes=[tden])
        P.op("dve", lambda e: e.tensor_tensor(out=tmpa, in0=li3, in1=li3, op=ALU.mult), reads=[tlami], writes=[ttmpa])
        P.op("dve", lambda e: e.tensor_tensor(out=den, in0=den, in1=tmpa, op=ALU.add), reads=[tden, ttmpa], writes=[tden])
        P.op("dve", lambda e: e.tensor_tensor(out=tmpa, in0=nr_, in1=lr3, op=ALU.mult), reads=[tnr, tlamr], writes=[ttmpa])
        P.op("dve", lambda e: e.tensor_tensor(out=tmpb, in0=pw(ms, 0), in1=li3, op=ALU.mult), reads=[tms, tlami],
             writes=[ttmpb])
        P.op("dve", lambda e: e.tensor_tensor(out=tmpa, in0=tmpa, in1=tmpb, op=ALU.add), reads=[ttmpa, ttmpb],
             writes=[ttmpa])
        P.op("dve", lambda e: e.reciprocal(out=den, in_=den), reads=[tden], writes=[tden])
        P.op("dve", lambda e: e.tensor_tensor(out=wr, in0=tmpa, in1=den, op=ALU.mult), reads=[ttmpa, tden], writes=[twr])
        P.op("dve", lambda e: e.tensor_tensor(out=tmpa, in0=pw(ms, 0), in1=lr3, op=ALU.mult), reads=[tms, tlamr],
             writes=[ttmpa])
        P.op("dve", lambda e: e.tensor_tensor(out=tmpb, in0=nr_, in1=li3, op=ALU.mult), reads=[tnr, tlami], writes=[ttmpb])
        P.op("dve", lambda e: e.tensor_tensor(out=tmpa, in0=tmpa, in1=tmpb, op=ALU.subtract), reads=[ttmpa, ttmpb],
             writes=[ttmpa])
        P.op("dve", lambda e: e.tensor_tensor(out=wi, in0=tmpa, in1=den, op=ALU.mult), reads=[ttmpa, tden], writes=[twi])
        P.op("dve", lambda e: e.tensor_scalar(out=wi1, in0=wi, scalar1=sg[:, 0:1], scalar2=None, op0=ALU.mult),
             reads=[twi, tsg], writes=[twi1])
        P.op("dve", lambda e: e.tensor_scalar(out=wi2, in0=wi, scalar1=sg[:, 1:2], scalar2=None, op0=ALU.mult),
             reads=[twi, tsg], writes=[twi2])
        bA, tbA = small("bA", [128, 64, 16])
        bB, tbB = small("bB", [128, 64, 16])
        tmp16, ttmp16 = small("tmp16", [128, 64, 16])

        def bc16(t):
            return t.rearrange("p r g -> p (r g)").unsqueeze(2).to_broadcast([128, 64, 16])
        P.op("dve", lambda e: e.tensor_tensor(out=bA, in0=BA, in1=bc16(wr), op=ALU.mult), reads=[tBA, twr], writes=[tbA])
        P.op("dve", lambda e: e.tensor_tensor(out=tmp16, in0=BB, in1=bc16(wi2), op=ALU.mult), reads=[tBB, twi2],
             writes=[ttmp16])
        P.op("dve", lambda e: e.tensor_tensor(out=bA, in0=bA, in1=tmp16, op=ALU.add), reads=[tbA, ttmp16], writes=[tbA])
        P.op("dve", lambda e: e.tensor_tensor(out=bB, in0=BB, in1=bc16(wr), op=ALU.mult), reads=[tBB, twr], writes=[tbB])
        P.op("dve", lambda e: e.tensor_tensor(out=tmp16, in0=BA, in1=bc16(wi1), op=ALU.mult), reads=[tBA, twi1],
             writes=[ttmp16])
        P.op("dve", lambda e: e.tensor_tensor(out=bB, in0=bB, in1=tmp16, op=ALU.add), reads=[tbB, ttmp16], writes=[tbB])
        P.op("dve", lambda e: e.tensor_copy(out=a64.rearrange("p (r g) -> p r g", r=2), in_=pw(mc, 2)), reads=[tmc],
             writes=[tab64])
        P.op("dve", lambda e: e.tensor_copy(out=b64.rearrange("p (r g) -> p r g", r=2), in_=pw(ms, 2)), reads=[tms],
             writes=[tab64])
        P.op("dve", lambda e: e.tensor_scalar(out=nb64, in0=b64, scalar1=-1.0, scalar2=None, op0=ALU.mult),
             reads=[tab64], writes=[tab64])
        l8a, tl8a = small("l8a", [128, 2, 32])
        l8b, tl8b = small("l8b", [128, 2, 32])
        P.op("dve", lambda e: e.tensor_copy(out=l8a, in_=pw(mc, 1)), reads=[tmc], writes=[tl8a])
        P.op("dve", lambda e: e.tensor_copy(out=l8b, in_=pw(mc1, 1)), reads=[tmc1], writes=[tl8b])
        P.op("dve", lambda e: e.tensor_scalar(out=l8b, in0=pw(ms, 1), scalar1=sg[:, 0:1], scalar2=None, op0=ALU.mult),
             reads=[tms, tsg], writes=[tl8b])
        Pt, tPt = small("Pt", [128, 8, 8, 16])
        Qt, tQt = small("Qt", [128, 8, 8, 16])
        M4t, tM4t = small("M4t", [128, 8, 8, 16])
        tq, ttq = small("tq", [128, 8, 8, 16])
        m1tmp, tm1tmp = small("m1tmp", [128, 128])
        ddiag, tddiag = small("ddiag", [128, 128])
        l8tmp, tl8tmp = small("l8tmp", [128, 128])
        pcn = 0
        for r in range(2):
            for gb in range(4):
                gsl = slice(gb * 8, (gb + 1) * 8)
                gdsl = slice(r * 32 + gb * 8, r * 32 + (gb + 1) * 8)

                def wtab(tab, w):
                    return tab[:, r, gsl, w * 8:(w + 1) * 8].unsqueeze(3).to_broadcast([128, 8, 8, 16])

                def ctab(tab):
                    return tab[:, gdsl, :].unsqueeze(2).to_broadcast([128, 8, 8, 16])
                for (dst, tdst, w, A, tA, Bm, tB, kind) in ((Pt, tPt, 0, bA, tbA, bB, tbB, "p"),
                                                            (Qt, tQt, 1, CA, tCA, CB, tCB, "q"),
                                                            (M4t, tM4t, 2, CA, tCA, CB, tCB, "q")):
                    if kind == "p":
                        m_a, tm_a, m_b, tm_b, op2 = mc, tmc, ms2, tms2, ALU.add
                    else:
                        m_a, tm_a, m_b, tm_b, op2 = mc1, tmc1, ms, tms, ALU.subtract
                    i0a, i1a = wtab(m_a, w), ctab(A)
                    i0b, i1b = wtab(m_b, w), ctab(Bm)
                    P.op("dve", lambda e, dst=dst, i0a=i0a, i1a=i1a: e.tensor_tensor(
                        out=dst, in0=i0a, in1=i1a, op=ALU.mult), reads=[tm_a, tA], writes=[tdst])
                    P.op("dve", lambda e, i0b=i0b, i1b=i1b: e.tensor_tensor(
                        out=tq, in0=i0b, in1=i1b, op=ALU.mult), reads=[tm_b, tB], writes=[ttq])
                    P.op("dve", lambda e, dst=dst, op2=op2: e.tensor_tensor(out=dst, in0=dst, in1=tq, op=op2),
                         reads=[tdst, ttq], writes=[tdst])
                for gi in range(8):
                    g = gb * 8 + gi
                    gd = r * 32 + g
                    Pg = Pt[:, gi].rearrange("p s c -> p (s c)")
                    Qg = Qt[:, gi].rearrange("p s c -> p (s c)")
                    M4g = M4t[:, gi].rearrange("p s c -> p (s c)")
                    pa_ = psb[pcn % 6]
                    tpa_ = tps[pcn % 6]
                    pcn += 1
                    P.op("pe", lambda e, pa_=pa_, Pg=Pg, Qg=Qg: e.matmul(pa_[:, 0:128], lhsT=Pg, rhs=Qg, start=True,
                                                                        stop=True),
                         reads=[tPt, tQt], writes=[tpa_])
                    P.op("dve", lambda e, pa_=pa_, r=r: e.tensor_tensor(out=m1tmp, in0=pa_[:, 0:128], in1=cmask[:, r, :],
                                                                       op=ALU.mult),
                         reads=[tpa_, tcmask], writes=[tm1tmp])
                    if r == 0:
                        P.op("dve", lambda e, g=g, gd=gd: e.scalar_tensor_tensor(
                            out=MATS[:, gd, 1, :], in0=ident_f, scalar=dcol[:, g:g + 1], in1=m1tmp, op0=ALU.mult,
                            op1=ALU.add), reads=[t_ident_f, tdcol, tm1tmp], writes=[tMATS[gd]])
                    else:
                        P.op("dve", lambda e, gd=gd: e.tensor_copy(out=MATS[:, gd, 1, :], in_=m1tmp),
                             reads=[tm1tmp], writes=[tMATS[gd]])
                    pb_ = psb[pcn % 6]
                    tpb_ = tps[pcn % 6]
                    pcn += 1
                    P.op("pe", lambda e, pb_=pb_, Pg=Pg: e.transpose(out=pb_[:, 0:128], in_=Pg, identity=ident_f),
                         reads=[tPt, t_ident_f], writes=[tpb_])
                    P.op("act", lambda e, pb_=pb_, gd=gd: e.activation(out=MATS[:, gd, 0, :], in_=pb_[:, 0:128],
                                                                      func=AF.Copy),
                         reads=[tpb_], writes=[tMATS[gd]])
                    P.op("act", lambda e, M4g=M4g, gd=gd: e.activation(out=MATS[:, gd, 2, :], in_=M4g, func=AF.Copy),
                         reads=[tM4t], writes=[tMATS[gd]])
                    P.op("dve", lambda e, r=r, g=g: e.tensor_scalar(out=l8tmp, in0=Jp, scalar1=l8b[:, r, g:g + 1],
                                                                   scalar2=None, op0=ALU.mult),
                         reads=[tJp, tl8b], writes=[tl8tmp])
                    P.op("dve", lambda e, r=r, g=g, gd=gd: e.scalar_tensor_tensor(
                        out=MATS[:, gd, 3, :], in0=ident_f, scalar=l8a[:, r, g:g + 1], in1=l8tmp, op0=ALU.mult,
                        op1=ALU.add), reads=[t_ident_f, tl8a, tl8tmp], writes=[tMATS[gd]])
        dump("mc", mc, tmc, F32)
        dump("ms", ms, tms, F32)
        dump("bA", bA, tbA, F32)
        dump("wr", wr, twr, F32)
        dump("wi", wi, twi, F32)
        dump("Pt", Pt, tPt, F32)
        dump("Qt", Qt, tQt, F32)
        tall = Tk("matsall")
        P.barrier()
        for i8 in range(8):
            dump("MATS%d" % i8, MATS[:, i8 * 8:(i8 + 1) * 8], tall, BF16)
        P.barrier()

    with ExitStack() as st:
        Ug = sbs(st, "Ug", [128, 32, 1024], BF16)
        tUg = [Tk("Ug%d" % i) for i in range(32)]
        for g in range(32):
            for t8 in range(8):
                P.dma("sp", Ug[t8 * 16:(t8 + 1) * 16, g, :], Ud[g // 8, t8, (g % 8) * 16:(g % 8 + 1) * 16, :], tUg[g],
                      reads=[tUd], writes=[tUg[g]])
        SS = sbs(st, "SS", [128, 2, 32, 128], F32)
        tSS = Tk("SS")
        G0 = sbs(st, "G0", [128, 2, 32, 128], BF16)
        tG0 = [Tk("G0_0"), Tk("G0_1")]
        hq = [sbs(st, "hq%d" % i, [128, 4, 128], BF16) for i in range(2)]
        thq = [Tk("hq%d" % i) for i in range(2)]
        t1b = sbs(st, "t1b", [128, 2, 32], F32)
        t2b = sbs(st, "t2b", [128, 2, 32], F32)
        tt1b, tt2b = Tk("t1b"), Tk("t2b")
        ystg = [sbs(st, "ystg%d" % i, [128, 1024], F32) for i in range(2)]
        tystg = [Tk("ystg%d" % i) for i in range(2)]
        yx = sbs(st, "yx", [128, 1024], F32)
        tyx = Tk("yx")
        yo = [sbs(st, "yo%d" % i, [128, 1024], BF16) for i in range(2)]
        tyo = [Tk("yo%d" % i) for i in range(2)]
        P.op("dve", lambda e: e.memset(G0, 0.0), writes=tG0)
        hb = 0
        for r in range(2):
            for quad in range(8):
                for n in range(8):
                    i = n if r == 0 else 7 - n
                    bank, tbank = psb[hb % 2], tps[hb % 2]
                    for j in range(4):
                        g = quad * 4 + j
                        gd = r * 32 + g
                        P.op("pe", lambda e, bank=bank, j=j, gd=gd, g=g, i=i, n=n: e.matmul(
                            bank[:, j * 128:(j + 1) * 128], lhsT=MATS[:, gd, 0, :], rhs=Ug[:, g, i:1024:8],
                            start=(j == 0), stop=(n == 0), skip_group_check=True), reads=[tMATS[gd], tUg[g]],
                            writes=[tbank])
                        if n > 0:
                            P.op("pe", lambda e, bank=bank, j=j, gd=gd, n=n: e.matmul(
                                bank[:, j * 128:(j + 1) * 128], lhsT=MATS[:, gd, 3, :], rhs=hq[(n - 1) % 2][:, j, :],
                                start=False, stop=True, skip_group_check=True), reads=[tMATS[gd], thq[(n - 1) % 2]],
                                writes=[tbank])
                    if n < 7:
                        P.op("act", lambda e, bank=bank, n=n: e.activation(
                            out=hq[n % 2].rearrange("p a b -> p (a b)"), in_=bank, func=AF.Copy),
                            reads=[tbank], writes=[thq[n % 2]])
                    else:
                        P.op("act", lambda e, bank=bank, quad=quad: e.activation(
                            out=SS[:, 0, quad * 4:(quad + 1) * 4, :].rearrange("p a b -> p (a b)"), in_=bank,
                            func=AF.Copy), reads=[tbank], writes=[tSS])
                    hb += 1
            for blk in range(8):
                bank, tbank = psb[2 + blk % 2], tps[2 + blk % 2]
                P.op("pe", lambda e, bank=bank, blk=blk: e.matmul(
                    bank, lhsT=Jm_f, rhs=SS[:, 0, blk * 4:(blk + 1) * 4, :].rearrange("p a b -> p (a b)"),
                    start=True, stop=True), reads=[tJm, tSS], writes=[tbank])
                P.op("dve", lambda e, bank=bank, blk=blk: e.tensor_copy(
                    out=SS[:, 1, blk * 4:(blk + 1) * 4, :].rearrange("p a b -> p (a b)"), in_=bank),
                    reads=[tbank], writes=[tSS])
            ks = list(range(128)) if r == 0 else list(range(127, -1, -1))
            arow = a64[:, r * 32:(r + 1) * 32]
            brow = b64[:, r * 32:(r + 1) * 32]
            nbrow = nb64[:, r * 32:(r + 1) * 32]
            a2 = arow.unsqueeze(1).to_broadcast([128, 2, 32])
            for kk in range(1, 128):
                kp, k = ks[kk - 1], ks[kk]
                P.op("dve", lambda e, kp=kp, a2=a2: e.tensor_tensor(out=t1b, in0=SS[:, :, :, kp], in1=a2, op=ALU.mult),
                     reads=[tSS, tab64], writes=[tt1b])
                P.op("dve", lambda e, kp=kp, brow=brow: e.tensor_tensor(out=t2b[:, 0, :], in0=SS[:, 1, :, kp], in1=brow,
                                                                       op=ALU.mult),
                     reads=[tSS, tab64], writes=[tt2b])
                P.op("dve", lambda e, kp=kp, nbrow=nbrow: e.tensor_tensor(out=t2b[:, 1, :], in0=SS[:, 0, :, kp],
                                                                         in1=nbrow, op=ALU.mult),
                     reads=[tSS, tab64], writes=[tt2b])
                P.op("dve", lambda e: e.tensor_tensor(out=t1b, in0=t1b, in1=t2b, op=ALU.add), reads=[tt1b, tt2b],
                     writes=[tt1b])
                P.op("dve", lambda e, k=k: e.tensor_tensor(out=SS[:, :, :, k], in0=SS[:, :, :, k], in1=t1b, op=ALU.add),
                     reads=[tSS, tt1b], writes=[tSS])
            dump("SS%d_0" % r, SS[:, 0], tSS, F32)
            dump("SS%d_1" % r, SS[:, 1], tSS, F32)
            if r == 0:
                P.op("dve", lambda e: e.tensor_copy(out=G0[:, 0, :, 1:128], in_=SS[:, 0, :, 0:127]), reads=[tSS],
                     writes=[tG0[0]])
            else:
                P.op("dve", lambda e: e.tensor_copy(out=G0[:, 1, :, 0:127], in_=SS[:, 0, :, 1:128]), reads=[tSS],
                     writes=[tG0[1]])
        hq2 = [hq[0].rearrange("p (a b) c -> p a b c", a=2), hq[1].rearrange("p (a b) c -> p a b c", a=2)]
        for g in range(32):
            Yb = [psb[2 + 2 * (g % 2)], psb[3 + 2 * (g % 2)]]
            tYb = [tps[2 + 2 * (g % 2)], tps[3 + 2 * (g % 2)]]
            cnt_i = [0] * 8
            ybank_started = [False, False]
            for n in range(8):
                bank, tbank = psb[hb % 2], tps[hb % 2]
                hb += 1
                for r in range(2):
                    i = n if r == 0 else 7 - n
                    gd = r * 32 + g
                    if n == 0:
                        prev, tprev = G0[:, r, g, :], tG0[r]
                    else:
                        prev, tprev = hq2[(n - 1) % 2][:, 0, r, :], thq[(n - 1) % 2]
                    yb, tyb = Yb[i // 4], tYb[i // 4]
                    yc = slice((i % 4) * 128, (i % 4 + 1) * 128)
                    u_i = Ug[:, g, i:1024:8]
                    st_ = not ybank_started[i // 4]
                    ybank_started[i // 4] = True
                    P.op("pe", lambda e, yb=yb, yc=yc, gd=gd, u_i=u_i, st_=st_: e.matmul(
                        yb[:, yc], lhsT=MATS[:, gd, 1, :], rhs=u_i, start=st_, stop=False, skip_group_check=True),
                        reads=[tMATS[gd], tUg[g]], writes=[tyb])
                    P.op("pe", lambda e, yb=yb, yc=yc, gd=gd, prev=prev, sp_=(cnt_i[i] == 1): e.matmul(
                        yb[:, yc], lhsT=MATS[:, gd, 2, :], rhs=prev, start=False, stop=sp_, skip_group_check=True),
                        reads=[tMATS[gd], tprev], writes=[tyb])
                    cnt_i[i] += 1
                    if n < 7:
                        P.op("pe", lambda e, bank=bank, r=r, gd=gd, u_i=u_i: e.matmul(
                            bank[:, r * 128:(r + 1) * 128], lhsT=MATS[:, gd, 0, :], rhs=u_i, start=(r == 0), stop=False,
                            skip_group_check=True), reads=[tMATS[gd], tUg[g]], writes=[tbank])
                        P.op("pe", lambda e, bank=bank, r=r, gd=gd, prev=prev: e.matmul(
                            bank[:, r * 128:(r + 1) * 128], lhsT=MATS[:, gd, 3, :], rhs=prev, start=False, stop=True,
                            skip_group_check=True), reads=[tMATS[gd], tprev], writes=[tbank])
                if n < 7:
                    P.op("act", lambda e, bank=bank, n=n: e.activation(
                        out=hq[n % 2][:, 0:2, :].rearrange("p a b -> p (a b)"), in_=bank[:, 0:256], func=AF.Copy),
                        reads=[tbank], writes=[thq[n % 2]])
            ys = ystg[g % 2]
            for b2 in range(2):
                P.op("act", lambda e, ys=ys, b2=b2, Yb=Yb: e.activation(
                    out=ys.rearrange("p (k i) -> p i k", i=8)[:, 4 * b2:4 * b2 + 4, :],
                    in_=Yb[b2].rearrange("p (i k) -> p i k", i=4), func=AF.Copy),
                    reads=[tYb[b2]], writes=[tystg[g % 2]])
            P.op("dve", lambda e, ys=ys: e.tensor_tensor(out=yx, in0=ys, in1=ys, op=ALU.mult), reads=[tystg[g % 2]],
                 writes=[tyx])
            P.op("dve", lambda e: e.tensor_scalar(out=yx, in0=yx, scalar1=0.044715, scalar2=1.0, op0=ALU.mult,
                                                  op1=ALU.add), reads=[tyx], writes=[tyx])
            P.op("dve", lambda e, ys=ys: e.tensor_tensor(out=yx, in0=yx, in1=ys, op=ALU.mult), reads=[tyx, tystg[g % 2]],
                 writes=[tyx])
            P.op("act", lambda e: e.activation(out=yx, in_=yx, func=AF.Sigmoid, scale=1.5957691216057308),
                 reads=[tyx], writes=[tyx])
            P.op("dve", lambda e, ys=ys, g=g: e.tensor_tensor(out=yo[g % 2], in0=yx, in1=ys, op=ALU.mult),
                 reads=[tyx, tystg[g % 2]], writes=[tyo[g % 2]])
            P.dma("act", Yd[g], yo[g % 2], tyo[g % 2], reads=[tyo[g % 2]], writes=[tYd])
        P.barrier()
    stM.close()
    if debug and "stop4" in debug:
        if "only4" in debug:
            P.emit([tYd] + dbg_tiles)
        else:
            P.emit([tYd, tATT, tQT, tKT, tVd, tZA, tUd, tZB, tGA, tGB] + dbg_tiles)
        return nc, dbg_outs

    with ExitStack() as st:
        wstf = [sbs(st, "wstf%d" % i, [128, 1024], F32) for i in range(2)]
        twstf = [Tk("wstf%d" % i) for i in range(2)]
        wa = sbs(st, "wa", [128, 4, 1024], BF16)
        wb = sbs(st, "wb", [128, 4, 1024], BF16)
        wgl = sbs(st, "wgl", [128, 4, 512], BF16)
        wo = sbs(st, "wo", [128, 8, 1024], BF16)
        twa, twb, twgl, two = Tk("wa"), Tk("wb"), Tk("wgl"), Tk("wo")
        wc = 0
        for (dst, tdst, src, nk, ncol) in ((wa, twa, wa_in, 4, 1024), (wb, twb, wb_in, 4, 1024),
                                            (wgl, twgl, wglu_in, 4, 512), (wo, two, wout_in, 8, 1024)):
            for k in range(nk):
                b = wc % 2
                wc += 1
                P.dma("sp", wstf[b][:, 0:ncol], src[k * 128:(k + 1) * 128, :], twstf[b], writes=[twstf[b]])
                P.op("act", lambda e, dst=dst, k=k, b=b, ncol=ncol: e.activation(out=dst[:, k, :], in_=wstf[b][:, 0:ncol],
                                                                              func=AF.Copy),
                     reads=[twstf[b]], writes=[tdst])
        fgb = sbs(st, "fgb", [128, 1024], F32)
        tfgb = Tk("fgb")
        P.dma("sp", fgb, fg_in.partition_broadcast(128), tfgb, writes=[tfgb])
        sY = [sbs(st, "sY%d" % i, [128, 4, 8, 64], BF16) for i in range(2)]
        tsY = [Tk("sY%d" % i) for i in range(2)]
        sl_ = sbs(st, "sl_", [128, 4, 512], BF16)
        tsl_ = Tk("sl_")
        att = [sbs(st, "att%d" % i, [128, 4, 512], BF16) for i in range(2)]
        tatt = [Tk("att%d" % i) for i in range(2)]
        zb = [sbs(st, "zb%d" % i, [128, 4, 512], BF16) for i in range(2)]
        tzb = [Tk("zb%d" % i) for i in range(2)]
        gab = [sbs(st, "gab%d" % i, [128, 8, 512], BF16) for i in range(2)]
        tgab = [Tk("gab%d" % i) for i in range(2)]
        gbb = [sbs(st, "gbb%d" % i, [128, 8, 512], BF16) for i in range(2)]
        tgbb = [Tk("gbb%d" % i) for i in range(2)]
        gl = sbs(st, "gl", [128, 4, 512], BF16)
        tgl = Tk("gl")
        s2 = sbs(st, "s2", [128, 4, 512], BF16)
        ts2 = Tk("s2")
        ma = sbs(st, "ma", [128, 512], F32)
        mb = sbs(st, "mb", [128, 512], F32)
        tma, tmb = Tk("ma"), Tk("mb")
        mg = sbs(st, "mg", [128, 8, 512], BF16)
        tmg = Tk("mg")
        xr = [sbs(st, "xr%d" % i, [128, 1024], F32) for i in range(2)]
        txr = [Tk("xr%d" % i) for i in range(2)]
        yt = [sbs(st, "yt%d" % i, [128, 1024], F32) for i in range(2)]
        tyt = [Tk("yt%d" % i) for i in range(2)]
        junk5 = sbs(st, "junk5", [128, 1024], BF16)
        tjunk5 = Tk("junk5")
        ss5 = [sbs(st, "ss5_%d" % i, [128, 1], F32) for i in range(2)]
        tss5 = [Tk("ss5_%d" % i) for i in range(2)]
        ty_ = Tk("y")
        pcn = 0
        xc = 0
        for tb in range(NB):
            bb = tb % 2
            tsl = slice(tb * 512, (tb + 1) * 512)
            for g in range(32):
                P.dma("sp", sY[bb][(g % 8) * 16:(g % 8 + 1) * 16, g // 8, :, :],
                      Yd[g, :, tb * 64:(tb + 1) * 64].rearrange("(t co) c -> co t c", co=16), tsY[bb],
                      reads=[tYd], writes=[tsY[bb]])
            P.dma("sp", att[bb], ATT[:, tsl].rearrange("(k p) t -> p k t", p=128), tatt[bb], reads=[tATT],
                  writes=[tatt[bb]])
            P.dma("sp", zb[bb], ZB[:, tsl].rearrange("(k p) t -> p k t", p=128), tzb[bb], reads=[tZB], writes=[tzb[bb]])
            P.dma("sp", gab[bb], GA[:, tsl].rearrange("(k p) t -> p k t", p=128), tgab[bb], reads=[tGA],
                  writes=[tgab[bb]])
            P.dma("sp", gbb[bb], GB[:, tsl].rearrange("(k p) t -> p k t", p=128), tgbb[bb], reads=[tGB],
                  writes=[tgbb[bb]])
            for kc in range(4):
                P.op("dve", lambda e, kc=kc, bb=bb: e.tensor_copy(
                    out=sl_[:, kc, :].rearrange("p (c t) -> p c t", t=8), in_=sY[bb][:, kc].rearrange("p t c -> p c t")),
                    reads=[tsY[bb]], writes=[tsl_])
            for ct in range(4):
                pi = pcn % 6
                pcn += 1
                for kc in range(4):
                    P.op("pe", lambda e, pi=pi, kc=kc, ct=ct: e.matmul(
                        psb[pi], lhsT=wgl[:, kc, ct * 128:(ct + 1) * 128], rhs=sl_[:, kc, :], start=(kc == 0),
                        stop=(kc == 3)), reads=[twgl, tsl_], writes=[tps[pi]])
                P.op("act", lambda e, pi=pi, ct=ct: e.activation(out=gl[:, ct, :], in_=psb[pi], func=AF.Sigmoid),
                     reads=[tps[pi]], writes=[tgl])
            P.op("dve", lambda e: e.tensor_tensor(out=s2, in0=sl_, in1=gl, op=ALU.mult), reads=[tsl_, tgl], writes=[ts2])
            P.op("dve", lambda e, bb=bb: e.tensor_tensor(out=s2, in0=s2, in1=zb[bb], op=ALU.mult), reads=[ts2, tzb[bb]],
                 writes=[ts2])
            for ct in range(8):
                pa_i, pb_i = pcn % 6, (pcn + 1) % 6
                pcn += 2
                for kc in range(4):
                    P.op("pe", lambda e, pa_i=pa_i, kc=kc, ct=ct, bb=bb: e.matmul(
                        psb[pa_i], lhsT=wa[:, kc, ct * 128:(ct + 1) * 128], rhs=att[bb][:, kc, :], start=(kc == 0),
                        stop=(kc == 3)), reads=[twa, tatt[bb]], writes=[tps[pa_i]])
                for kc in range(4):
                    P.op("pe", lambda e, pb_i=pb_i, kc=kc, ct=ct: e.matmul(
                        psb[pb_i], lhsT=wb[:, kc, ct * 128:(ct + 1) * 128], rhs=s2[:, kc, :], start=(kc == 0),
                        stop=(kc == 3)), reads=[twb, ts2], writes=[tps[pb_i]])
                P.op("dve", lambda e, pa_i=pa_i, ct=ct, bb=bb: e.tensor_tensor(out=ma, in0=psb[pa_i], in1=gab[bb][:, ct, :],
                                                                              op=ALU.mult),
                     reads=[tps[pa_i], tgab[bb]], writes=[tma])
                P.op("dve", lambda e, pb_i=pb_i, ct=ct, bb=bb: e.tensor_tensor(out=mb, in0=psb[pb_i], in1=gbb[bb][:, ct, :],
                                                                              op=ALU.mult),
                     reads=[tps[pb_i], tgbb[bb]], writes=[tmb])
                P.op("dve", lambda e, ct=ct: e.tensor_tensor(out=mg[:, ct, :], in0=ma, in1=mb, op=ALU.add),
                     reads=[tma, tmb], writes=[tmg])
            for t4 in range(4):
                tt = tb * 4 + t4
                xb = xc % 2
                xc += 1
                P.dma("sp", xr[xb], x[tt * 128:(tt + 1) * 128, :], txr[xb], writes=[txr[xb]])
                for hf in range(2):
                    pi = pcn % 6
                    pcn += 1
                    for kc in range(8):
                        P.op("pe", lambda e, pi=pi, kc=kc, t4=t4, hf=hf: e.matmul(
                            psb[pi], lhsT=mg[:, kc, t4 * 128:(t4 + 1) * 128], rhs=wo[:, kc, hf * 512:(hf + 1) * 512],
                            start=(kc == 0), stop=(kc == 7)), reads=[tmg, two], writes=[tps[pi]])
                    P.op("dve", lambda e, pi=pi, hf=hf, xb=xb: e.tensor_tensor(
                        out=yt[xb][:, hf * 512:(hf + 1) * 512], in0=psb[pi], in1=xr[xb][:, hf * 512:(hf + 1) * 512],
                        op=ALU.add), reads=[tps[pi], txr[xb]], writes=[tyt[xb]])
                P.op("act", lambda e, xb=xb: e.activation(out=junk5, in_=yt[xb], func=AF.Square, accum_out=ss5[xb]),
                     reads=[tyt[xb]], writes=[tjunk5, tss5[xb]])
                P.op("dve", lambda e, xb=xb: e.tensor_scalar(out=ss5[xb], in0=ss5[xb], scalar1=1.0 / D, scalar2=EPS,
                                                             op0=ALU.mult, op1=ALU.add), reads=[tss5[xb]],
                     writes=[tss5[xb]])
                P.op("act", lambda e, xb=xb: e.activation(out=ss5[xb], in_=ss5[xb], func=AF.Ln),
                     reads=[tss5[xb]], writes=[tss5[xb]])
                P.op("act", lambda e, xb=xb: e.activation(out=ss5[xb], in_=ss5[xb], func=AF.Exp, scale=-0.5),
                     reads=[tss5[xb]], writes=[tss5[xb]])
                P.op("act", lambda e, xb=xb: e.activation(out=yt[xb], in_=yt[xb], func=AF.Copy, scale=ss5[xb]),
                     reads=[tyt[xb], tss5[xb]], writes=[tyt[xb]])
                P.op("dve", lambda e, xb=xb: e.tensor_tensor(out=xr[xb], in0=yt[xb], in1=fgb, op=ALU.mult),
                     reads=[tyt[xb], tfgb], writes=[txr[xb]])
                P.dma("act", y[tt * 128:(tt + 1) * 128, :], xr[xb], txr[xb], reads=[txr[xb]], writes=[ty_])
        P.barrier()
        P.emit([ty_])
        return nc, dbg_outs


def rope_tables():
    half = 32
    inv = (np.float32(10000.0) ** (-np.arange(half, dtype=np.float32) / half)).astype(np.float32)
    pos = np.arange(L, dtype=np.float32)
    ang = (pos[None, :] * inv[:, None]).astype(np.float32)
    c = np.cos(ang.astype(np.float64)).astype(np.float32)
    s = np.sin(ang.astype(np.float64)).astype(np.float32)
    return np.tile(c, (4, 1)), np.tile(s, (4, 1))


def band_mask():
    p = np.arange(128)[:, None]
    q = np.arange(256)[None, :]
    return (((q - p) >= 0) & ((q - p) <= 128)).astype(np.float32)


def ssm_layouts(inputs):
    f = np.float32
    def q2(a):
        t = a.reshape(64, 64).T
        return np.ascontiguousarray(np.concatenate([t, t], 0).astype(f))
    lamr = q2(inputs["lam_re"][0]); lami = q2(inputs["lam_im"][0])
    lstep = np.ascontiguousarray(np.broadcast_to(inputs["log_step"][0].reshape(1, 64), (128, 64)).astype(f))
    br = inputs["b_re"][0].reshape(64, 64, 16).transpose(1, 0, 2)
    bi = inputs["b_im"][0].reshape(64, 64, 16).transpose(1, 0, 2)
    cr = inputs["c_re"][0].reshape(64, 16, 64).transpose(2, 0, 1)
    ci = inputs["c_im"][0].reshape(64, 16, 64).transpose(2, 0, 1)
    cat = lambda a, b: np.ascontiguousarray(np.concatenate([a, b], 0).astype(f))
    ex = np.zeros((128, 2, 4, 8), f)
    t = np.arange(8, dtype=f)
    for r in range(2):
        tp = t if r == 0 else 7 - t
        ex[:, r, 0, :] = 7 - tp
        ex[:, r, 1, :] = -(7 - tp)
        ex[:, r, 2, :] = tp + 1
        ex[:, r, 3, 0:3] = (1, 8, 64)
    sg = np.ones((128, 2), f); sg[64:, 0] = -1; sg[:64, 1] = -1
    jp = np.zeros((128, 128), f); jm = np.zeros((128, 128), f)
    for n in range(64):
        jp[n, n + 64] = 1; jp[n + 64, n] = 1
        jm[n + 64, n] = -1; jm[n, n + 64] = 1
    d = inputs["d_skip"][0].reshape(32, 16)
    dcol = np.ascontiguousarray(np.tile(d.T, (8, 1)).astype(f))
    cm = np.zeros((128, 2, 128), f)
    for s_ in range(8):
        for t_ in range(8):
            if t_ >= s_:
                cm[s_ * 16:(s_ + 1) * 16, 0, t_ * 16:(t_ + 1) * 16] = 1
            if t_ <= s_:
                cm[s_ * 16:(s_ + 1) * 16, 1, t_ * 16:(t_ + 1) * 16] = 1
    return {"lamr_q": lamr, "lami_q": lami, "lstep_q": lstep, "ex_tab": ex.reshape(128, 2, 32), "sg_tab": sg,
            "ba_q": cat(br, bi), "bb_q": cat(bi, br), "ca_q": cat(cr, ci), "cb_q": cat(ci, cr),
            "dcol": dcol, "cmask": cm, "jm": jm, "jp": jp}


def make_in_maps(inputs):
    xs = [inputs["x_prompt"][i] for i in range(4)] + [inputs["x_sample"][i] for i in range(2)]
    xs = xs + [xs[0], xs[1]]
    cosT, sinT = rope_tables()
    common = {
        "w_in": np.ascontiguousarray(inputs["w_in"][0]),
        "norm_g": np.ascontiguousarray(inputs["norm_g"][0].reshape(8, 128).T),
        "ident": np.eye(128, dtype=np.float32),
        "cosT": cosT, "sinT": sinT,
        "mask01": band_mask(),
        **ssm_layouts(inputs),
        "w_a": np.ascontiguousarray(inputs["w_branch_a"][0]), "w_b": np.ascontiguousarray(inputs["w_branch_b"][0]),
        "w_glu": np.ascontiguousarray(inputs["w_glu"][0]), "w_out": np.ascontiguousarray(inputs["w_out"][0]),
        "final_g": np.ascontiguousarray(inputs["final_g"]),
    }
    return [dict(common, x=np.ascontiguousarray(xs[c])) for c in range(NCORES)]


def kernel(**inputs):
    nc, _ = build()
    in_maps = make_in_maps(inputs)
    res = run_bass_kernel_spmd(nc, in_maps, core_ids=list(range(NCORES)))
    ys = [np.asarray(r["y"]).reshape(L, D) for r in res.results]
    return (np.stack(ys[0:4]).astype(np.float32), np.stack(ys[4:6]).astype(np.float32))
```

```python
import numpy as np
import ml_dtypes
import concourse.bass as bass
import concourse.mybir as mybir
from concourse.bass_utils import run_bass_kernel_spmd

F32 = mybir.dt.float32
BF16 = mybir.dt.bfloat16
ALU = mybir.AluOpType
AF = mybir.ActivationFunctionType

L = 8192
D = 1024
NCORES = 8
EPS = 1e-6
EPOCH = 30000


class Tk:
    __slots__ = ("w", "rs", "dsem", "dcnt", "name")

    def __init__(self, name=""):
        self.w = {}
        self.rs = {}
        self.dsem = None
        self.dcnt = 0
        self.name = name


class Prog:
    ENG = ("pe", "act", "dve", "pool", "sp")

    def __init__(self, nc):
        self.nc = nc
        self.ops = {e: [] for e in self.ENG}
        self.cnt = {e: 0 for e in self.ENG}
        self.esems = {e: [] for e in self.ENG}
        self.waited = {e: {} for e in self.ENG}
        self.nsem = 0
        self.dtiles = []

    _uid = [0]

    def newsem(self, name):
        self.nsem += 1
        Prog._uid[0] += 1
        return self.nc.alloc_semaphore("%s_%d" % (name, Prog._uid[0]))

    def _esem(self, e, idx):
        lst = self.esems[e]
        while len(lst) <= idx:
            lst.append(self.newsem("es_%s_%d" % (e, len(lst))))
        return lst[idx]

    def _collect(self, e, reads, writes):
        need = {}
        for t in reads:
            for s, v in t.w.items():
                if need.get(s, 0) < v:
                    need[s] = v
        for t in writes:
            for s, v in t.w.items():
                if need.get(s, 0) < v:
                    need[s] = v
            for s, v in t.rs.items():
                if need.get(s, 0) < v:
                    need[s] = v
        waits = []
        wd = self.waited[e]
        own = set(id(s) for s in self.esems[e]) if e == "pe" else ()
        for s, v in need.items():
            if id(s) in own:
                continue
            if wd.get(s, 0) >= v:
                continue
            wd[s] = v
            waits.append((s, v))
        return waits

    def op(self, e, fn, reads=(), writes=()):
        waits = self._collect(e, reads, writes)
        seq = self.cnt[e]
        self.cnt[e] += 1
        sem = self._esem(e, seq // EPOCH)
        val = seq % EPOCH + 1
        self.ops[e].append((waits, fn, sem, 1))
        for t in writes:
            t.w = {sem: val}
            t.rs = {}
        for t in reads:
            if t.rs.get(sem, 0) < val:
                t.rs[sem] = val

    def dma(self, e, out, in_, sb, reads=(), writes=()):
        waits = self._collect(e, reads, writes)
        if sb.dsem is None:
            sb.dsem = self.newsem("ds_" + sb.name)
            self.dtiles.append(sb)
        sb.dcnt += 16
        sem, val = sb.dsem, sb.dcnt
        self.ops[e].append((waits, lambda eng: eng.dma_start(out=out, in_=in_), sem, 16))
        for t in writes:
            if t is sb:
                t.w = {sem: val}
                t.rs = {}
            else:
                t.w[sem] = val
        for t in reads:
            if t.rs.get(sem, 0) < val:
                t.rs[sem] = val

    def barrier(self):
        ev = {}
        for e in self.ENG:
            n = self.cnt[e]
            if n > 0:
                ev[self.esems[e][(n - 1) // EPOCH]] = (n - 1) % EPOCH + 1
        for t in self.dtiles:
            ev[t.dsem] = t.dcnt
        for e in self.ENG:
            wd = self.waited[e]
            own = set(id(x) for x in self.esems[e])
            waits = []
            for sm, v in ev.items():
                if id(sm) in own or wd.get(sm, 0) >= v:
                    continue
                wd[sm] = v
                waits.append((sm, v))
            if waits:
                self.ops[e].append((waits, None, None, 0))

    def emit(self, final_tiles):
        nc = self.nc
        fin = {}
        for t in final_tiles:
            for s, v in list(t.w.items()) + list(t.rs.items()):
                if fin.get(s, 0) < v:
                    fin[s] = v
        with nc.Block() as block:
            def body(e):
                def run(eng):
                    for waits, fn, sem, inc in self.ops[e]:
                        for s, v in waits:
                            eng.wait_ge(s, v)
                        if fn is not None:
                            fn(eng).then_inc(sem, inc)
                    if e == "sp":
                        for s, v in fin.items():
                            eng.wait_ge(s, v)
                return run
            block.tensor(body("pe"))
            block.scalar(body("act"))
            block.vector(body("dve"))
            block.gpsimd(body("pool"))
            block.sync(body("sp"))


def ssl(start, count, step):
    return slice(start, start + (count - 1) * step + 1, step)


def build(debug=None):
    nc = bass.Bass("TRN2", target_bir_lowering=False)
    P = Prog(nc)

    def din(name, shape, dt=F32):
        return nc.dram_tensor(name, list(shape), dt, kind="ExternalInput").ap()

    dbg_outs = {}

    def dscr(name, shape, dt=BF16):
        kind = "ExternalOutput" if (debug and name in debug) else "Internal"
        if debug and ("in_" + name) in debug:
            kind = "ExternalInput"
        t = nc.dram_tensor(name, list(shape), dt, kind=kind).ap()
        if kind == "ExternalOutput":
            dbg_outs[name] = t
        return t

    dbg_tiles = []

    def dump(name, ap, tk, dt):
        if not (debug and "dumps" in debug):
            return
        t = nc.dram_tensor("dbg_" + name, list(ap.shape), dt, kind="ExternalOutput").ap()
        dbg_outs["dbg_" + name] = t
        dt_ = Tk("dbgd_" + name)
        dbg_tiles.append(dt_)
        P.dma("sp", t, ap, tk, reads=[tk], writes=[dt_])

    x = din("x", [L, D])
    w_in = din("w_in", [D, 8192])
    norm_g = din("norm_g", [128, 8])
    ident_in = din("ident", [128, 128])
    cos_in = din("cosT", [128, L])
    sin_in = din("sinT", [128, L])
    mask_in = din("mask01", [128, 256])
    jm_in = din("jm", [128, 128])
    jp_in = din("jp", [128, 128])
    lamr_in = din("lamr_q", [128, 64])
    lami_in = din("lami_q", [128, 64])
    lstep_in = din("lstep_q", [128, 64])
    ex_in = din("ex_tab", [128, 2, 32])
    sg_in = din("sg_tab", [128, 2])
    ba_in = din("ba_q", [128, 64, 16])
    bb_in = din("bb_q", [128, 64, 16])
    ca_in = din("ca_q", [128, 64, 16])
    cb_in = din("cb_q", [128, 64, 16])
    dcol_in = din("dcol", [128, 32])
    cmask_in = din("cmask", [128, 2, 128])
    wa_in = din("w_a", [512, 1024])
    wb_in = din("w_b", [512, 1024])
    wglu_in = din("w_glu", [512, 512])
    wout_in = din("w_out", [1024, 1024])
    fg_in = din("final_g", [1024])
    y = nc.dram_tensor("y", [L, D], F32, kind="ExternalOutput").ap()

    QT = dscr("QT", [2, 24, 32, L])
    KT = dscr("KT", [2, 24, 32, L])
    Vd = dscr("Vd", [L, 1536])
    ZA = dscr("ZA", [512, L])
    Ud = dscr("Ud", [4, 8, 128, 1024])
    ZB = dscr("ZB", [512, L])
    GA = dscr("GA", [1024, L])
    GB = dscr("GB", [1024, L])
    tQT, tKT, tVd, tZA, tUd, tZB, tGA, tGB = [Tk(n) for n in
                                              ("QT", "KT", "Vd", "ZA", "Ud", "ZB", "GA", "GB")]

    def sb(name, shape, dt):
        return nc.alloc_sbuf_tensor(name, list(shape), dt).ap()

    def ps(name, shape, dt=F32):
        return nc.alloc_psum_tensor(name, list(shape), dt).ap()

    ident_f = sb("ident_f", [128, 128], F32)
    ident_b = sb("ident_b", [128, 128], BF16)
    gcol = sb("gcol", [128, 8], F32)
    t_ident_f, t_ident_b, t_gcol = Tk("identf"), Tk("identb"), Tk("gcol")
    def load_consts():
        P.dma("sp", ident_f, ident_in, t_ident_f, writes=[t_ident_f])
        P.dma("sp", gcol, norm_g, t_gcol, writes=[t_gcol])
        P.op("dve", lambda e: e.tensor_copy(out=ident_b, in_=ident_f), reads=[t_ident_f], writes=[t_ident_b])
    load_consts()

    psb = [ps("psb%d" % i, [128, 512]) for i in range(7)]
    tps = [Tk("psb%d" % i) for i in range(7)]
    pst = ps("pst", [128, 1024], BF16)
    tpst = Tk("pst")

    from contextlib import ExitStack

    def sbs(st, name, shape, dt):
        h = st.enter_context(nc.sbuf_tensor(name, list(shape), dt))
        return h.ap() if hasattr(h, "ap") else h[:]

    NTT = L // 128
    NB = L // 512
    w_in_v = w_in.rearrange("(k p) c -> p k c", p=128)
    st12 = ExitStack()
    hT = sbs(st12, "hT", [128, 8, L], BF16)
    thT = [Tk("hT%d" % i) for i in range(NB)]

    with ExitStack() as st:
        xt = [sbs(st, "xt%d" % i, [128, D], F32) for i in range(2)]
        txt = [Tk("xt%d" % i) for i in range(2)]
        xn = [sbs(st, "xn%d" % i, [128, D], BF16) for i in range(2)]
        txn = [Tk("xn%d" % i) for i in range(2)]
        junk = sbs(st, "junk", [128, D], BF16)
        tjunk = Tk("junk")
        ss = [sbs(st, "ss%d" % i, [128, 1], F32) for i in range(2)]
        tss = [Tk("ss%d" % i) for i in range(2)]
        rs_ = [sbs(st, "rs%d" % i, [128, 1], F32) for i in range(2)]
        trs = [Tk("rs%d" % i) for i in range(2)]
        for tt in range(NTT):
            b = tt % 2
            P.dma("sp", xt[b], x[tt * 128:(tt + 1) * 128, :], txt[b], writes=[txt[b]])
            P.op("act", lambda e, b=b: e.activation(out=junk, in_=xt[b], func=AF.Square, accum_out=ss[b]),
                 reads=[txt[b]], writes=[tjunk, tss[b]])
            P.op("dve", lambda e, b=b: e.tensor_scalar(out=rs_[b], in0=ss[b], scalar1=1.0 / D, scalar2=EPS,
                                                       op0=ALU.mult, op1=ALU.add),
                 reads=[tss[b]], writes=[trs[b]])
            P.op("act", lambda e, b=b: e.activation(out=rs_[b], in_=rs_[b], func=AF.Ln),
                 reads=[trs[b]], writes=[trs[b]])
            P.op("act", lambda e, b=b: e.activation(out=rs_[b], in_=rs_[b], func=AF.Exp, scale=-0.5),
                 reads=[trs[b]], writes=[trs[b]])
            P.op("act", lambda e, b=b: e.activation(out=xn[b], in_=xt[b], func=AF.Copy, scale=rs_[b]),
                 reads=[txt[b], trs[b]], writes=[txn[b]])
            for k in range(8):
                P.op("pe", lambda e, b=b, k=k: e.transpose(out=pst[:, k * 128:(k + 1) * 128],
                                                          in_=xn[b][:, k * 128:(k + 1) * 128], identity=ident_b),
                     reads=[txn[b], t_ident_b], writes=[tpst])
            P.op("dve", lambda e, tt=tt: e.tensor_copy(out=hT[:, :, tt * 128:(tt + 1) * 128],
                                                       in_=pst.rearrange("p (k t) -> p k t", k=8)),
                 reads=[tpst], writes=[thT[tt // 4]])
        P.barrier()

    with ExitStack() as st:
        wst = [sbs(st, "wst%d" % i, [128, 8, 256], F32) for i in range(1)]
        twst = [Tk("wst%d" % i) for i in range(1)]
        wcnt = [0]

        def load_w(dst_ap, tdst, c0, n, perm=False):
            for j in range(n // 256):
                b = wcnt[0] % len(wst)
                wcnt[0] += 1
                P.dma("sp", wst[b], w_in_v[:, :, c0 + j * 256:c0 + (j + 1) * 256], twst[b], writes=[twst[b]])
                for k in range(8):
                    if perm:
                        o_ap = dst_ap[:, k, :].rearrange("p (two h i) -> p h two i", two=2, i=32)[:, 4 * j:4 * j + 4]
                        i_ap = wst[b][:, k, :].rearrange("p (h two i) -> p h two i", two=2, i=32)
                    else:
                        o_ap = dst_ap[:, k, j * 256:(j + 1) * 256]
                        i_ap = wst[b][:, k, :]
                    P.op("dve", lambda e, o_ap=o_ap, i_ap=i_ap, k=k: e.tensor_scalar(
                        out=o_ap, in0=i_ap, scalar1=gcol[:, k:k + 1], scalar2=None, op0=ALU.mult),
                        reads=[twst[b], t_gcol], writes=[tdst])

        wbig = sbs(st, "wbig", [128, 8, 1536], BF16)
        twbig = Tk("wbig")
        wg = [sbs(st, "wg%d" % i, [128, 8, 512], BF16) for i in range(2)]
        twg = [Tk("wg%d" % i) for i in range(2)]
        stg = [sbs(st, "stg%d" % i, [128, 512], BF16) for i in range(3)]
        tstg = [Tk("stg%d" % i) for i in range(3)]

        def zug_gen():
            jobs = [(4608, "za", 0), (5632, "zb", 0), (5120, "u", 0), (6144, "ga", 0), (6656, "ga", 512),
                    (7168, "gb", 0), (7680, "gb", 512)]
            ZBK = (4, 5, 6)
            sc = 0
            zc = 0
            for ji, (c0, kind, roff) in enumerate(jobs):
                wb_ = ji % 2
                load_w(wg[wb_], twg[wb_], c0, 512)
                for sub in range(4):
                    for tb in range(NB):
                        pi = ZBK[zc % 3]
                        zc += 1
                        tsl = slice(tb * 512, (tb + 1) * 512)
                        for k in range(8):
                            P.op("pe", lambda e, pi=pi, k=k, wb_=wb_, sub=sub, tsl=tsl: e.matmul(
                                psb[pi], lhsT=wg[wb_][:, k, sub * 128:(sub + 1) * 128], rhs=hT[:, k, tsl],
                                start=(k == 0), stop=(k == 7)),
                                reads=[twg[wb_], thT[tb]], writes=[tps[pi]])
                        s_ = sc % 3
                        sc += 1
                        rows = slice(roff + sub * 128, roff + (sub + 1) * 128)
                        if kind == "u":
                            P.op("act", lambda e, pi=pi, s_=s_: e.activation(
                                out=stg[s_].rearrange("p (t c) -> p t c", t=8),
                                in_=psb[pi].rearrange("p (c t) -> p t c", t=8), func=AF.Copy),
                                reads=[tps[pi]], writes=[tstg[s_]])
                            P.dma("act", Ud[sub, :, :, tb * 64:(tb + 1) * 64].rearrange("t p c -> p t c"),
                                  stg[s_].rearrange("p (t c) -> p t c", t=8),
                                  tstg[s_], reads=[tstg[s_]], writes=[tUd])
                        else:
                            fn = AF.Silu if kind in ("za", "zb") else AF.Sigmoid
                            dd, td = {"za": (ZA, tZA), "zb": (ZB, tZB), "ga": (GA, tGA), "gb": (GB, tGB)}[kind]
                            P.op("act", lambda e, pi=pi, s_=s_, fn=fn: e.activation(out=stg[s_], in_=psb[pi], func=fn),
                                 reads=[tps[pi]], writes=[tstg[s_]])
                            P.dma("act", dd[rows, tsl], stg[s_], tstg[s_], reads=[tstg[s_]], writes=[td])
                        yield

        zg = zug_gen()
        st2a = ExitStack()
        cs = [sbs(st2a, "cs%d" % i, [128, 2, 512], F32) for i in range(2)]
        tcs = [Tk("cs%d" % i) for i in range(2)]
        rtmp = [sbs(st2a, "rtmp%d" % i, [128, 512], F32) for i in range(4)]
        trtmp = [Tk("rtmp%d" % i) for i in range(4)]
        ro = [sbs(st2a, "ro%d" % i, [128, 2, 512], BF16) for i in range(3)]
        tro = [Tk("ro%d" % i) for i in range(3)]
        rcnt = 0
        ccnt = 0
        for qk in range(2):
            load_w(wbig, twbig, qk * 1536, 1536, perm=True)
            dst, tdst = (QT, tQT) if qk == 0 else (KT, tKT)
            for tb in range(NB):
                cb = ccnt % 2
                ccnt += 1
                tsl = slice(tb * 512, (tb + 1) * 512)
                P.dma("sp", cs[cb][:, 0, :], cos_in[:, tsl], tcs[cb], writes=[tcs[cb]])
                P.dma("sp", cs[cb][:, 1, :], sin_in[:, tsl], tcs[cb], writes=[tcs[cb]])
                for ht in range(6):
                    ia = 2 * (ht % 2)
                    pa, pb = psb[ia], psb[ia + 1]
                    ta, tb_ = tps[ia], tps[ia + 1]
                    for half, (pp, tp) in enumerate(((pa, ta), (pb, tb_))):
                        for k in range(8):
                            P.op("pe", lambda e, pp=pp, k=k, ht=ht, half=half, tsl=tsl: e.matmul(
                                pp, lhsT=wbig[:, k, half * 768 + ht * 128:half * 768 + (ht + 1) * 128], rhs=hT[:, k, tsl],
                                start=(k == 0), stop=(k == 7)),
                                reads=[twbig, thT[tb]], writes=[tp])
                    r = rcnt % 3
                    rcnt += 1
                    C, S = cs[cb][:, 0, :], cs[cb][:, 1, :]
                    P.op("dve", lambda e, pa=pa, C=C: e.tensor_tensor(out=rtmp[0], in0=pa, in1=C, op=ALU.mult),
                         reads=[ta, tcs[cb]], writes=[trtmp[0]])
                    P.op("dve", lambda e, pb=pb, S=S: e.tensor_tensor(out=rtmp[1], in0=pb, in1=S, op=ALU.mult),
                         reads=[tb_, tcs[cb]], writes=[trtmp[1]])
                    P.op("dve", lambda e, r=r: e.tensor_tensor(out=ro[r][:, 0, :], in0=rtmp[0], in1=rtmp[1],
                                                               op=ALU.subtract),
                         reads=[trtmp[0], trtmp[1]], writes=[tro[r]])
                    P.op("dve", lambda e, pb=pb, C=C: e.tensor_tensor(out=rtmp[2], in0=pb, in1=C, op=ALU.mult),
                         reads=[tb_, tcs[cb]], writes=[trtmp[2]])
                    P.op("dve", lambda e, pa=pa, S=S: e.tensor_tensor(out=rtmp[3], in0=pa, in1=S, op=ALU.mult),
                         reads=[ta, tcs[cb]], writes=[trtmp[3]])
                    P.op("dve", lambda e, r=r: e.tensor_tensor(out=ro[r][:, 1, :], in0=rtmp[2], in1=rtmp[3],
                                                               op=ALU.add),
                         reads=[trtmp[2], trtmp[3]], writes=[tro[r]])
                    h0 = ht * 4
                    for half in range(2):
                        P.dma("sp", dst[half, h0:h0 + 4, :, tsl].rearrange("h i t -> (h i) t"),
                              ro[r][:, half, :], tro[r], reads=[tro[r]], writes=[tdst])
                    for _ in range(3):
                        next(zg, None)
        for _ in zg:
            pass
        P.barrier()
        st2a.close()
        load_w(wbig, twbig, 3072, 1536)
        vst = [sbs(st, "vst%d" % i, [128, 1536], BF16) for i in range(2)]
        tvst = [Tk("vst%d" % i) for i in range(2)]
        pc = 0
        for tt in range(NTT):
            vb = tt % 2
            for cb3 in range(3):
                pi = pc % 6
                pc += 1
                for k in range(8):
                    P.op("pe", lambda e, pi=pi, k=k, tt=tt, cb3=cb3: e.matmul(
                        psb[pi], lhsT=hT[:, k, tt * 128:(tt + 1) * 128], rhs=wbig[:, k, cb3 * 512:(cb3 + 1) * 512],
                        start=(k == 0), stop=(k == 7)),
                        reads=[twbig, thT[tt // 4]], writes=[tps[pi]])
                P.op("act", lambda e, pi=pi, vb=vb, cb3=cb3: e.activation(
                    out=vst[vb][:, cb3 * 512:(cb3 + 1) * 512], in_=psb[pi], func=AF.Copy),
                    reads=[tps[pi]], writes=[tvst[vb]])
            P.dma("act", Vd[tt * 128:(tt + 1) * 128, :], vst[vb], tvst[vb], reads=[tvst[vb]], writes=[tVd])
        P.barrier()
    st12.close()
    if debug and "stop2" in debug:
        P.emit([tQT, tKT, tVd, tZA, tUd, tZB, tGA, tGB])
        return nc, dbg_outs

    ATT = dscr("ATT", [512, L])
    tATT = Tk("ATT")
    PADK = 1024
    with ExitStack() as st:
        QTh = [sbs(st, "QTh%d" % i, [64, L], BF16) for i in range(2)]
        tQTh = [Tk("QTh%d" % i) for i in range(2)]
        KTh = [sbs(st, "KTh%d" % i, [64, L + 2 * PADK], BF16) for i in range(2)]
        tKTh = [Tk("KTh%d" % i) for i in range(2)]
        NVT = 80
        Vt = [sbs(st, "Vt%d" % i, [128, NVT, 64], BF16) for i in range(2)]
        tVt = [Tk("Vt%d" % i) for i in range(2)]
        ACC = sbs(st, "ACC", [64, 2, L], F32)
        tACC = Tk("ACC")
        pT = [sbs(st, "pT%d" % i, [128, 256], BF16) for i in range(3)]
        tpT = [Tk("pT%d" % i) for i in range(3)]
        onesk = sbs(st, "onesk", [128, 3, 64], BF16)
        tonesk = Tk("onesk")
        mask_f = sbs(st, "mask_f", [128, 256], F32)
        mask_b = sbs(st, "mask_b", [128, 256], BF16)
        tmask_f, tmask_b = Tk("maskf"), Tk("maskb")
        za_t = sbs(st, "za_t", [64, 2048], BF16)
        tza = Tk("za_t")
        dv = sbs(st, "dv", [64, 2048], F32)
        tdv = Tk("dv")
        ao = [sbs(st, "ao%d" % i, [64, 2048], BF16) for i in range(2)]
        tao = [Tk("ao%d" % i) for i in range(2)]

        P.dma("sp", mask_f, mask_in, tmask_f, writes=[tmask_f])
        P.op("dve", lambda e: e.tensor_copy(out=mask_b, in_=mask_f), reads=[tmask_f], writes=[tmask_b])
        P.op("dve", lambda e: e.memset(onesk, 1.0), writes=[tonesk])
        P.op("dve", lambda e: e.memset(onesk[0:64, 1, :], 0.0), writes=[tonesk])
        P.op("dve", lambda e: e.memset(onesk[64:128, 2, :], 0.0), writes=[tonesk])
        for i in range(2):
            P.op("dve", lambda e, i=i: e.memset(KTh[i][:, 0:PADK], 0.0), writes=[tKTh[i]])
            P.op("dve", lambda e, i=i: e.memset(KTh[i][:, PADK + L:], 0.0), writes=[tKTh[i]])

        DIL = (1, 4, 16)
        slot = 0
        sidx = 0
        oidx = 0
        pidx = 0
        for h in range(8):
            for g in range(3):
                d = DIL[g]
                hg = g * 8 + h
                Ls = L // d
                nqb = Ls // 128
                nt = nqb + 1
                sl = slot % 2
                slot += 1
                for half in range(2):
                    P.dma("sp", QTh[sl][half * 32:(half + 1) * 32, :], QT[half, hg, :, :], tQTh[sl],
                          writes=[tQTh[sl]])
                    P.dma("sp", KTh[sl][half * 32:(half + 1) * 32, PADK:PADK + L], KT[half, hg, :, :], tKTh[sl],
                          writes=[tKTh[sl]])
                P.op("dve", lambda e, sl=sl: e.memset(Vt[sl], 0.0), writes=[tVt[sl]])
                vcols = slice(hg * 64, (hg + 1) * 64)
                for r in range(d):
                    base = r * nt
                    P.dma("sp", Vt[sl][64:128, base, :], Vd[ssl(r, 64, d), vcols], tVt[sl], reads=[tVd],
                          writes=[tVt[sl]])
                    j0 = 64
                    nmid = nt - 2
                    srcv = Vd[ssl(r + d * j0, 128 * nmid, d), vcols].rearrange("(tau p) c -> p tau c", p=128)
                    P.dma("sp", Vt[sl][:, base + 1:base + 1 + nmid, :], srcv, tVt[sl], reads=[tVd],
                          writes=[tVt[sl]])
                    jl = Ls - 64
                    P.dma("sp", Vt[sl][0:64, base + nt - 1, :], Vd[ssl(r + d * jl, 64, d), vcols], tVt[sl],
                          reads=[tVd], writes=[tVt[sl]])
                SBK = (0, 1, 6)
                tl = [(r, tau) for r in range(d) for tau in range(nt)]
                info = {}
                ocur = {}

                def stS(ix):
                    r, tau = tl[ix]
                    qb_lo = max(tau - 1, 0)
                    qb_hi = min(tau, nqb - 1)
                    nq = (qb_hi - qb_lo + 1) * 128
                    mcol0 = (qb_lo - (tau - 1)) * 128
                    kstart = PADK + r + d * (128 * tau - 64)
                    kap = KTh[sl][:, ssl(kstart, 128, d)]
                    qstart = r + d * 128 * qb_lo
                    qap = QTh[sl][:, ssl(qstart, nq, d)]
                    sp_, tsp = psb[SBK[ix % 3]], tps[SBK[ix % 3]]
                    P.op("pe", lambda e, sp_=sp_, kap=kap, qap=qap, nq=nq: e.matmul(
                        sp_[:, 0:nq], lhsT=kap, rhs=qap, start=True, stop=True),
                        reads=[tKTh[sl], tQTh[sl]], writes=[tsp])
                    info[ix] = (r, tau, qb_lo, qb_hi, nq, mcol0, sp_, tsp)

                def stE(ix):
                    r, tau, qb_lo, qb_hi, nq, mcol0, sp_, tsp = info[ix]
                    pi_ = ix % 3
                    P.op("act", lambda e, sp_=sp_, pi_=pi_, nq=nq: e.activation(
                        out=pT[pi_][:, 0:nq], in_=sp_[:, 0:nq], func=AF.Exp, scale=0.125),
                        reads=[tsp], writes=[tpT[pi_]])
                    P.op("dve", lambda e, pi_=pi_, nq=nq, mcol0=mcol0: e.tensor_tensor(
                        out=pT[pi_][:, 0:nq], in0=pT[pi_][:, 0:nq], in1=mask_b[:, mcol0:mcol0 + nq],
                        op=ALU.mult),
                        reads=[tpT[pi_], tmask_b], writes=[tpT[pi_]])

                def stPV(ix, oidx_box):
                    r, tau, qb_lo, qb_hi, nq, mcol0, sp_, tsp = info.pop(ix)
                    pi_ = ix % 3
                    ok = 1 if tau == 0 else (2 if tau == nt - 1 else 0)
                    for qb in range(qb_lo, qb_hi + 1):
                        first = (qb == tau)
                        if first:
                            ocur[(r, qb)] = 2 + (oidx_box[0] % 4)
                            oidx_box[0] += 1
                        oi = ocur[(r, qb)]
                        op_ = psb[oi].rearrange("p (two q) -> p two q", two=2)
                        c0 = (qb - qb_lo) * 128
                        P.op("pe", lambda e, op_=op_, pi_=pi_, c0=c0, tile=r * nt + tau, first=first, sl=sl: e.matmul(
                            op_[0:64, 0, 0:128], lhsT=Vt[sl][:, tile, :], rhs=pT[pi_][:, c0:c0 + 128],
                            start=first, stop=False, skip_group_check=True),
                            reads=[tVt[sl], tpT[pi_]], writes=[tps[oi]])
                        P.op("pe", lambda e, op_=op_, pi_=pi_, c0=c0, ok=ok, first=first: e.matmul(
                            op_[0:64, 1, 0:128], lhsT=onesk[:, ok, :], rhs=pT[pi_][:, c0:c0 + 128],
                            start=False, stop=(not first), skip_group_check=True),
                            reads=[tonesk, tpT[pi_]], writes=[tps[oi]])
                        if not first:
                            t0 = r + d * 128 * qb
                            acc_ap = ACC[:, :, ssl(t0, 128, d)]
                            src = op_[0:64, :, 0:128]
                            if g == 0:
                                P.op("dve", lambda e, acc_ap=acc_ap, src=src: e.tensor_copy(out=acc_ap, in_=src),
                                     reads=[tps[oi]], writes=[tACC])
                            else:
                                P.op("dve", lambda e, acc_ap=acc_ap, src=src: e.tensor_tensor(
                                    out=acc_ap, in0=src, in1=acc_ap, op=ALU.add),
                                    reads=[tps[oi], tACC], writes=[tACC])

                n_t = len(tl)
                obox = [oidx]
                for i0 in range(min(3, n_t)):
                    stS(i0)
                stE(0)
                for ix in range(n_t):
                    if ix + 1 < n_t:
                        stE(ix + 1)
                    if ix + 3 < n_t:
                        stS(ix + 3)
                    stPV(ix, obox)
                oidx = obox[0]
            for c4 in range(4):
                csl = slice(c4 * 2048, (c4 + 1) * 2048)
                a_ = (h * 4 + c4) % 2
                P.dma("sp", za_t, ZA[h * 64:(h + 1) * 64, csl], tza, reads=[tZA], writes=[tza])
                P.op("act", lambda e, csl=csl: e.activation(out=dv, in_=ACC[:, 1, csl], func=AF.Ln), reads=[tACC],
                     writes=[tdv])
                P.op("act", lambda e: e.activation(out=dv, in_=dv, func=AF.Exp, scale=-1.0), reads=[tdv], writes=[tdv])
                P.op("dve", lambda e, csl=csl: e.tensor_tensor(out=dv, in0=dv, in1=ACC[:, 0, csl], op=ALU.mult),
                     reads=[tACC, tdv], writes=[tdv])
                P.op("dve", lambda e, a_=a_: e.tensor_tensor(out=ao[a_], in0=dv, in1=za_t, op=ALU.mult),
                     reads=[tdv, tza], writes=[tao[a_]])
                P.dma("act", ATT[h * 64:(h + 1) * 64, csl], ao[a_], tao[a_], reads=[tao[a_]], writes=[tATT])
        P.barrier()
    if debug and "stop3" in debug:
        P.emit([tATT, tQT, tKT, tVd, tZA, tUd, tZB, tGA, tGB])
        return nc, dbg_outs

    if debug and "only4" in debug:
        P = Prog(nc)
        for t_ in (t_ident_f, t_ident_b, t_gcol, tUd, tATT) + tuple(tps) + (tpst,):
            t_.w, t_.rs, t_.dsem, t_.dcnt = {}, {}, None, 0
        load_consts()
    Yd = dscr("Yd", [32, 128, 1024])
    tYd = Tk("Yd")
    PI = float(np.pi)
    stM = ExitStack()
    MATS = sbs(stM, "MATS", [128, 64, 4, 128], BF16)
    tMATS = [Tk("MATS%d" % i) for i in range(64)]
    a64 = sbs(stM, "a64", [128, 64], F32)
    b64 = sbs(stM, "b64", [128, 64], F32)
    nb64 = sbs(stM, "nb64", [128, 64], F32)
    tab64 = Tk("ab64")
    Jm_f = sbs(stM, "Jm_f", [128, 128], F32)
    tJm = Tk("Jm")
    P.dma("sp", Jm_f, jm_in, tJm, writes=[tJm])
    with ExitStack() as st:
        def small(name, shape, dt=F32):
            return sbs(st, "s4_" + name, shape, dt), Tk(name)
        lamr, tlamr = small("lamr", [128, 64])
        lami, tlami = small("lami", [128, 64])
        stp, tstp = small("stp", [128, 64])
        ar, tar = small("ar", [128, 64])
        ai, tai = small("ai", [128, 64])
        EX, tEX = small("EX", [128, 2, 32])
        sg, tsg = small("sg", [128, 2])
        BA, tBA = small("BA", [128, 64, 16])
        BB, tBB = small("BB", [128, 64, 16])
        CA, tCA = small("CA", [128, 64, 16])
        CB, tCB = small("CB", [128, 64, 16])
        dcol, tdcol = small("dcol", [128, 32])
        Jp, tJp = small("Jp", [128, 128])
        cmask, tcmask = small("cmask", [128, 2, 128])
        for dst, t_, src in ((lamr, tlamr, lamr_in), (lami, tlami, lami_in), (stp, tstp, lstep_in), (EX, tEX, ex_in),
                             (sg, tsg, sg_in), (BA, tBA, ba_in), (BB, tBB, bb_in), (CA, tCA, ca_in), (CB, tCB, cb_in),
                             (dcol, tdcol, dcol_in), (Jp, tJp, jp_in), (cmask, tcmask, cmask_in)):
            P.dma("sp", dst, src, t_, writes=[t_])
        P.op("act", lambda e: e.activation(out=stp, in_=stp, func=AF.Exp), reads=[tstp], writes=[tstp])
        P.op("dve", lambda e: e.tensor_tensor(out=ar, in0=lamr, in1=stp, op=ALU.mult), reads=[tlamr, tstp], writes=[tar])
        P.op("dve", lambda e: e.tensor_tensor(out=ai, in0=lami, in1=stp, op=ALU.mult), reads=[tlami, tstp], writes=[tai])
        ang, tang = small("ang", [128, 2, 32, 32])
        ang2, tang2 = small("ang2", [128, 2, 32, 32])
        mgl, tmgl = small("mgl", [128, 2, 32, 32])
        mc, tmc = small("mc", [128, 2, 32, 32])
        ms, tms = small("ms", [128, 2, 32, 32])
        mc1, tmc1 = small("mc1", [128, 2, 32, 32])
        ms2, tms2 = small("ms2", [128, 2, 32, 32])
        for r in range(2):
            aib = ai[:, r * 32:(r + 1) * 32].unsqueeze(2).to_broadcast([128, 32, 32])
            arb = ar[:, r * 32:(r + 1) * 32].unsqueeze(2).to_broadcast([128, 32, 32])
            exb = EX[:, r, :].unsqueeze(1).to_broadcast([128, 32, 32])
            P.op("dve", lambda e, r=r, aib=aib, exb=exb: e.tensor_tensor(out=ang[:, r], in0=aib, in1=exb, op=ALU.mult),
                 reads=[tai, tEX], writes=[tang])
            P.op("dve", lambda e, r=r, arb=arb, exb=exb: e.tensor_tensor(out=mgl[:, r], in0=arb, in1=exb, op=ALU.mult),
                 reads=[tar, tEX], writes=[tmgl])
        angf = ang.rearrange("p a b c -> p (a b c)")
        ang2f = ang2.rearrange("p a b c -> p (a b c)")
        mglf = mgl.rearrange("p a b c -> p (a b c)")
        mcf = mc.rearrange("p a b c -> p (a b c)")
        msf = ms.rearrange("p a b c -> p (a b c)")
        mc1f = mc1.rearrange("p a b c -> p (a b c)")
        ms2f = ms2.rearrange("p a b c -> p (a b c)")
        OFF = 64.0 * PI
        INV2PI = 1.0 / (2 * PI)
        kint, tkint = small("kint", [128, 2048], mybir.dt.int32)
        kf, tkf = small("kf", [128, 2048])
        P.op("dve", lambda e: e.tensor_scalar(out=ang2f, in0=angf, scalar1=OFF + PI / 2, scalar2=INV2PI, op0=ALU.add,
                                              op1=ALU.mult), reads=[tang], writes=[tang2])
        P.op("dve", lambda e: e.tensor_scalar(out=angf, in0=angf, scalar1=OFF, scalar2=INV2PI, op0=ALU.add,
                                              op1=ALU.mult), reads=[tang], writes=[tang])
        for af_, taf_ in ((ang2f, tang2), (angf, tang)):
            P.op("dve", lambda e, af_=af_: e.tensor_copy(out=kint, in_=af_), reads=[taf_], writes=[tkint])
            P.op("dve", lambda e: e.tensor_copy(out=kf, in_=kint), reads=[tkint], writes=[tkf])
            P.op("dve", lambda e, af_=af_: e.tensor_tensor(out=af_, in0=af_, in1=kf, op=ALU.subtract), reads=[taf_, tkf],
                 writes=[taf_])
            P.op("dve", lambda e, af_=af_: e.tensor_scalar(out=kf, in0=af_, scalar1=0.5, scalar2=None, op0=ALU.is_gt),
                 reads=[taf_], writes=[tkf])
            P.op("dve", lambda e, af_=af_: e.tensor_tensor(out=af_, in0=af_, in1=kf, op=ALU.subtract), reads=[taf_, tkf],
                 writes=[taf_])
            P.op("dve", lambda e, af_=af_: e.tensor_scalar(out=kf, in0=af_, scalar1=-0.5, scalar2=None, op0=ALU.is_lt),
                 reads=[taf_], writes=[tkf])
            P.op("dve", lambda e, af_=af_: e.tensor_tensor(out=af_, in0=af_, in1=kf, op=ALU.add), reads=[taf_, tkf],
                 writes=[taf_])
        P.op("act", lambda e: e.activation(out=ang2f, in_=ang2f, func=AF.Sin, scale=2 * PI), reads=[tang2], writes=[tang2])
        P.op("act", lambda e: e.activation(out=angf, in_=angf, func=AF.Sin, scale=2 * PI), reads=[tang], writes=[tang])
        P.op("act", lambda e: e.activation(out=mglf, in_=mglf, func=AF.Exp), reads=[tmgl], writes=[tmgl])
        P.op("dve", lambda e: e.tensor_tensor(out=mcf, in0=mglf, in1=ang2f, op=ALU.mult), reads=[tmgl, tang2], writes=[tmc])
        P.op("dve", lambda e: e.tensor_tensor(out=msf, in0=mglf, in1=angf, op=ALU.mult), reads=[tmgl, tang], writes=[tms])
        P.op("dve", lambda e: e.tensor_scalar(out=mc1f, in0=mcf, scalar1=sg[:, 0:1], scalar2=None, op0=ALU.mult),
             reads=[tmc, tsg], writes=[tmc1])
        P.op("dve", lambda e: e.tensor_scalar(out=ms2f, in0=msf, scalar1=sg[:, 1:2], scalar2=None, op0=ALU.mult),
             reads=[tms, tsg], writes=[tms2])
        def pw(tab, j):
            return tab[:, :, :, 24 + j]
        l1r, tl1r = small("l1r", [128, 2, 32])
        nr_, tnr = small("nr_", [128, 2, 32])
        den, tden = small("den", [128, 2, 32])
        tmpa, ttmpa = small("tmpa", [128, 2, 32])
        tmpb, ttmpb = small("tmpb", [128, 2, 32])
        wr, twr = small("wr", [128, 2, 32])
        wi, twi = small("wi", [128, 2, 32])
        wi1, twi1 = small("wi1", [128, 2, 32])
        wi2, twi2 = small("wi2", [128, 2, 32])
        lr3 = lamr.rearrange("p (r g) -> p r g", r=2)
        li3 = lami.rearrange("p (r g) -> p r g", r=2)
        P.op("dve", lambda e: e.tensor_scalar(out=nr_, in0=pw(mc, 0), scalar1=-1.0, scalar2=None, op0=ALU.add),
             reads=[tmc], writes=[tnr])
        P.op("dve", lambda e: e.tensor_tensor(out=den, in0=lr3, in1=lr3, op=ALU.mult), reads=[tlamr], writes=[tden])
        P.op("dve", lambda e: e.tensor_tensor(out=tmpa, in0=li3, in1=li3, op=ALU.mult), reads=[tlami], writes=[ttmpa])
        P.op("dve", lambda e: e.tensor_tensor(out=den, in0=den, in1=tmpa, op=ALU.add), reads=[tden, ttmpa], writes=[tden])
        P.op("dve", lambda e: e.tensor_tensor(out=tmpa, in0=nr_, in1=lr3, op=ALU.mult), reads=[tnr, tlamr], writes=[ttmpa])
        P.op("dve", lambda e: e.tensor_tensor(out=tmpb, in0=pw(ms, 0), in1=li3, op=ALU.mult), reads=[tms, tlami],
             writes=[ttmpb])
        P.op("dve", lambda e: e.tensor_tensor(out=tmpa, in0=tmpa, in1=tmpb, op=ALU.add), reads=[ttmpa, ttmpb],
             writes=[ttmpa])
        P.op("dve", lambda e: e.reciprocal(out=den, in_=den), reads=[tden], writes=[tden])
        P.op("dve", lambda e: e.tensor_tensor(out=wr, in0=tmpa, in1=den, op=ALU.mult), reads=[ttmpa, tden], writes=[twr])
        P.op("dve", lambda e: e.tensor_tensor(out=tmpa, in0=pw(ms, 0), in1=lr3, op=ALU.mult), reads=[tms, tlamr],
             writes=[ttmpa])
        P.op("dve", lambda e: e.tensor_tensor(out=tmpb, in0=nr_, in1=li3, op=ALU.mult), reads=[tnr, tlami], writes=[ttmpb])
        P.op("dve", lambda e: e.tensor_tensor(out=tmpa, in0=tmpa, in1=tmpb, op=ALU.subtract), reads=[ttmpa, ttmpb],
             writes=[ttmpa])
        P.op("dve", lambda e: e.tensor_tensor(out=wi, in0=tmpa, in1=den, op=ALU.mult), reads=[ttmpa, tden], writes=[twi])
        P.op("dve", lambda e: e.tensor_scalar(out=wi1, in0=wi, scalar1=sg[:, 0:1], scalar2=None, op0=ALU.mult),
             reads=[twi, tsg], writes=[twi1])
        P.op("dve", lambda e: e.tensor_scalar(out=wi2, in0=wi, scalar1=sg[:, 1:2], scalar2=None, op0=ALU.mult),
             reads=[twi, tsg], writes=[twi2])
        bA, tbA = small("bA", [128, 64, 16])
        bB, tbB = small("bB", [128, 64, 16])
        tmp16, ttmp16 = small("tmp16", [128, 64, 16])

        def bc16(t):
            return t.rearrange("p r g -> p (r g)").unsqueeze(2).to_broadcast([128, 64, 16])
        P.op("dve", lambda e: e.tensor_tensor(out=bA, in0=BA, in1=bc16(wr), op=ALU.mult), reads=[tBA, twr], writes=[tbA])
        P.op("dve", lambda e: e.tensor_tensor(out=tmp16, in0=BB, in1=bc16(wi2), op=ALU.mult), reads=[tBB, twi2],
             writes=[ttmp16])
        P.op("dve", lambda e: e.tensor_tensor(out=bA, in0=bA, in1=tmp16, op=ALU.add), reads=[tbA, ttmp16], writes=[tbA])
        P.op("dve", lambda e: e.tensor_tensor(out=bB, in0=BB, in1=bc16(wr), op=ALU.mult), reads=[tBB, twr], writes=[tbB])
        P.op("dve", lambda e: e.tensor_tensor(out=tmp16, in0=BA, in1=bc16(wi1), op=ALU.mult), reads=[tBA, twi1],
             writes=[ttmp16])
        P.op("dve", lambda e: e.tensor_tensor(out=bB, in0=bB, in1=tmp16, op=ALU.add), reads=[tbB, ttmp16], writes=[tbB])
        P.op("dve", lambda e: e.tensor_copy(out=a64.rearrange("p (r g) -> p r g", r=2), in_=pw(mc, 2)), reads=[tmc],
             writes=[tab64])
        P.op("dve", lambda e: e.tensor_copy(out=b64.rearrange("p (r g) -> p r g", r=2), in_=pw(ms, 2)), reads=[tms],
             writes=[tab64])
        P.op("dve", lambda e: e.tensor_scalar(out=nb64, in0=b64, scalar1=-1.0, scalar2=None, op0=ALU.mult),
             reads=[tab64], writes=[tab64])
        l8a, tl8a = small("l8a", [128, 2, 32])
        l8b, tl8b = small("l8b", [128, 2, 32])
        P.op("dve", lambda e: e.tensor_copy(out=l8a, in_=pw(mc, 1)), reads=[tmc], writes=[tl8a])
        P.op("dve", lambda e: e.tensor_copy(out=l8b, in_=pw(mc1, 1)), reads=[tmc1], writes=[tl8b])
        P.op("dve", lambda e: e.tensor_scalar(out=l8b, in0=pw(ms, 1), scalar1=sg[:, 0:1], scalar2=None, op0=ALU.mult),
             reads=[tms, tsg], writes=[tl8b])
        Pt, tPt = small("Pt", [128, 8, 8, 16])
        Qt, tQt = small("Qt", [128, 8, 8, 16])
        M4t, tM4t = small("M4t", [128, 8, 8, 16])
        tq, ttq = small("tq", [128, 8, 8, 16])
        m1tmp, tm1tmp = small("m1tmp", [128, 128])
        ddiag, tddiag = small("ddiag", [128, 128])
        l8tmp, tl8tmp = small("l8tmp", [128, 128])
        pcn = 0
        for r in range(2):
            for gb in range(4):
                gsl = slice(gb * 8, (gb + 1) * 8)
                gdsl = slice(r * 32 + gb * 8, r * 32 + (gb + 1) * 8)

                def wtab(tab, w):
                    return tab[:, r, gsl, w * 8:(w + 1) * 8].unsqueeze(3).to_broadcast([128, 8, 8, 16])

                def ctab(tab):
                    return tab[:, gdsl, :].unsqueeze(2).to_broadcast([128, 8, 8, 16])
                for (dst, tdst, w, A, tA, Bm, tB, kind) in ((Pt, tPt, 0, bA, tbA, bB, tbB, "p"),
                                                            (Qt, tQt, 1, CA, tCA, CB, tCB, "q"),
                                                            (M4t, tM4t, 2, CA, tCA, CB, tCB, "q")):
                    if kind == "p":
                        m_a, tm_a, m_b, tm_b, op2 = mc, tmc, ms2, tms2, ALU.add
                    else:
                        m_a, tm_a, m_b, tm_b, op2 = mc1, tmc1, ms, tms, ALU.subtract
                    i0a, i1a = wtab(m_a, w), ctab(A)
                    i0b, i1b = wtab(m_b, w), ctab(Bm)
                    P.op("dve", lambda e, dst=dst, i0a=i0a, i1a=i1a: e.tensor_tensor(
                        out=dst, in0=i0a, in1=i1a, op=ALU.mult), reads=[tm_a, tA], writes=[tdst])
                    P.op("dve", lambda e, i0b=i0b, i1b=i1b: e.tensor_tensor(
                        out=tq, in0=i0b, in1=i1b, op=ALU.mult), reads=[tm_b, tB], writes=[ttq])
                    P.op("dve", lambda e, dst=dst, op2=op2: e.tensor_tensor(out=dst, in0=dst, in1=tq, op=op2),
                         reads=[tdst, ttq], writes=[tdst])
                for gi in range(8):
                    g = gb * 8 + gi
                    gd = r * 32 + g
                    Pg = Pt[:, gi].rearrange("p s c -> p (s c)")
                    Qg = Qt[:, gi].rearrange("p s c -> p (s c)")
                    M4g = M4t[:, gi].rearrange("p s c -> p (s c)")
                    pa_ = psb[pcn % 6]
                    tpa_ = tps[pcn % 6]
                    pcn += 1
                    P.op("pe", lambda e, pa_=pa_, Pg=Pg, Qg=Qg: e.matmul(pa_[:, 0:128], lhsT=Pg, rhs=Qg, start=True,
                                                                        stop=True),
                         reads=[tPt, tQt], writes=[tpa_])
                    P.op("dve", lambda e, pa_=pa_, r=r: e.tensor_tensor(out=m1tmp, in0=pa_[:, 0:128], in1=cmask[:, r, :],
                                                                       op=ALU.mult),
                         reads=[tpa_, tcmask], writes=[tm1tmp])
                    if r == 0:
                        P.op("dve", lambda e, g=g, gd=gd: e.scalar_tensor_tensor(
                            out=MATS[:, gd, 1, :], in0=ident_f, scalar=dcol[:, g:g + 1], in1=m1tmp, op0=ALU.mult,
                            op1=ALU.add), reads=[t_ident_f, tdcol, tm1tmp], writes=[tMATS[gd]])
                    else:
                        P.op("dve", lambda e, gd=gd: e.tensor_copy(out=MATS[:, gd, 1, :], in_=m1tmp),
                             reads=[tm1tmp], writes=[tMATS[gd]])
                    pb_ = psb[pcn % 6]
                    tpb_ = tps[pcn % 6]
                    pcn += 1
                    P.op("pe", lambda e, pb_=pb_, Pg=Pg: e.transpose(out=pb_[:, 0:128], in_=Pg, identity=ident_f),
                         reads=[tPt, t_ident_f], writes=[tpb_])
                    P.op("act", lambda e, pb_=pb_, gd=gd: e.activation(out=MATS[:, gd, 0, :], in_=pb_[:, 0:128],
                                                                      func=AF.Copy),
                         reads=[tpb_], writes=[tMATS[gd]])
                    P.op("act", lambda e, M4g=M4g, gd=gd: e.activation(out=MATS[:, gd, 2, :], in_=M4g, func=AF.Copy),
                         reads=[tM4t], writes=[tMATS[gd]])
                    P.op("dve", lambda e, r=r, g=g: e.tensor_scalar(out=l8tmp, in0=Jp, scalar1=l8b[:, r, g:g + 1],
                                                                   scalar2=None, op0=ALU.mult),
                         reads=[tJp, tl8b], writes=[tl8tmp])
                    P.op("dve", lambda e, r=r, g=g, gd=gd: e.scalar_tensor_tensor(
                        out=MATS[:, gd, 3, :], in0=ident_f, scalar=l8a[:, r, g:g + 1], in1=l8tmp, op0=ALU.mult,
                        op1=ALU.add), reads=[t_ident_f, tl8a, tl8tmp], writes=[tMATS[gd]])
        dump("mc", mc, tmc, F32)
        dump("ms", ms, tms, F32)
        dump("bA", bA, tbA, F32)
        dump("wr", wr, twr, F32)
        dump("wi", wi, twi, F32)
        dump("Pt", Pt, tPt, F32)
        dump("Qt", Qt, tQt, F32)
        tall = Tk("matsall")
        P.barrier()
        for i8 in range(8):
            dump("MATS%d" % i8, MATS[:, i8 * 8:(i8 + 1) * 8], tall, BF16)
        P.barrier()

    with ExitStack() as st:
        Ug = sbs(st, "Ug", [128, 32, 1024], BF16)
        tUg = [Tk("Ug%d" % i) for i in range(32)]
        for g in range(32):
            for t8 in range(8):
                P.dma("sp", Ug[t8 * 16:(t8 + 1) * 16, g, :], Ud[g // 8, t8, (g % 8) * 16:(g % 8 + 1) * 16, :], tUg[g],
                      reads=[tUd], writes=[tUg[g]])
        SS = sbs(st, "SS", [128, 2, 32, 128], F32)
        tSS = Tk("SS")
        G0 = sbs(st, "G0", [128, 2, 32, 128], BF16)
        tG0 = [Tk("G0_0"), Tk("G0_1")]
        hq = [sbs(st, "hq%d" % i, [128, 4, 128], BF16) for i in range(2)]
        thq = [Tk("hq%d" % i) for i in range(2)]
        t1b = sbs(st, "t1b", [128, 2, 32], F32)
        t2b = sbs(st, "t2b", [128, 2, 32], F32)
        tt1b, tt2b = Tk("t1b"), Tk("t2b")
        ystg = [sbs(st, "ystg%d" % i, [128, 1024], F32) for i in range(2)]
        tystg = [Tk("ystg%d" % i) for i in range(2)]
        yx = sbs(st, "yx", [128, 1024], F32)
        tyx = Tk("yx")
        yo = [sbs(st, "yo%d" % i, [128, 1024], BF16) for i in range(2)]
        tyo = [Tk("yo%d" % i) for i in range(2)]
        P.op("dve", lambda e: e.memset(G0, 0.0), writes=tG0)
        hb = 0
        for r in range(2):
            for quad in range(8):
                for n in range(8):
                    i = n if r == 0 else 7 - n
                    bank, tbank = psb[hb % 2], tps[hb % 2]
                    for j in range(4):
                        g = quad * 4 + j
                        gd = r * 32 + g
                        P.op("pe", lambda e, bank=bank, j=j, gd=gd, g=g, i=i, n=n: e.matmul(
                            bank[:, j * 128:(j + 1) * 128], lhsT=MATS[:, gd, 0, :], rhs=Ug[:, g, i:1024:8],
                            start=(j == 0), stop=(n == 0), skip_group_check=True), reads=[tMATS[gd], tUg[g]],
                            writes=[tbank])
                        if n > 0:
                            P.op("pe", lambda e, bank=bank, j=j, gd=gd, n=n: e.matmul(
                                bank[:, j * 128:(j + 1) * 128], lhsT=MATS[:, gd, 3, :], rhs=hq[(n - 1) % 2][:, j, :],
                                start=False, stop=True, skip_group_check=True), reads=[tMATS[gd], thq[(n - 1) % 2]],
                                writes=[tbank])
                    if n < 7:
                        P.op("act", lambda e, bank=bank, n=n: e.activation(
                            out=hq[n % 2].rearrange("p a b -> p (a b)"), in_=bank, func=AF.Copy),
                            reads=[tbank], writes=[thq[n % 2]])
                    else:
                        P.op("act", lambda e, bank=bank, quad=quad: e.activation(
                            out=SS[:, 0, quad * 4:(quad + 1) * 4, :].rearrange("p a b -> p (a b)"), in_=bank,
                            func=AF.Copy), reads=[tbank], writes=[tSS])
                    hb += 1
            for blk in range(8):
                bank, tbank = psb[2 + blk % 2], tps[2 + blk % 2]
                P.op("pe", lambda e, bank=bank, blk=blk: e.matmul(
                    bank, lhsT=Jm_f, rhs=SS[:, 0, blk * 4:(blk + 1) * 4, :].rearrange("p a b -> p (a b)"),
                    start=True, stop=True), reads=[tJm, tSS], writes=[tbank])
                P.op("dve", lambda e, bank=bank, blk=blk: e.tensor_copy(
                    out=SS[:, 1, blk * 4:(blk + 1) * 4, :].rearrange("p a b -> p (a b)"), in_=bank),
                    reads=[tbank], writes=[tSS])
            ks = list(range(128)) if r == 0 else list(range(127, -1, -1))
            arow = a64[:, r * 32:(r + 1) * 32]
            brow = b64[:, r * 32:(r + 1) * 32]
            nbrow = nb64[:, r * 32:(r + 1) * 32]
            a2 = arow.unsqueeze(1).to_broadcast([128, 2, 32])
            for kk in range(1, 128):
                kp, k = ks[kk - 1], ks[kk]
                P.op("dve", lambda e, kp=kp, a2=a2: e.tensor_tensor(out=t1b, in0=SS[:, :, :, kp], in1=a2, op=ALU.mult),
                     reads=[tSS, tab64], writes=[tt1b])
                P.op("dve", lambda e, kp=kp, brow=brow: e.tensor_tensor(out=t2b[:, 0, :], in0=SS[:, 1, :, kp], in1=brow,
                                                                       op=ALU.mult),
                     reads=[tSS, tab64], writes=[tt2b])
                P.op("dve", lambda e, kp=kp, nbrow=nbrow: e.tensor_tensor(out=t2b[:, 1, :], in0=SS[:, 0, :, kp],
                                                                         in1=nbrow, op=ALU.mult),
                     reads=[tSS, tab64], writes=[tt2b])
                P.op("dve", lambda e: e.tensor_tensor(out=t1b, in0=t1b, in1=t2b, op=ALU.add), reads=[tt1b, tt2b],
                     writes=[tt1b])
                P.op("dve", lambda e, k=k: e.tensor_tensor(out=SS[:, :, :, k], in0=SS[:, :, :, k], in1=t1b, op=ALU.add),
                     reads=[tSS, tt1b], writes=[tSS])
            dump("SS%d_0" % r, SS[:, 0], tSS, F32)
            dump("SS%d_1" % r, SS[:, 1], tSS, F32)
            if r == 0:
                P.op("dve", lambda e: e.tensor_copy(out=G0[:, 0, :, 1:128], in_=SS[:, 0, :, 0:127]), reads=[tSS],
                     writes=[tG0[0]])
            else:
                P.op("dve", lambda e: e.tensor_copy(out=G0[:, 1, :, 0:127], in_=SS[:, 0, :, 1:128]), reads=[tSS],
                     writes=[tG0[1]])
        hq2 = [hq[0].rearrange("p (a b) c -> p a b c", a=2), hq[1].rearrange("p (a b) c -> p a b c", a=2)]
        for g in range(32):
            Yb = [psb[2 + 2 * (g % 2)], psb[3 + 2 * (g % 2)]]
            tYb = [tps[2 + 2 * (g % 2)], tps[3 + 2 * (g % 2)]]
            cnt_i = [0] * 8
            ybank_started = [False, False]
            for n in range(8):
                bank, tbank = psb[hb % 2], tps[hb % 2]
                hb += 1
                for r in range(2):
                    i = n if r == 0 else 7 - n
                    gd = r * 32 + g
                    if n == 0:
                        prev, tprev = G0[:, r, g, :], tG0[r]
                    else:
                        prev, tprev = hq2[(n - 1) % 2][:, 0, r, :], thq[(n - 1) % 2]
                    yb, tyb = Yb[i // 4], tYb[i // 4]
                    yc = slice((i % 4) * 128, (i % 4 + 1) * 128)
                    u_i = Ug[:, g, i:1024:8]
                    st_ = not ybank_started[i // 4]
                    ybank_started[i // 4] = True
                    P.op("pe", lambda e, yb=yb, yc=yc, gd=gd, u_i=u_i, st_=st_: e.matmul(
                        yb[:, yc], lhsT=MATS[:, gd, 1, :], rhs=u_i, start=st_, stop=False, skip_group_check=True),
                        reads=[tMATS[gd], tUg[g]], writes=[tyb])
                    P.op("pe", lambda e, yb=yb, yc=yc, gd=gd, prev=prev, sp_=(cnt_i[i] == 1): e.matmul(
                        yb[:, yc], lhsT=MATS[:, gd, 2, :], rhs=prev, start=False, stop=sp_, skip_group_check=True),
                        reads=[tMATS[gd], tprev], writes=[tyb])
                    cnt_i[i] += 1
                    if n < 7:
                        P.op("pe", lambda e, bank=bank, r=r, gd=gd, u_i=u_i: e.matmul(
                            bank[:, r * 128:(r + 1) * 128], lhsT=MATS[:, gd, 0, :], rhs=u_i, start=(r == 0), stop=False,
                            skip_group_check=True), reads=[tMATS[gd], tUg[g]], writes=[tbank])
                        P.op("pe", lambda e, bank=bank, r=r, gd=gd, prev=prev: e.matmul(
                            bank[:, r * 128:(r + 1) * 128], lhsT=MATS[:, gd, 3, :], rhs=prev, start=False, stop=True,
                            skip_group_check=True), reads=[tMATS[gd], tprev], writes=[tbank])
                if n < 7:
                    P.op("act", lambda e, bank=bank, n=n: e.activation(
                        out=hq[n % 2][:, 0:2, :].rearrange("p a b -> p (a b)"), in_=bank[:, 0:256], func=AF.Copy),
                        reads=[tbank], writes=[thq[n % 2]])
            ys = ystg[g % 2]
            for b2 in range(2):
                P.op("act", lambda e, ys=ys, b2=b2, Yb=Yb: e.activation(
                    out=ys.rearrange("p (k i) -> p i k", i=8)[:, 4 * b2:4 * b2 + 4, :],
                    in_=Yb[b2].rearrange("p (i k) -> p i k", i=4), func=AF.Copy),
                    reads=[tYb[b2]], writes=[tystg[g % 2]])
            P.op("dve", lambda e, ys=ys: e.tensor_tensor(out=yx, in0=ys, in1=ys, op=ALU.mult), reads=[tystg[g % 2]],
                 writes=[tyx])
            P.op("dve", lambda e: e.tensor_scalar(out=yx, in0=yx, scalar1=0.044715, scalar2=1.0, op0=ALU.mult,
                                                  op1=ALU.add), reads=[tyx], writes=[tyx])
            P.op("dve", lambda e, ys=ys: e.tensor_tensor(out=yx, in0=yx, in1=ys, op=ALU.mult), reads=[tyx, tystg[g % 2]],
                 writes=[tyx])
            P.op("act", lambda e: e.activation(out=yx, in_=yx, func=AF.Sigmoid, scale=1.5957691216057308),
                 reads=[tyx], writes=[tyx])
            P.op("dve", lambda e, ys=ys, g=g: e.tensor_tensor(out=yo[g % 2], in0=yx, in1=ys, op=ALU.mult),
                 reads=[tyx, tystg[g % 2]], writes=[tyo[g % 2]])
            P.dma("act", Yd[g], yo[g % 2], tyo[g % 2], reads=[tyo[g % 2]], writes=[tYd])
        P.barrier()
    stM.close()
    if debug and "stop4" in debug:
        if "only4" in debug:
            P.emit([tYd] + dbg_tiles)
        else:
            P.emit([tYd, tATT, tQT, tKT, tVd, tZA, tUd, tZB, tGA, tGB] + dbg_tiles)
        return nc, dbg_outs

    with ExitStack() as st:
        wstf = [sbs(st, "wstf%d" % i, [128, 1024], F32) for i in range(2)]
        twstf = [Tk("wstf%d" % i) for i in range(2)]
        wa = sbs(st, "wa", [128, 4, 1024], BF16)
        wb = sbs(st, "wb", [128, 4, 1024], BF16)
        wgl = sbs(st, "wgl", [128, 4, 512], BF16)
        wo = sbs(st, "wo", [128, 8, 1024], BF16)
        twa, twb, twgl, two = Tk("wa"), Tk("wb"), Tk("wgl"), Tk("wo")
        wc = 0
        for (dst, tdst, src, nk, ncol) in ((wa, twa, wa_in, 4, 1024), (wb, twb, wb_in, 4, 1024),
                                            (wgl, twgl, wglu_in, 4, 512), (wo, two, wout_in, 8, 1024)):
            for k in range(nk):
                b = wc % 2
                wc += 1
                P.dma("sp", wstf[b][:, 0:ncol], src[k * 128:(k + 1) * 128, :], twstf[b], writes=[twstf[b]])
                P.op("act", lambda e, dst=dst, k=k, b=b, ncol=ncol: e.activation(out=dst[:, k, :], in_=wstf[b][:, 0:ncol],
                                                                              func=AF.Copy),
                     reads=[twstf[b]], writes=[tdst])
        fgb = sbs(st, "fgb", [128, 1024], F32)
        tfgb = Tk("fgb")
        P.dma("sp", fgb, fg_in.partition_broadcast(128), tfgb, writes=[tfgb])
        sYall = sbs(st, "sYall", [128, 4, 8, 1024], BF16)
        tsY = Tk("sYall")
        for g in range(32):
            P.dma("sp", sYall[(g % 8) * 16:(g % 8 + 1) * 16, g // 8, :, :],
                  Yd[g].rearrange("(t co) c -> co t c", co=16), tsY, reads=[tYd], writes=[tsY])
        sl_ = sbs(st, "sl_", [128, 4, 512], BF16)
        tsl_ = Tk("sl_")
        att = [sbs(st, "att%d" % i, [128, 4, 512], BF16) for i in range(2)]
        tatt = [Tk("att%d" % i) for i in range(2)]
        zb = [sbs(st, "zb%d" % i, [128, 4, 512], BF16) for i in range(2)]
        tzb = [Tk("zb%d" % i) for i in range(2)]
        gab = [sbs(st, "gab%d" % i, [128, 8, 512], BF16) for i in range(2)]
        tgab = [Tk("gab%d" % i) for i in range(2)]
        gbb = [sbs(st, "gbb%d" % i, [128, 8, 512], BF16) for i in range(2)]
        tgbb = [Tk("gbb%d" % i) for i in range(2)]
        gl = sbs(st, "gl", [128, 4, 512], BF16)
        tgl = Tk("gl")
        s2 = sbs(st, "s2", [128, 4, 512], BF16)
        ts2 = Tk("s2")
        ma = sbs(st, "ma", [128, 512], F32)
        mb = sbs(st, "mb", [128, 512], F32)
        tma, tmb = Tk("ma"), Tk("mb")
        mg = sbs(st, "mg", [128, 8, 512], BF16)
        tmg = Tk("mg")
        xr = [sbs(st, "xr%d" % i, [128, 1024], F32) for i in range(2)]
        txr = [Tk("xr%d" % i) for i in range(2)]
        yt = [sbs(st, "yt%d" % i, [128, 1024], F32) for i in range(2)]
        tyt = [Tk("yt%d" % i) for i in range(2)]
        junk5 = sbs(st, "junk5", [128, 1024], BF16)
        tjunk5 = Tk("junk5")
        ss5 = [sbs(st, "ss5_%d" % i, [128, 1], F32) for i in range(2)]
        tss5 = [Tk("ss5_%d" % i) for i in range(2)]
        ty_ = Tk("y")
        pcn = 0
        xc = 0
        for tb in range(NB):
            bb = tb % 2
            tsl = slice(tb * 512, (tb + 1) * 512)
            P.dma("sp", att[bb], ATT[:, tsl].rearrange("(k p) t -> p k t", p=128), tatt[bb], reads=[tATT],
                  writes=[tatt[bb]])
            P.dma("sp", zb[bb], ZB[:, tsl].rearrange("(k p) t -> p k t", p=128), tzb[bb], reads=[tZB], writes=[tzb[bb]])
            P.dma("sp", gab[bb], GA[:, tsl].rearrange("(k p) t -> p k t", p=128), tgab[bb], reads=[tGA],
                  writes=[tgab[bb]])
            P.dma("sp", gbb[bb], GB[:, tsl].rearrange("(k p) t -> p k t", p=128), tgbb[bb], reads=[tGB],
                  writes=[tgbb[bb]])
            for kc in range(4):
                P.op("dve", lambda e, kc=kc, tb=tb: e.tensor_copy(
                    out=sl_[:, kc, :].rearrange("p (c t) -> p c t", t=8),
                    in_=sYall[:, kc, :, tb * 64:(tb + 1) * 64].rearrange("p t c -> p c t")),
                    reads=[tsY], writes=[tsl_])
            for ct in range(4):
                pi = pcn % 6
                pcn += 1
                for kc in range(4):
                    P.op("pe", lambda e, pi=pi, kc=kc, ct=ct: e.matmul(
                        psb[pi], lhsT=wgl[:, kc, ct * 128:(ct + 1) * 128], rhs=sl_[:, kc, :], start=(kc == 0),
                        stop=(kc == 3)), reads=[twgl, tsl_], writes=[tps[pi]])
                P.op("act", lambda e, pi=pi, ct=ct: e.activation(out=gl[:, ct, :], in_=psb[pi], func=AF.Sigmoid),
                     reads=[tps[pi]], writes=[tgl])
            P.op("dve", lambda e: e.tensor_tensor(out=s2, in0=sl_, in1=gl, op=ALU.mult), reads=[tsl_, tgl], writes=[ts2])
            P.op("dve", lambda e, bb=bb: e.tensor_tensor(out=s2, in0=s2, in1=zb[bb], op=ALU.mult), reads=[ts2, tzb[bb]],
                 writes=[ts2])
            for ct in range(8):
                pa_i, pb_i = pcn % 6, (pcn + 1) % 6
                pcn += 2
                for kc in range(4):
                    P.op("pe", lambda e, pa_i=pa_i, kc=kc, ct=ct, bb=bb: e.matmul(
                        psb[pa_i], lhsT=wa[:, kc, ct * 128:(ct + 1) * 128], rhs=att[bb][:, kc, :], start=(kc == 0),
                        stop=(kc == 3)), reads=[twa, tatt[bb]], writes=[tps[pa_i]])
                for kc in range(4):
                    P.op("pe", lambda e, pb_i=pb_i, kc=kc, ct=ct: e.matmul(
                        psb[pb_i], lhsT=wb[:, kc, ct * 128:(ct + 1) * 128], rhs=s2[:, kc, :], start=(kc == 0),
                        stop=(kc == 3)), reads=[twb, ts2], writes=[tps[pb_i]])
                P.op("dve", lambda e, pa_i=pa_i, ct=ct, bb=bb: e.tensor_tensor(out=ma, in0=psb[pa_i], in1=gab[bb][:, ct, :],
                                                                              op=ALU.mult),
                     reads=[tps[pa_i], tgab[bb]], writes=[tma])
                P.op("dve", lambda e, pb_i=pb_i, ct=ct, bb=bb: e.tensor_tensor(out=mb, in0=psb[pb_i], in1=gbb[bb][:, ct, :],
                                                                              op=ALU.mult),
                     reads=[tps[pb_i], tgbb[bb]], writes=[tmb])
                P.op("dve", lambda e, ct=ct: e.tensor_tensor(out=mg[:, ct, :], in0=ma, in1=mb, op=ALU.add),
                     reads=[tma, tmb], writes=[tmg])
            for t4 in range(4):
                tt = tb * 4 + t4
                xb = xc % 2
                xc += 1
                P.dma("sp", xr[xb], x[tt * 128:(tt + 1) * 128, :], txr[xb], writes=[txr[xb]])
                for hf in range(2):
                    pi = pcn % 6
                    pcn += 1
                    for kc in range(8):
                        P.op("pe", lambda e, pi=pi, kc=kc, t4=t4, hf=hf: e.matmul(
                            psb[pi], lhsT=mg[:, kc, t4 * 128:(t4 + 1) * 128], rhs=wo[:, kc, hf * 512:(hf + 1) * 512],
                            start=(kc == 0), stop=(kc == 7)), reads=[tmg, two], writes=[tps[pi]])
                    P.op("dve", lambda e, pi=pi, hf=hf, xb=xb: e.tensor_tensor(
                        out=yt[xb][:, hf * 512:(hf + 1) * 512], in0=psb[pi], in1=xr[xb][:, hf * 512:(hf + 1) * 512],
                        op=ALU.add), reads=[tps[pi], txr[xb]], writes=[tyt[xb]])
                P.op("act", lambda e, xb=xb: e.activation(out=junk5, in_=yt[xb], func=AF.Square, accum_out=ss5[xb]),
                     reads=[tyt[xb]], writes=[tjunk5, tss5[xb]])
                P.op("dve", lambda e, xb=xb: e.tensor_scalar(out=ss5[xb], in0=ss5[xb], scalar1=1.0 / D, scalar2=EPS,
                                                             op0=ALU.mult, op1=ALU.add), reads=[tss5[xb]],
                     writes=[tss5[xb]])
                P.op("act", lambda e, xb=xb: e.activation(out=ss5[xb], in_=ss5[xb], func=AF.Ln),
                     reads=[tss5[xb]], writes=[tss5[xb]])
                P.op("act", lambda e, xb=xb: e.activation(out=ss5[xb], in_=ss5[xb], func=AF.Exp, scale=-0.5),
                     reads=[tss5[xb]], writes=[tss5[xb]])
                P.op("act", lambda e, xb=xb: e.activation(out=yt[xb], in_=yt[xb], func=AF.Copy, scale=ss5[xb]),
                     reads=[tyt[xb], tss5[xb]], writes=[tyt[xb]])
                P.op("dve", lambda e, xb=xb: e.tensor_tensor(out=xr[xb], in0=yt[xb], in1=fgb, op=ALU.mult),
                     reads=[tyt[xb], tfgb], writes=[txr[xb]])
                P.dma("act", y[tt * 128:(tt + 1) * 128, :], xr[xb], txr[xb], reads=[txr[xb]], writes=[ty_])
        P.barrier()
        P.emit([ty_])
        return nc, dbg_outs


def rope_tables():
    half = 32
    inv = (np.float32(10000.0) ** (-np.arange(half, dtype=np.float32) / half)).astype(np.float32)
    pos = np.arange(L, dtype=np.float32)
    ang = (pos[None, :] * inv[:, None]).astype(np.float32)
    c = np.cos(ang.astype(np.float64)).astype(np.float32)
    s = np.sin(ang.astype(np.float64)).astype(np.float32)
    return np.tile(c, (4, 1)), np.tile(s, (4, 1))


def band_mask():
    p = np.arange(128)[:, None]
    q = np.arange(256)[None, :]
    return (((q - p) >= 0) & ((q - p) <= 128)).astype(np.float32)


def ssm_layouts(inputs):
    f = np.float32
    def q2(a):
        t = a.reshape(64, 64).T
        return np.ascontiguousarray(np.concatenate([t, t], 0).astype(f))
    lamr = q2(inputs["lam_re"][0]); lami = q2(inputs["lam_im"][0])
    lstep = np.ascontiguousarray(np.broadcast_to(inputs["log_step"][0].reshape(1, 64), (128, 64)).astype(f))
    br = inputs["b_re"][0].reshape(64, 64, 16).transpose(1, 0, 2)
    bi = inputs["b_im"][0].reshape(64, 64, 16).transpose(1, 0, 2)
    cr = inputs["c_re"][0].reshape(64, 16, 64).transpose(2, 0, 1)
    ci = inputs["c_im"][0].reshape(64, 16, 64).transpose(2, 0, 1)
    cat = lambda a, b: np.ascontiguousarray(np.concatenate([a, b], 0).astype(f))
    ex = np.zeros((128, 2, 4, 8), f)
    t = np.arange(8, dtype=f)
    for r in range(2):
        tp = t if r == 0 else 7 - t
        ex[:, r, 0, :] = 7 - tp
        ex[:, r, 1, :] = -(7 - tp)
        ex[:, r, 2, :] = tp + 1
        ex[:, r, 3, 0:3] = (1, 8, 64)
    sg = np.ones((128, 2), f); sg[64:, 0] = -1; sg[:64, 1] = -1
    jp = np.zeros((128, 128), f); jm = np.zeros((128, 128), f)
    for n in range(64):
        jp[n, n + 64] = 1; jp[n + 64, n] = 1
        jm[n + 64, n] = -1; jm[n, n + 64] = 1
    d = inputs["d_skip"][0].reshape(32, 16)
    dcol = np.ascontiguousarray(np.tile(d.T, (8, 1)).astype(f))
    cm = np.zeros((128, 2, 128), f)
    for s_ in range(8):
        for t_ in range(8):
            if t_ >= s_:
                cm[s_ * 16:(s_ + 1) * 16, 0, t_ * 16:(t_ + 1) * 16] = 1
            if t_ <= s_:
                cm[s_ * 16:(s_ + 1) * 16, 1, t_ * 16:(t_ + 1) * 16] = 1
    return {"lamr_q": lamr, "lami_q": lami, "lstep_q": lstep, "ex_tab": ex.reshape(128, 2, 32), "sg_tab": sg,
            "ba_q": cat(br, bi), "bb_q": cat(bi, br), "ca_q": cat(cr, ci), "cb_q": cat(ci, cr),
            "dcol": dcol, "cmask": cm, "jm": jm, "jp": jp}


def make_in_maps(inputs):
    xs = [inputs["x_prompt"][i] for i in range(4)] + [inputs["x_sample"][i] for i in range(2)]
    xs = xs + [xs[0], xs[1]]
    cosT, sinT = rope_tables()
    common = {
        "w_in": np.ascontiguousarray(inputs["w_in"][0]),
        "norm_g": np.ascontiguousarray(inputs["norm_g"][0].reshape(8, 128).T),
        "ident": np.eye(128, dtype=np.float32),
        "cosT": cosT, "sinT": sinT,
        "mask01": band_mask(),
        **ssm_layouts(inputs),
        "w_a": np.ascontiguousarray(inputs["w_branch_a"][0]), "w_b": np.ascontiguousarray(inputs["w_branch_b"][0]),
        "w_glu": np.ascontiguousarray(inputs["w_glu"][0]), "w_out": np.ascontiguousarray(inputs["w_out"][0]),
        "final_g": np.ascontiguousarray(inputs["final_g"]),
    }
    return [dict(common, x=np.ascontiguousarray(xs[c])) for c in range(NCORES)]


def kernel(**inputs):
    nc, _ = build()
    in_maps = make_in_maps(inputs)
    res = run_bass_kernel_spmd(nc, in_maps, core_ids=list(range(NCORES)))
    ys = [np.asarray(r["y"]).reshape(L, D) for r in res.results]
    return (np.stack(ys[0:4]).astype(np.float32), np.stack(ys[4:6]).astype(np.float32))
```

```python
import numpy as np
import ml_dtypes
import concourse.bass as bass
import concourse.mybir as mybir
from concourse.bass_utils import run_bass_kernel_spmd

F32 = mybir.dt.float32
BF16 = mybir.dt.bfloat16
ALU = mybir.AluOpType
AF = mybir.ActivationFunctionType

L = 8192
D = 1024
NCORES = 8
EPS = 1e-6
EPOCH = 30000


class Tk:
    __slots__ = ("w", "rs", "dsem", "dcnt", "name")

    def __init__(self, name=""):
        self.w = {}
        self.rs = {}
        self.dsem = None
        self.dcnt = 0
        self.name = name


class Prog:
    ENG = ("pe", "act", "dve", "pool", "sp")

    def __init__(self, nc):
        self.nc = nc
        self.ops = {e: [] for e in self.ENG}
        self.cnt = {e: 0 for e in self.ENG}
        self.esems = {e: [] for e in self.ENG}
        self.waited = {e: {} for e in self.ENG}
        self.nsem = 0
        self.dtiles = []

    _uid = [0]

    def newsem(self, name):
        self.nsem += 1
        Prog._uid[0] += 1
        return self.nc.alloc_semaphore("%s_%d" % (name, Prog._uid[0]))

    def _esem(self, e, idx):
        lst = self.esems[e]
        while len(lst) <= idx:
            lst.append(self.newsem("es_%s_%d" % (e, len(lst))))
        return lst[idx]

    def _collect(self, e, reads, writes):
        need = {}
        for t in reads:
            for s, v in t.w.items():
                if need.get(s, 0) < v:
                    need[s] = v
        for t in writes:
            for s, v in t.w.items():
                if need.get(s, 0) < v:
                    need[s] = v
            for s, v in t.rs.items():
                if need.get(s, 0) < v:
                    need[s] = v
        waits = []
        wd = self.waited[e]
        own = set(id(s) for s in self.esems[e]) if e == "pe" else ()
        for s, v in need.items():
            if id(s) in own:
                continue
            if wd.get(s, 0) >= v:
                continue
            wd[s] = v
            waits.append((s, v))
        return waits

    def op(self, e, fn, reads=(), writes=()):
        waits = self._collect(e, reads, writes)
        seq = self.cnt[e]
        self.cnt[e] += 1
        sem = self._esem(e, seq // EPOCH)
        val = seq % EPOCH + 1
        self.ops[e].append((waits, fn, sem, 1))
        for t in writes:
            t.w = {sem: val}
            t.rs = {}
        for t in reads:
            if t.rs.get(sem, 0) < val:
                t.rs[sem] = val

    def dma(self, e, out, in_, sb, reads=(), writes=()):
        waits = self._collect(e, reads, writes)
        if sb.dsem is None:
            sb.dsem = self.newsem("ds_" + sb.name)
            self.dtiles.append(sb)
        sb.dcnt += 16
        sem, val = sb.dsem, sb.dcnt
        self.ops[e].append((waits, lambda eng: eng.dma_start(out=out, in_=in_), sem, 16))
        for t in writes:
            if t is sb:
                t.w = {sem: val}
                t.rs = {}
            else:
                t.w[sem] = val
        for t in reads:
            if t.rs.get(sem, 0) < val:
                t.rs[sem] = val

    def barrier(self):
        ev = {}
        for e in self.ENG:
            n = self.cnt[e]
            if n > 0:
                ev[self.esems[e][(n - 1) // EPOCH]] = (n - 1) % EPOCH + 1
        for t in self.dtiles:
            ev[t.dsem] = t.dcnt
        for e in self.ENG:
            wd = self.waited[e]
            own = set(id(x) for x in self.esems[e])
            waits = []
            for sm, v in ev.items():
                if id(sm) in own or wd.get(sm, 0) >= v:
                    continue
                wd[sm] = v
                waits.append((sm, v))
            if waits:
                self.ops[e].append((waits, None, None, 0))

    def emit(self, final_tiles):
        nc = self.nc
        fin = {}
        for t in final_tiles:
            for s, v in list(t.w.items()) + list(t.rs.items()):
                if fin.get(s, 0) < v:
                    fin[s] = v
        with nc.Block() as block:
            def body(e):
                def run(eng):
                    for waits, fn, sem, inc in self.ops[e]:
                        for s, v in waits:
                            eng.wait_ge(s, v)
                        if fn is not None:
                            fn(eng).then_inc(sem, inc)
                    if e == "sp":
                        for s, v in fin.items():
                            eng.wait_ge(s, v)
                return run
            block.tensor(body("pe"))
            block.scalar(body("act"))
            block.vector(body("dve"))
            block.gpsimd(body("pool"))
            block.sync(body("sp"))


def ssl(start, count, step):
    return slice(start, start + (count - 1) * step + 1, step)


def build(debug=None):
    nc = bass.Bass("TRN2", target_bir_lowering=False)
    P = Prog(nc)

    def din(name, shape, dt=F32):
        return nc.dram_tensor(name, list(shape), dt, kind="ExternalInput").ap()

    dbg_outs = {}

    def dscr(name, shape, dt=BF16):
        kind = "ExternalOutput" if (debug and name in debug) else "Internal"
        if debug and ("in_" + name) in debug:
            kind = "ExternalInput"
        t = nc.dram_tensor(name, list(shape), dt, kind=kind).ap()
        if kind == "ExternalOutput":
            dbg_outs[name] = t
        return t

    dbg_tiles = []

    def dump(name, ap, tk, dt):
        if not (debug and "dumps" in debug):
            return
        t = nc.dram_tensor("dbg_" + name, list(ap.shape), dt, kind="ExternalOutput").ap()
        dbg_outs["dbg_" + name] = t
        dt_ = Tk("dbgd_" + name)
        dbg_tiles.append(dt_)
        P.dma("sp", t, ap, tk, reads=[tk], writes=[dt_])

    x = din("x", [L, D])
    w_in = din("w_in", [D, 8192])
    norm_g = din("norm_g", [128, 8])
    ident_in = din("ident", [128, 128])
    cos_in = din("cosT", [128, L])
    sin_in = din("sinT", [128, L])
    mask_in = din("mask01", [128, 256])
    jm_in = din("jm", [128, 128])
    jp_in = din("jp", [128, 128])
    lamr_in = din("lamr_q", [128, 64])
    lami_in = din("lami_q", [128, 64])
    lstep_in = din("lstep_q", [128, 64])
    ex_in = din("ex_tab", [128, 2, 32])
    sg_in = din("sg_tab", [128, 2])
    ba_in = din("ba_q", [128, 64, 16])
    bb_in = din("bb_q", [128, 64, 16])
    ca_in = din("ca_q", [128, 64, 16])
    cb_in = din("cb_q", [128, 64, 16])
    dcol_in = din("dcol", [128, 32])
    cmask_in = din("cmask", [128, 2, 128])
    wa_in = din("w_a", [512, 1024])
    wb_in = din("w_b", [512, 1024])
    wglu_in = din("w_glu", [512, 512])
    wout_in = din("w_out", [1024, 1024])
    fg_in = din("final_g", [1024])
    y = nc.dram_tensor("y", [L, D], F32, kind="ExternalOutput").ap()

    QT = dscr("QT", [2, 24, 32, L])
    KT = dscr("KT", [2, 24, 32, L])
    Vd = dscr("Vd", [L, 1536])
    ZA = dscr("ZA", [512, L])
    Ud = dscr("Ud", [4, 8, 128, 1024])
    ZB = dscr("ZB", [512, L])
    GA = dscr("GA", [1024, L])
    GB = dscr("GB", [1024, L])
    tQT, tKT, tVd, tZA, tUd, tZB, tGA, tGB = [Tk(n) for n in
                                              ("QT", "KT", "Vd", "ZA", "Ud", "ZB", "GA", "GB")]

    def sb(name, shape, dt):
        return nc.alloc_sbuf_tensor(name, list(shape), dt).ap()

    def ps(name, shape, dt=F32):
        return nc.alloc_psum_tensor(name, list(shape), dt).ap()

    ident_f = sb("ident_f", [128, 128], F32)
    ident_b = sb("ident_b", [128, 128], BF16)
    gcol = sb("gcol", [128, 8], F32)
    t_ident_f, t_ident_b, t_gcol = Tk("identf"), Tk("identb"), Tk("gcol")
    def load_consts():
        P.dma("sp", ident_f, ident_in, t_ident_f, writes=[t_ident_f])
        P.dma("sp", gcol, norm_g, t_gcol, writes=[t_gcol])
        P.op("dve", lambda e: e.tensor_copy(out=ident_b, in_=ident_f), reads=[t_ident_f], writes=[t_ident_b])
    load_consts()

    psb = [ps("psb%d" % i, [128, 512]) for i in range(7)]
    tps = [Tk("psb%d" % i) for i in range(7)]
    pst = ps("pst", [128, 1024], BF16)
    tpst = Tk("pst")

    from contextlib import ExitStack

    def sbs(st, name, shape, dt):
        h = st.enter_context(nc.sbuf_tensor(name, list(shape), dt))
        return h.ap() if hasattr(h, "ap") else h[:]

    NTT = L // 128
    NB = L // 512
    w_in_v = w_in.rearrange("(k p) c -> p k c", p=128)
    st12 = ExitStack()
    hT = sbs(st12, "hT", [128, 8, L], BF16)
    thT = [Tk("hT%d" % i) for i in range(NB)]

    with ExitStack() as st:
        xt = [sbs(st, "xt%d" % i, [128, D], F32) for i in range(2)]
        txt = [Tk("xt%d" % i) for i in range(2)]
        xn = [sbs(st, "xn%d" % i, [128, D], BF16) for i in range(2)]
        txn = [Tk("xn%d" % i) for i in range(2)]
        junk = sbs(st, "junk", [128, D], BF16)
        tjunk = Tk("junk")
        ss = [sbs(st, "ss%d" % i, [128, 1], F32) for i in range(2)]
        tss = [Tk("ss%d" % i) for i in range(2)]
        rs_ = [sbs(st, "rs%d" % i, [128, 1], F32) for i in range(2)]
        trs = [Tk("rs%d" % i) for i in range(2)]
        for tt in range(NTT):
            b = tt % 2
            P.dma("sp", xt[b], x[tt * 128:(tt + 1) * 128, :], txt[b], writes=[txt[b]])
            P.op("act", lambda e, b=b: e.activation(out=junk, in_=xt[b], func=AF.Square, accum_out=ss[b]),
                 reads=[txt[b]], writes=[tjunk, tss[b]])
            P.op("dve", lambda e, b=b: e.tensor_scalar(out=rs_[b], in0=ss[b], scalar1=1.0 / D, scalar2=EPS,
                                                       op0=ALU.mult, op1=ALU.add),
                 reads=[tss[b]], writes=[trs[b]])
            P.op("act", lambda e, b=b: e.activation(out=rs_[b], in_=rs_[b], func=AF.Ln),
                 reads=[trs[b]], writes=[trs[b]])
            P.op("act", lambda e, b=b: e.activation(out=rs_[b], in_=rs_[b], func=AF.Exp, scale=-0.5),
                 reads=[trs[b]], writes=[trs[b]])
            P.op("act", lambda e, b=b: e.activation(out=xn[b], in_=xt[b], func=AF.Copy, scale=rs_[b]),
                 reads=[txt[b], trs[b]], writes=[txn[b]])
            for k in range(8):
                P.op("pe", lambda e, b=b, k=k: e.transpose(out=pst[:, k * 128:(k + 1) * 128],
                                                          in_=xn[b][:, k * 128:(k + 1) * 128], identity=ident_b),
                     reads=[txn[b], t_ident_b], writes=[tpst])
            P.op("dve", lambda e, tt=tt: e.tensor_copy(out=hT[:, :, tt * 128:(tt + 1) * 128],
                                                       in_=pst.rearrange("p (k t) -> p k t", k=8)),
                 reads=[tpst], writes=[thT[tt // 4]])
        P.barrier()

    with ExitStack() as st:
        wst = [sbs(st, "wst%d" % i, [128, 8, 256], F32) for i in range(1)]
        twst = [Tk("wst%d" % i) for i in range(1)]
        wcnt = [0]

        def load_w(dst_ap, tdst, c0, n, perm=False):
            for j in range(n // 256):
                b = wcnt[0] % len(wst)
                wcnt[0] += 1
                P.dma("sp", wst[b], w_in_v[:, :, c0 + j * 256:c0 + (j + 1) * 256], twst[b], writes=[twst[b]])
                for k in range(8):
                    if perm:
                        o_ap = dst_ap[:, k, :].rearrange("p (two h i) -> p h two i", two=2, i=32)[:, 4 * j:4 * j + 4]
                        i_ap = wst[b][:, k, :].rearrange("p (h two i) -> p h two i", two=2, i=32)
                    else:
                        o_ap = dst_ap[:, k, j * 256:(j + 1) * 256]
                        i_ap = wst[b][:, k, :]
                    P.op("dve", lambda e, o_ap=o_ap, i_ap=i_ap, k=k: e.tensor_scalar(
                        out=o_ap, in0=i_ap, scalar1=gcol[:, k:k + 1], scalar2=None, op0=ALU.mult),
                        reads=[twst[b], t_gcol], writes=[tdst])

        wbig = sbs(st, "wbig", [128, 8, 1536], BF16)
        twbig = Tk("wbig")
        wg = [sbs(st, "wg%d" % i, [128, 8, 512], BF16) for i in range(2)]
        twg = [Tk("wg%d" % i) for i in range(2)]
        stg = [sbs(st, "stg%d" % i, [128, 512], BF16) for i in range(3)]
        tstg = [Tk("stg%d" % i) for i in range(3)]

        def zug_gen():
            jobs = [(4608, "za", 0), (5632, "zb", 0), (5120, "u", 0), (6144, "ga", 0), (6656, "ga", 512),
                    (7168, "gb", 0), (7680, "gb", 512)]
            ZBK = (4, 5, 6)
            sc = 0
            zc = 0
            for ji, (c0, kind, roff) in enumerate(jobs):
                wb_ = ji % 2
                load_w(wg[wb_], twg[wb_], c0, 512)
                for sub in range(4):
                    for tb in range(NB):
                        pi = ZBK[zc % 3]
                        zc += 1
                        tsl = slice(tb * 512, (tb + 1) * 512)
                        for k in range(8):
                            P.op("pe", lambda e, pi=pi, k=k, wb_=wb_, sub=sub, tsl=tsl: e.matmul(
                                psb[pi], lhsT=wg[wb_][:, k, sub * 128:(sub + 1) * 128], rhs=hT[:, k, tsl],
                                start=(k == 0), stop=(k == 7)),
                                reads=[twg[wb_], thT[tb]], writes=[tps[pi]])
                        s_ = sc % 3
                        sc += 1
                        rows = slice(roff + sub * 128, roff + (sub + 1) * 128)
                        if kind == "u":
                            P.op("act", lambda e, pi=pi, s_=s_: e.activation(
                                out=stg[s_].rearrange("p (t c) -> p t c", t=8),
                                in_=psb[pi].rearrange("p (c t) -> p t c", t=8), func=AF.Copy),
                                reads=[tps[pi]], writes=[tstg[s_]])
                            P.dma("act", Ud[sub, :, :, tb * 64:(tb + 1) * 64].rearrange("t p c -> p t c"),
                                  stg[s_].rearrange("p (t c) -> p t c", t=8),
                                  tstg[s_], reads=[tstg[s_]], writes=[tUd])
                        else:
                            fn = AF.Silu if kind in ("za", "zb") else AF.Sigmoid
                            dd, td = {"za": (ZA, tZA), "zb": (ZB, tZB), "ga": (GA, tGA), "gb": (GB, tGB)}[kind]
                            P.op("act", lambda e, pi=pi, s_=s_, fn=fn: e.activation(out=stg[s_], in_=psb[pi], func=fn),
                                 reads=[tps[pi]], writes=[tstg[s_]])
                            P.dma("act", dd[rows, tsl], stg[s_], tstg[s_], reads=[tstg[s_]], writes=[td])
                        yield

        zg = zug_gen()
        st2a = ExitStack()
        cs = [sbs(st2a, "cs%d" % i, [128, 2, 512], F32) for i in range(2)]
        tcs = [Tk("cs%d" % i) for i in range(2)]
        rtmp = [sbs(st2a, "rtmp%d" % i, [128, 512], F32) for i in range(4)]
        trtmp = [Tk("rtmp%d" % i) for i in range(4)]
        ro = [sbs(st2a, "ro%d" % i, [128, 2, 512], BF16) for i in range(3)]
        tro = [Tk("ro%d" % i) for i in range(3)]
        rcnt = 0
        ccnt = 0
        for qk in range(2):
            load_w(wbig, twbig, qk * 1536, 1536, perm=True)
            dst, tdst = (QT, tQT) if qk == 0 else (KT, tKT)
            for tb in range(NB):
                cb = ccnt % 2
                ccnt += 1
                tsl = slice(tb * 512, (tb + 1) * 512)
                P.dma("sp", cs[cb][:, 0, :], cos_in[:, tsl], tcs[cb], writes=[tcs[cb]])
                P.dma("sp", cs[cb][:, 1, :], sin_in[:, tsl], tcs[cb], writes=[tcs[cb]])
                for ht in range(6):
                    ia = 2 * (ht % 2)
                    pa, pb = psb[ia], psb[ia + 1]
                    ta, tb_ = tps[ia], tps[ia + 1]
                    for half, (pp, tp) in enumerate(((pa, ta), (pb, tb_))):
                        for k in range(8):
                            P.op("pe", lambda e, pp=pp, k=k, ht=ht, half=half, tsl=tsl: e.matmul(
                                pp, lhsT=wbig[:, k, half * 768 + ht * 128:half * 768 + (ht + 1) * 128], rhs=hT[:, k, tsl],
                                start=(k == 0), stop=(k == 7)),
                                reads=[twbig, thT[tb]], writes=[tp])
                    r = rcnt % 3
                    rcnt += 1
                    C, S = cs[cb][:, 0, :], cs[cb][:, 1, :]
                    P.op("dve", lambda e, pa=pa, C=C: e.tensor_tensor(out=rtmp[0], in0=pa, in1=C, op=ALU.mult),
                         reads=[ta, tcs[cb]], writes=[trtmp[0]])
                    P.op("dve", lambda e, pb=pb, S=S: e.tensor_tensor(out=rtmp[1], in0=pb, in1=S, op=ALU.mult),
                         reads=[tb_, tcs[cb]], writes=[trtmp[1]])
                    P.op("dve", lambda e, r=r: e.tensor_tensor(out=ro[r][:, 0, :], in0=rtmp[0], in1=rtmp[1],
                                                               op=ALU.subtract),
                         reads=[trtmp[0], trtmp[1]], writes=[tro[r]])
                    P.op("dve", lambda e, pb=pb, C=C: e.tensor_tensor(out=rtmp[2], in0=pb, in1=C, op=ALU.mult),
                         reads=[tb_, tcs[cb]], writes=[trtmp[2]])
                    P.op("dve", lambda e, pa=pa, S=S: e.tensor_tensor(out=rtmp[3], in0=pa, in1=S, op=ALU.mult),
                         reads=[ta, tcs[cb]], writes=[trtmp[3]])
                    P.op("dve", lambda e, r=r: e.tensor_tensor(out=ro[r][:, 1, :], in0=rtmp[2], in1=rtmp[3],
                                                               op=ALU.add),
                         reads=[trtmp[2], trtmp[3]], writes=[tro[r]])
                    h0 = ht * 4
                    for half in range(2):
                        P.dma("sp", dst[half, h0:h0 + 4, :, tsl].rearrange("h i t -> (h i) t"),
                              ro[r][:, half, :], tro[r], reads=[tro[r]], writes=[tdst])
                    for _ in range(3):
                        next(zg, None)
        for _ in zg:
            pass
        P.barrier()
        st2a.close()
        load_w(wbig, twbig, 3072, 1536)
        vst = [sbs(st, "vst%d" % i, [128, 1536], BF16) for i in range(2)]
        tvst = [Tk("vst%d" % i) for i in range(2)]
        pc = 0
        for tt in range(NTT):
            vb = tt % 2
            for cb3 in range(3):
                pi = pc % 6
                pc += 1
                for k in range(8):
                    P.op("pe", lambda e, pi=pi, k=k, tt=tt, cb3=cb3: e.matmul(
                        psb[pi], lhsT=hT[:, k, tt * 128:(tt + 1) * 128], rhs=wbig[:, k, cb3 * 512:(cb3 + 1) * 512],
                        start=(k == 0), stop=(k == 7)),
                        reads=[twbig, thT[tt // 4]], writes=[tps[pi]])
                P.op("act", lambda e, pi=pi, vb=vb, cb3=cb3: e.activation(
                    out=vst[vb][:, cb3 * 512:(cb3 + 1) * 512], in_=psb[pi], func=AF.Copy),
                    reads=[tps[pi]], writes=[tvst[vb]])
            P.dma("act", Vd[tt * 128:(tt + 1) * 128, :], vst[vb], tvst[vb], reads=[tvst[vb]], writes=[tVd])
        P.barrier()
    st12.close()
    if debug and "stop2" in debug:
        P.emit([tQT, tKT, tVd, tZA, tUd, tZB, tGA, tGB])
        return nc, dbg_outs

    ATT = dscr("ATT", [512, L])
    tATT = Tk("ATT")
    PADK = 1024
    with ExitStack() as st:
        QTh = [sbs(st, "QTh%d" % i, [64, L], BF16) for i in range(2)]
        tQTh = [Tk("QTh%d" % i) for i in range(2)]
        KTh = [sbs(st, "KTh%d" % i, [64, L + 2 * PADK], BF16) for i in range(2)]
        tKTh = [Tk("KTh%d" % i) for i in range(2)]
        NVT = 80
        Vt = [sbs(st, "Vt%d" % i, [128, NVT, 64], BF16) for i in range(2)]
        tVt = [Tk("Vt%d" % i) for i in range(2)]
        ACC = sbs(st, "ACC", [64, 2, L], F32)
        tACC = Tk("ACC")
        pT = [sbs(st, "pT%d" % i, [128, 256], BF16) for i in range(3)]
        tpT = [Tk("pT%d" % i) for i in range(3)]
        onesk = sbs(st, "onesk", [128, 3, 64], BF16)
        tonesk = Tk("onesk")
        mask_f = sbs(st, "mask_f", [128, 256], F32)
        mask_b = sbs(st, "mask_b", [128, 256], BF16)
        tmask_f, tmask_b = Tk("maskf"), Tk("maskb")
        za_t = sbs(st, "za_t", [64, 2048], BF16)
        tza = Tk("za_t")
        dv = sbs(st, "dv", [64, 2048], F32)
        tdv = Tk("dv")
        ao = [sbs(st, "ao%d" % i, [64, 2048], BF16) for i in range(2)]
        tao = [Tk("ao%d" % i) for i in range(2)]

        P.dma("sp", mask_f, mask_in, tmask_f, writes=[tmask_f])
        P.op("dve", lambda e: e.tensor_copy(out=mask_b, in_=mask_f), reads=[tmask_f], writes=[tmask_b])
        P.op("dve", lambda e: e.memset(onesk, 1.0), writes=[tonesk])
        P.op("dve", lambda e: e.memset(onesk[0:64, 1, :], 0.0), writes=[tonesk])
        P.op("dve", lambda e: e.memset(onesk[64:128, 2, :], 0.0), writes=[tonesk])
        for i in range(2):
            P.op("dve", lambda e, i=i: e.memset(KTh[i][:, 0:PADK], 0.0), writes=[tKTh[i]])
            P.op("dve", lambda e, i=i: e.memset(KTh[i][:, PADK + L:], 0.0), writes=[tKTh[i]])

        DIL = (1, 4, 16)
        slot = 0
        sidx = 0
        oidx = 0
        pidx = 0
        def p3_setup(h, g, sl):
            d = DIL[g]
            hg = g * 8 + h
            Ls = L // d
            nt = Ls // 128 + 1
            for half in range(2):
                P.dma("sp", QTh[sl][half * 32:(half + 1) * 32, :], QT[half, hg, :, :], tQTh[sl],
                      writes=[tQTh[sl]])
                P.dma("sp", KTh[sl][half * 32:(half + 1) * 32, PADK:PADK + L], KT[half, hg, :, :], tKTh[sl],
                      writes=[tKTh[sl]])
            P.op("dve", lambda e, sl=sl: e.memset(Vt[sl], 0.0), writes=[tVt[sl]])
            vcols = slice(hg * 64, (hg + 1) * 64)
            for r in range(d):
                base = r * nt
                P.dma("sp", Vt[sl][64:128, base, :], Vd[ssl(r, 64, d), vcols], tVt[sl], reads=[tVd],
                      writes=[tVt[sl]])
                j0 = 64
                nmid = nt - 2
                srcv = Vd[ssl(r + d * j0, 128 * nmid, d), vcols].rearrange("(tau p) c -> p tau c", p=128)
                P.dma("sp", Vt[sl][:, base + 1:base + 1 + nmid, :], srcv, tVt[sl], reads=[tVd],
                      writes=[tVt[sl]])
                jl = Ls - 64
                P.dma("sp", Vt[sl][0:64, base + nt - 1, :], Vd[ssl(r + d * jl, 64, d), vcols], tVt[sl],
                      reads=[tVd], writes=[tVt[sl]])

        jobs3 = [(h_, g_) for h_ in range(8) for g_ in range(3)]
        p3_setup(jobs3[0][0], jobs3[0][1], 0)
        for h in range(8):
            for g in range(3):
                d = DIL[g]
                hg = g * 8 + h
                Ls = L // d
                nqb = Ls // 128
                nt = nqb + 1
                sl = slot % 2
                slot += 1
                if slot < len(jobs3):
                    p3_setup(jobs3[slot][0], jobs3[slot][1], slot % 2)
                SBK = (0, 1, 6)
                tl = [(r, tau) for r in range(d) for tau in range(nt)]
                info = {}
                ocur = {}

                def stS(ix):
                    r, tau = tl[ix]
                    qb_lo = max(tau - 1, 0)
                    qb_hi = min(tau, nqb - 1)
                    nq = (qb_hi - qb_lo + 1) * 128
                    mcol0 = (qb_lo - (tau - 1)) * 128
                    kstart = PADK + r + d * (128 * tau - 64)
                    kap = KTh[sl][:, ssl(kstart, 128, d)]
                    qstart = r + d * 128 * qb_lo
                    qap = QTh[sl][:, ssl(qstart, nq, d)]
                    sp_, tsp = psb[SBK[ix % 3]], tps[SBK[ix % 3]]
                    P.op("pe", lambda e, sp_=sp_, kap=kap, qap=qap, nq=nq: e.matmul(
                        sp_[:, 0:nq], lhsT=kap, rhs=qap, start=True, stop=True),
                        reads=[tKTh[sl], tQTh[sl]], writes=[tsp])
                    info[ix] = (r, tau, qb_lo, qb_hi, nq, mcol0, sp_, tsp)

                def stE(ix):
                    r, tau, qb_lo, qb_hi, nq, mcol0, sp_, tsp = info[ix]
                    pi_ = ix % 3
                    P.op("act", lambda e, sp_=sp_, pi_=pi_, nq=nq: e.activation(
                        out=pT[pi_][:, 0:nq], in_=sp_[:, 0:nq], func=AF.Exp, scale=0.125),
                        reads=[tsp], writes=[tpT[pi_]])
                    P.op("dve", lambda e, pi_=pi_, nq=nq, mcol0=mcol0: e.tensor_tensor(
                        out=pT[pi_][:, 0:nq], in0=pT[pi_][:, 0:nq], in1=mask_b[:, mcol0:mcol0 + nq],
                        op=ALU.mult),
                        reads=[tpT[pi_], tmask_b], writes=[tpT[pi_]])

                def stPV(ix, oidx_box):
                    r, tau, qb_lo, qb_hi, nq, mcol0, sp_, tsp = info.pop(ix)
                    pi_ = ix % 3
                    ok = 1 if tau == 0 else (2 if tau == nt - 1 else 0)
                    for qb in range(qb_lo, qb_hi + 1):
                        first = (qb == tau)
                        if first:
                            ocur[(r, qb)] = 2 + (oidx_box[0] % 4)
                            oidx_box[0] += 1
                        oi = ocur[(r, qb)]
                        op_ = psb[oi].rearrange("p (two q) -> p two q", two=2)
                        c0 = (qb - qb_lo) * 128
                        P.op("pe", lambda e, op_=op_, pi_=pi_, c0=c0, tile=r * nt + tau, first=first, sl=sl: e.matmul(
                            op_[0:64, 0, 0:128], lhsT=Vt[sl][:, tile, :], rhs=pT[pi_][:, c0:c0 + 128],
                            start=first, stop=False, skip_group_check=True),
                            reads=[tVt[sl], tpT[pi_]], writes=[tps[oi]])
                        P.op("pe", lambda e, op_=op_, pi_=pi_, c0=c0, ok=ok, first=first: e.matmul(
                            op_[0:64, 1, 0:128], lhsT=onesk[:, ok, :], rhs=pT[pi_][:, c0:c0 + 128],
                            start=False, stop=(not first), skip_group_check=True),
                            reads=[tonesk, tpT[pi_]], writes=[tps[oi]])
                        if not first:
                            t0 = r + d * 128 * qb
                            acc_ap = ACC[:, :, ssl(t0, 128, d)]
                            src = op_[0:64, :, 0:128]
                            if g == 0:
                                P.op("dve", lambda e, acc_ap=acc_ap, src=src: e.tensor_copy(out=acc_ap, in_=src),
                                     reads=[tps[oi]], writes=[tACC])
                            else:
                                P.op("dve", lambda e, acc_ap=acc_ap, src=src: e.tensor_tensor(
                                    out=acc_ap, in0=src, in1=acc_ap, op=ALU.add),
                                    reads=[tps[oi], tACC], writes=[tACC])

                n_t = len(tl)
                obox = [oidx]
                for i0 in range(min(3, n_t)):
                    stS(i0)
                stE(0)
                for ix in range(n_t):
                    if ix + 1 < n_t:
                        stE(ix + 1)
                    if ix + 3 < n_t:
                        stS(ix + 3)
                    stPV(ix, obox)
                oidx = obox[0]
            for c4 in range(4):
                csl = slice(c4 * 2048, (c4 + 1) * 2048)
                a_ = (h * 4 + c4) % 2
                P.dma("sp", za_t, ZA[h * 64:(h + 1) * 64, csl], tza, reads=[tZA], writes=[tza])
                P.op("act", lambda e, csl=csl: e.activation(out=dv, in_=ACC[:, 1, csl], func=AF.Ln), reads=[tACC],
                     writes=[tdv])
                P.op("act", lambda e: e.activation(out=dv, in_=dv, func=AF.Exp, scale=-1.0), reads=[tdv], writes=[tdv])
                P.op("dve", lambda e, csl=csl: e.tensor_tensor(out=dv, in0=dv, in1=ACC[:, 0, csl], op=ALU.mult),
                     reads=[tACC, tdv], writes=[tdv])
                P.op("dve", lambda e, a_=a_: e.tensor_tensor(out=ao[a_], in0=dv, in1=za_t, op=ALU.mult),
                     reads=[tdv, tza], writes=[tao[a_]])
                P.dma("act", ATT[h * 64:(h + 1) * 64, csl], ao[a_], tao[a_], reads=[tao[a_]], writes=[tATT])
        P.barrier()
    if debug and "stop3" in debug:
        P.emit([tATT, tQT, tKT, tVd, tZA, tUd, tZB, tGA, tGB])
        return nc, dbg_outs

    if debug and "only4" in debug:
        P = Prog(nc)
        for t_ in (t_ident_f, t_ident_b, t_gcol, tUd, tATT) + tuple(tps) + (tpst,):
            t_.w, t_.rs, t_.dsem, t_.dcnt = {}, {}, None, 0
        load_consts()
    Yd = dscr("Yd", [32, 128, 1024])
    tYd = Tk("Yd")
    PI = float(np.pi)
    stM = ExitStack()
    MATS = sbs(stM, "MATS", [128, 64, 4, 128], BF16)
    tMATS = [Tk("MATS%d" % i) for i in range(64)]
    a64 = sbs(stM, "a64", [128, 64], F32)
    b64 = sbs(stM, "b64", [128, 64], F32)
    nb64 = sbs(stM, "nb64", [128, 64], F32)
    tab64 = Tk("ab64")
    Jm_f = sbs(stM, "Jm_f", [128, 128], F32)
    tJm = Tk("Jm")
    P.dma("sp", Jm_f, jm_in, tJm, writes=[tJm])
    with ExitStack() as st:
        def small(name, shape, dt=F32):
            return sbs(st, "s4_" + name, shape, dt), Tk(name)
        lamr, tlamr = small("lamr", [128, 64])
        lami, tlami = small("lami", [128, 64])
        stp, tstp = small("stp", [128, 64])
        ar, tar = small("ar", [128, 64])
        ai, tai = small("ai", [128, 64])
        EX, tEX = small("EX", [128, 2, 32])
        sg, tsg = small("sg", [128, 2])
        BA, tBA = small("BA", [128, 64, 16])
        BB, tBB = small("BB", [128, 64, 16])
        CA, tCA = small("CA", [128, 64, 16])
        CB, tCB = small("CB", [128, 64, 16])
        dcol, tdcol = small("dcol", [128, 32])
        Jp, tJp = small("Jp", [128, 128])
        cmask, tcmask = small("cmask", [128, 2, 128])
        for dst, t_, src in ((lamr, tlamr, lamr_in), (lami, tlami, lami_in), (stp, tstp, lstep_in), (EX, tEX, ex_in),
                             (sg, tsg, sg_in), (BA, tBA, ba_in), (BB, tBB, bb_in), (CA, tCA, ca_in), (CB, tCB, cb_in),
                             (dcol, tdcol, dcol_in), (Jp, tJp, jp_in), (cmask, tcmask, cmask_in)):
            P.dma("sp", dst, src, t_, writes=[t_])
        P.op("act", lambda e: e.activation(out=stp, in_=stp, func=AF.Exp), reads=[tstp], writes=[tstp])
        P.op("dve", lambda e: e.tensor_tensor(out=ar, in0=lamr, in1=stp, op=ALU.mult), reads=[tlamr, tstp], writes=[tar])
        P.op("dve", lambda e: e.tensor_tensor(out=ai, in0=lami, in1=stp, op=ALU.mult), reads=[tlami, tstp], writes=[tai])
        ang, tang = small("ang", [128, 2, 32, 32])
        ang2, tang2 = small("ang2", [128, 2, 32, 32])
        mgl, tmgl = small("mgl", [128, 2, 32, 32])
        mc, tmc = small("mc", [128, 2, 32, 32])
        ms, tms = small("ms", [128, 2, 32, 32])
        mc1, tmc1 = small("mc1", [128, 2, 32, 32])
        ms2, tms2 = small("ms2", [128, 2, 32, 32])
        for r in range(2):
            aib = ai[:, r * 32:(r + 1) * 32].unsqueeze(2).to_broadcast([128, 32, 32])
            arb = ar[:, r * 32:(r + 1) * 32].unsqueeze(2).to_broadcast([128, 32, 32])
            exb = EX[:, r, :].unsqueeze(1).to_broadcast([128, 32, 32])
            P.op("dve", lambda e, r=r, aib=aib, exb=exb: e.tensor_tensor(out=ang[:, r], in0=aib, in1=exb, op=ALU.mult),
                 reads=[tai, tEX], writes=[tang])
            P.op("dve", lambda e, r=r, arb=arb, exb=exb: e.tensor_tensor(out=mgl[:, r], in0=arb, in1=exb, op=ALU.mult),
                 reads=[tar, tEX], writes=[tmgl])
        angf = ang.rearrange("p a b c -> p (a b c)")
        ang2f = ang2.rearrange("p a b c -> p (a b c)")
        mglf = mgl.rearrange("p a b c -> p (a b c)")
        mcf = mc.rearrange("p a b c -> p (a b c)")
        msf = ms.rearrange("p a b c -> p (a b c)")
        mc1f = mc1.rearrange("p a b c -> p (a b c)")
        ms2f = ms2.rearrange("p a b c -> p (a b c)")
        OFF = 64.0 * PI
        INV2PI = 1.0 / (2 * PI)
        kint, tkint = small("kint", [128, 2048], mybir.dt.int32)
        kf, tkf = small("kf", [128, 2048])
        P.op("dve", lambda e: e.tensor_scalar(out=ang2f, in0=angf, scalar1=OFF + PI / 2, scalar2=INV2PI, op0=ALU.add,
                                              op1=ALU.mult), reads=[tang], writes=[tang2])
        P.op("dve", lambda e: e.tensor_scalar(out=angf, in0=angf, scalar1=OFF, scalar2=INV2PI, op0=ALU.add,
                                              op1=ALU.mult), reads=[tang], writes=[tang])
        for af_, taf_ in ((ang2f, tang2), (angf, tang)):
            P.op("dve", lambda e, af_=af_: e.tensor_copy(out=kint, in_=af_), reads=[taf_], writes=[tkint])
            P.op("dve", lambda e: e.tensor_copy(out=kf, in_=kint), reads=[tkint], writes=[tkf])
            P.op("dve", lambda e, af_=af_: e.tensor_tensor(out=af_, in0=af_, in1=kf, op=ALU.subtract), reads=[taf_, tkf],
                 writes=[taf_])
            P.op("dve", lambda e, af_=af_: e.tensor_scalar(out=kf, in0=af_, scalar1=0.5, scalar2=None, op0=ALU.is_gt),
                 reads=[taf_], writes=[tkf])
            P.op("dve", lambda e, af_=af_: e.tensor_tensor(out=af_, in0=af_, in1=kf, op=ALU.subtract), reads=[taf_, tkf],
                 writes=[taf_])
            P.op("dve", lambda e, af_=af_: e.tensor_scalar(out=kf, in0=af_, scalar1=-0.5, scalar2=None, op0=ALU.is_lt),
                 reads=[taf_], writes=[tkf])
            P.op("dve", lambda e, af_=af_: e.tensor_tensor(out=af_, in0=af_, in1=kf, op=ALU.add), reads=[taf_, tkf],
                 writes=[taf_])
        P.op("act", lambda e: e.activation(out=ang2f, in_=ang2f, func=AF.Sin, scale=2 * PI), reads=[tang2], writes=[tang2])
        P.op("act", lambda e: e.activation(out=angf, in_=angf, func=AF.Sin, scale=2 * PI), reads=[tang], writes=[tang])
        P.op("act", lambda e: e.activation(out=mglf, in_=mglf, func=AF.Exp), reads=[tmgl], writes=[tmgl])
        P.op("dve", lambda e: e.tensor_tensor(out=mcf, in0=mglf, in1=ang2f, op=ALU.mult), reads=[tmgl, tang2], writes=[tmc])
        P.op("dve", lambda e: e.tensor_tensor(out=msf, in0=mglf, in1=angf, op=ALU.mult), reads=[tmgl, tang], writes=[tms])
        P.op("dve", lambda e: e.tensor_scalar(out=mc1f, in0=mcf, scalar1=sg[:, 0:1], scalar2=None, op0=ALU.mult),
             reads=[tmc, tsg], writes=[tmc1])
        P.op("dve", lambda e: e.tensor_scalar(out=ms2f, in0=msf, scalar1=sg[:, 1:2], scalar2=None, op0=ALU.mult),
             reads=[tms, tsg], writes=[tms2])
        def pw(tab, j):
            return tab[:, :, :, 24 + j]
        l1r, tl1r = small("l1r", [128, 2, 32])
        nr_, tnr = small("nr_", [128, 2, 32])
        den, tden = small("den", [128, 2, 32])
        tmpa, ttmpa = small("tmpa", [128, 2, 32])
        tmpb, ttmpb = small("tmpb", [128, 2, 32])
        wr, twr = small("wr", [128, 2, 32])
        wi, twi = small("wi", [128, 2, 32])
        wi1, twi1 = small("wi1", [128, 2, 32])
        wi2, twi2 = small("wi2", [128, 2, 32])
        lr3 = lamr.rearrange("p (r g) -> p r g", r=2)
        li3 = lami.rearrange("p (r g) -> p r g", r=2)
        P.op("dve", lambda e: e.tensor_scalar(out=nr_, in0=pw(mc, 0), scalar1=-1.0, scalar2=None, op0=ALU.add),
             reads=[tmc], writes=[tnr])
        P.op("dve", lambda e: e.tensor_tensor(out=den, in0=lr3, in1=lr3, op=ALU.mult), reads=[tlamr], writes=[tden])
        P.op("dve", lambda e: e.tensor_tensor(out=tmpa, in0=li3, in1=li3, op=ALU.mult), reads=[tlami], writes=[ttmpa])
        P.op("dve", lambda e: e.tensor_tensor(out=den, in0=den, in1=tmpa, op=ALU.add), reads=[tden, ttmpa], writes=[tden])
        P.op("dve", lambda e: e.tensor_tensor(out=tmpa, in0=nr_, in1=lr3, op=ALU.mult), reads=[tnr, tlamr], writes=[ttmpa])
        P.op("dve", lambda e: e.tensor_tensor(out=tmpb, in0=pw(ms, 0), in1=li3, op=ALU.mult), reads=[tms, tlami],
             writes=[ttmpb])
        P.op("dve", lambda e: e.tensor_tensor(out=tmpa, in0=tmpa, in1=tmpb, op=ALU.add), reads=[ttmpa, ttmpb],
             writes=[ttmpa])
        P.op("dve", lambda e: e.reciprocal(out=den, in_=den), reads=[tden], writes=[tden])
        P.op("dve", lambda e: e.tensor_tensor(out=wr, in0=tmpa, in1=den, op=ALU.mult), reads=[ttmpa, tden], writes=[twr])
        P.op("dve", lambda e: e.tensor_tensor(out=tmpa, in0=pw(ms, 0), in1=lr3, op=ALU.mult), reads=[tms, tlamr],
             writes=[ttmpa])
        P.op("dve", lambda e: e.tensor_tensor(out=tmpb, in0=nr_, in1=li3, op=ALU.mult), reads=[tnr, tlami], writes=[ttmpb])
        P.op("dve", lambda e: e.tensor_tensor(out=tmpa, in0=tmpa, in1=tmpb, op=ALU.subtract), reads=[ttmpa, ttmpb],
             writes=[ttmpa])
        P.op("dve", lambda e: e.tensor_tensor(out=wi, in0=tmpa, in1=den, op=ALU.mult), reads=[ttmpa, tden], writes=[twi])
        P.op("dve", lambda e: e.tensor_scalar(out=wi1, in0=wi, scalar1=sg[:, 0:1], scalar2=None, op0=ALU.mult),
             reads=[twi, tsg], writes=[twi1])
        P.op("dve", lambda e: e.tensor_scalar(out=wi2, in0=wi, scalar1=sg[:, 1:2], scalar2=None, op0=ALU.mult),
             reads=[twi, tsg], writes=[twi2])
        bA, tbA = small("bA", [128, 64, 16])
        bB, tbB = small("bB", [128, 64, 16])
        tmp16, ttmp16 = small("tmp16", [128, 64, 16])

        def bc16(t):
            return t.rearrange("p r g -> p (r g)").unsqueeze(2).to_broadcast([128, 64, 16])
        P.op("dve", lambda e: e.tensor_tensor(out=bA, in0=BA, in1=bc16(wr), op=ALU.mult), reads=[tBA, twr], writes=[tbA])
        P.op("dve", lambda e: e.tensor_tensor(out=tmp16, in0=BB, in1=bc16(wi2), op=ALU.mult), reads=[tBB, twi2],
             writes=[ttmp16])
        P.op("dve", lambda e: e.tensor_tensor(out=bA, in0=bA, in1=tmp16, op=ALU.add), reads=[tbA, ttmp16], writes=[tbA])
        P.op("dve", lambda e: e.tensor_tensor(out=bB, in0=BB, in1=bc16(wr), op=ALU.mult), reads=[tBB, twr], writes=[tbB])
        P.op("dve", lambda e: e.tensor_tensor(out=tmp16, in0=BA, in1=bc16(wi1), op=ALU.mult), reads=[tBA, twi1],
             writes=[ttmp16])
        P.op("dve", lambda e: e.tensor_tensor(out=bB, in0=bB, in1=tmp16, op=ALU.add), reads=[tbB, ttmp16], writes=[tbB])
        P.op("dve", lambda e: e.tensor_copy(out=a64.rearrange("p (r g) -> p r g", r=2), in_=pw(mc, 2)), reads=[tmc],
             writes=[tab64])
        P.op("dve", lambda e: e.tensor_copy(out=b64.rearrange("p (r g) -> p r g", r=2), in_=pw(ms, 2)), reads=[tms],
             writes=[tab64])
        P.op("dve", lambda e: e.tensor_scalar(out=nb64, in0=b64, scalar1=-1.0, scalar2=None, op0=ALU.mult),
             reads=[tab64], writes=[tab64])
        l8a, tl8a = small("l8a", [128, 2, 32])
        l8b, tl8b = small("l8b", [128, 2, 32])
        P.op("dve", lambda e: e.tensor_copy(out=l8a, in_=pw(mc, 1)), reads=[tmc], writes=[tl8a])
        P.op("dve", lambda e: e.tensor_copy(out=l8b, in_=pw(mc1, 1)), reads=[tmc1], writes=[tl8b])
        P.op("dve", lambda e: e.tensor_scalar(out=l8b, in0=pw(ms, 1), scalar1=sg[:, 0:1], scalar2=None, op0=ALU.mult),
             reads=[tms, tsg], writes=[tl8b])
        Pt, tPt = small("Pt", [128, 8, 8, 16])
        Qt, tQt = small("Qt", [128, 8, 8, 16])
        M4t, tM4t = small("M4t", [128, 8, 8, 16])
        tq, ttq = small("tq", [128, 8, 8, 16])
        m1tmp, tm1tmp = small("m1tmp", [128, 128])
        ddiag, tddiag = small("ddiag", [128, 128])
        l8tmp, tl8tmp = small("l8tmp", [128, 128])
        pcn = 0
        for r in range(2):
            for gb in range(4):
                gsl = slice(gb * 8, (gb + 1) * 8)
                gdsl = slice(r * 32 + gb * 8, r * 32 + (gb + 1) * 8)

                def wtab(tab, w):
                    return tab[:, r, gsl, w * 8:(w + 1) * 8].unsqueeze(3).to_broadcast([128, 8, 8, 16])

                def ctab(tab):
                    return tab[:, gdsl, :].unsqueeze(2).to_broadcast([128, 8, 8, 16])
                for (dst, tdst, w, A, tA, Bm, tB, kind) in ((Pt, tPt, 0, bA, tbA, bB, tbB, "p"),
                                                            (Qt, tQt, 1, CA, tCA, CB, tCB, "q"),
                                                            (M4t, tM4t, 2, CA, tCA, CB, tCB, "q")):
                    if kind == "p":
                        m_a, tm_a, m_b, tm_b, op2 = mc, tmc, ms2, tms2, ALU.add
                    else:
                        m_a, tm_a, m_b, tm_b, op2 = mc1, tmc1, ms, tms, ALU.subtract
                    i0a, i1a = wtab(m_a, w), ctab(A)
                    i0b, i1b = wtab(m_b, w), ctab(Bm)
                    P.op("dve", lambda e, dst=dst, i0a=i0a, i1a=i1a: e.tensor_tensor(
                        out=dst, in0=i0a, in1=i1a, op=ALU.mult), reads=[tm_a, tA], writes=[tdst])
                    P.op("dve", lambda e, i0b=i0b, i1b=i1b: e.tensor_tensor(
                        out=tq, in0=i0b, in1=i1b, op=ALU.mult), reads=[tm_b, tB], writes=[ttq])
                    P.op("dve", lambda e, dst=dst, op2=op2: e.tensor_tensor(out=dst, in0=dst, in1=tq, op=op2),
                         reads=[tdst, ttq], writes=[tdst])
                for gi in range(8):
                    g = gb * 8 + gi
                    gd = r * 32 + g
                    Pg = Pt[:, gi].rearrange("p s c -> p (s c)")
                    Qg = Qt[:, gi].rearrange("p s c -> p (s c)")
                    M4g = M4t[:, gi].rearrange("p s c -> p (s c)")
                    pa_ = psb[pcn % 6]
                    tpa_ = tps[pcn % 6]
                    pcn += 1
                    P.op("pe", lambda e, pa_=pa_, Pg=Pg, Qg=Qg: e.matmul(pa_[:, 0:128], lhsT=Pg, rhs=Qg, start=True,
                                                                        stop=True),
                         reads=[tPt, tQt], writes=[tpa_])
                    P.op("dve", lambda e, pa_=pa_, r=r: e.tensor_tensor(out=m1tmp, in0=pa_[:, 0:128], in1=cmask[:, r, :],
                                                                       op=ALU.mult),
                         reads=[tpa_, tcmask], writes=[tm1tmp])
                    if r == 0:
                        P.op("dve", lambda e, g=g, gd=gd: e.scalar_tensor_tensor(
                            out=MATS[:, gd, 1, :], in0=ident_f, scalar=dcol[:, g:g + 1], in1=m1tmp, op0=ALU.mult,
                            op1=ALU.add), reads=[t_ident_f, tdcol, tm1tmp], writes=[tMATS[gd]])
                    else:
                        P.op("dve", lambda e, gd=gd: e.tensor_copy(out=MATS[:, gd, 1, :], in_=m1tmp),
                             reads=[tm1tmp], writes=[tMATS[gd]])
                    pb_ = psb[pcn % 6]
                    tpb_ = tps[pcn % 6]
                    pcn += 1
                    P.op("pe", lambda e, pb_=pb_, Pg=Pg: e.transpose(out=pb_[:, 0:128], in_=Pg, identity=ident_f),
                         reads=[tPt, t_ident_f], writes=[tpb_])
                    P.op("act", lambda e, pb_=pb_, gd=gd: e.activation(out=MATS[:, gd, 0, :], in_=pb_[:, 0:128],
                                                                      func=AF.Copy),
                         reads=[tpb_], writes=[tMATS[gd]])
                    P.op("act", lambda e, M4g=M4g, gd=gd: e.activation(out=MATS[:, gd, 2, :], in_=M4g, func=AF.Copy),
                         reads=[tM4t], writes=[tMATS[gd]])
                    P.op("dve", lambda e, r=r, g=g: e.tensor_scalar(out=l8tmp, in0=Jp, scalar1=l8b[:, r, g:g + 1],
                                                                   scalar2=None, op0=ALU.mult),
                         reads=[tJp, tl8b], writes=[tl8tmp])
                    P.op("dve", lambda e, r=r, g=g, gd=gd: e.scalar_tensor_tensor(
                        out=MATS[:, gd, 3, :], in0=ident_f, scalar=l8a[:, r, g:g + 1], in1=l8tmp, op0=ALU.mult,
                        op1=ALU.add), reads=[t_ident_f, tl8a, tl8tmp], writes=[tMATS[gd]])
        dump("mc", mc, tmc, F32)
        dump("ms", ms, tms, F32)
        dump("bA", bA, tbA, F32)
        dump("wr", wr, twr, F32)
        dump("wi", wi, twi, F32)
        dump("Pt", Pt, tPt, F32)
        dump("Qt", Qt, tQt, F32)
        tall = Tk("matsall")
        P.barrier()
        for i8 in range(8):
            dump("MATS%d" % i8, MATS[:, i8 * 8:(i8 + 1) * 8], tall, BF16)
        P.barrier()

    with ExitStack() as st:
        Ug = sbs(st, "Ug", [128, 32, 1024], BF16)
        tUg = [Tk("Ug%d" % i) for i in range(32)]
        for g in range(32):
            for t8 in range(8):
                P.dma("sp", Ug[t8 * 16:(t8 + 1) * 16, g, :], Ud[g // 8, t8, (g % 8) * 16:(g % 8 + 1) * 16, :], tUg[g],
                      reads=[tUd], writes=[tUg[g]])
        SS = sbs(st, "SS", [128, 2, 32, 128], F32)
        tSS = Tk("SS")
        G0 = sbs(st, "G0", [128, 2, 32, 128], BF16)
        tG0 = [Tk("G0_0"), Tk("G0_1")]
        hq = [sbs(st, "hq%d" % i, [128, 4, 128], BF16) for i in range(2)]
        thq = [Tk("hq%d" % i) for i in range(2)]
        t1b = sbs(st, "t1b", [128, 2, 32], F32)
        t2b = sbs(st, "t2b", [128, 2, 32], F32)
        tt1b, tt2b = Tk("t1b"), Tk("t2b")
        ystg = [sbs(st, "ystg%d" % i, [128, 1024], F32) for i in range(2)]
        tystg = [Tk("ystg%d" % i) for i in range(2)]
        yx = sbs(st, "yx", [128, 1024], F32)
        tyx = Tk("yx")
        yo = [sbs(st, "yo%d" % i, [128, 1024], BF16) for i in range(2)]
        tyo = [Tk("yo%d" % i) for i in range(2)]
        P.op("dve", lambda e: e.memset(G0, 0.0), writes=tG0)
        hb = 0
        for r in range(2):
            for quad in range(8):
                for n in range(8):
                    i = n if r == 0 else 7 - n
                    bank, tbank = psb[hb % 2], tps[hb % 2]
                    for j in range(4):
                        g = quad * 4 + j
                        gd = r * 32 + g
                        P.op("pe", lambda e, bank=bank, j=j, gd=gd, g=g, i=i, n=n: e.matmul(
                            bank[:, j * 128:(j + 1) * 128], lhsT=MATS[:, gd, 0, :], rhs=Ug[:, g, i:1024:8],
                            start=(j == 0), stop=(n == 0), skip_group_check=True), reads=[tMATS[gd], tUg[g]],
                            writes=[tbank])
                        if n > 0:
                            P.op("pe", lambda e, bank=bank, j=j, gd=gd, n=n: e.matmul(
                                bank[:, j * 128:(j + 1) * 128], lhsT=MATS[:, gd, 3, :], rhs=hq[(n - 1) % 2][:, j, :],
                                start=False, stop=True, skip_group_check=True), reads=[tMATS[gd], thq[(n - 1) % 2]],
                                writes=[tbank])
                    if n < 7:
                        P.op("act", lambda e, bank=bank, n=n: e.activation(
                            out=hq[n % 2].rearrange("p a b -> p (a b)"), in_=bank, func=AF.Copy),
                            reads=[tbank], writes=[thq[n % 2]])
                    else:
                        P.op("act", lambda e, bank=bank, quad=quad: e.activation(
                            out=SS[:, 0, quad * 4:(quad + 1) * 4, :].rearrange("p a b -> p (a b)"), in_=bank,
                            func=AF.Copy), reads=[tbank], writes=[tSS])
                    hb += 1
            for blk in range(8):
                bank, tbank = psb[2 + blk % 2], tps[2 + blk % 2]
                P.op("pe", lambda e, bank=bank, blk=blk: e.matmul(
                    bank, lhsT=Jm_f, rhs=SS[:, 0, blk * 4:(blk + 1) * 4, :].rearrange("p a b -> p (a b)"),
                    start=True, stop=True), reads=[tJm, tSS], writes=[tbank])
                P.op("dve", lambda e, bank=bank, blk=blk: e.tensor_copy(
                    out=SS[:, 1, blk * 4:(blk + 1) * 4, :].rearrange("p a b -> p (a b)"), in_=bank),
                    reads=[tbank], writes=[tSS])
            ks = list(range(128)) if r == 0 else list(range(127, -1, -1))
            arow = a64[:, r * 32:(r + 1) * 32]
            brow = b64[:, r * 32:(r + 1) * 32]
            nbrow = nb64[:, r * 32:(r + 1) * 32]
            a2 = arow.unsqueeze(1).to_broadcast([128, 2, 32])
            for kk in range(1, 128):
                kp, k = ks[kk - 1], ks[kk]
                P.op("dve", lambda e, kp=kp, a2=a2: e.tensor_tensor(out=t1b, in0=SS[:, :, :, kp], in1=a2, op=ALU.mult),
                     reads=[tSS, tab64], writes=[tt1b])
                P.op("dve", lambda e, kp=kp, brow=brow: e.tensor_tensor(out=t2b[:, 0, :], in0=SS[:, 1, :, kp], in1=brow,
                                                                       op=ALU.mult),
                     reads=[tSS, tab64], writes=[tt2b])
                P.op("dve", lambda e, kp=kp, nbrow=nbrow: e.tensor_tensor(out=t2b[:, 1, :], in0=SS[:, 0, :, kp],
                                                                         in1=nbrow, op=ALU.mult),
                     reads=[tSS, tab64], writes=[tt2b])
                P.op("dve", lambda e: e.tensor_tensor(out=t1b, in0=t1b, in1=t2b, op=ALU.add), reads=[tt1b, tt2b],
                     writes=[tt1b])
                P.op("dve", lambda e, k=k: e.tensor_tensor(out=SS[:, :, :, k], in0=SS[:, :, :, k], in1=t1b, op=ALU.add),
                     reads=[tSS, tt1b], writes=[tSS])
            dump("SS%d_0" % r, SS[:, 0], tSS, F32)
            dump("SS%d_1" % r, SS[:, 1], tSS, F32)
            if r == 0:
                P.op("dve", lambda e: e.tensor_copy(out=G0[:, 0, :, 1:128], in_=SS[:, 0, :, 0:127]), reads=[tSS],
                     writes=[tG0[0]])
            else:
                P.op("dve", lambda e: e.tensor_copy(out=G0[:, 1, :, 0:127], in_=SS[:, 0, :, 1:128]), reads=[tSS],
                     writes=[tG0[1]])
        hq2 = [hq[0].rearrange("p (a b) c -> p a b c", a=2), hq[1].rearrange("p (a b) c -> p a b c", a=2)]
        for g in range(32):
            Yb = [psb[2 + 2 * (g % 2)], psb[3 + 2 * (g % 2)]]
            tYb = [tps[2 + 2 * (g % 2)], tps[3 + 2 * (g % 2)]]
            cnt_i = [0] * 8
            ybank_started = [False, False]
            for n in range(8):
                bank, tbank = psb[hb % 2], tps[hb % 2]
                hb += 1
                for r in range(2):
                    i = n if r == 0 else 7 - n
                    gd = r * 32 + g
                    if n == 0:
                        prev, tprev = G0[:, r, g, :], tG0[r]
                    else:
                        prev, tprev = hq2[(n - 1) % 2][:, 0, r, :], thq[(n - 1) % 2]
                    yb, tyb = Yb[i // 4], tYb[i // 4]
                    yc = slice((i % 4) * 128, (i % 4 + 1) * 128)
                    u_i = Ug[:, g, i:1024:8]
                    st_ = not ybank_started[i // 4]
                    ybank_started[i // 4] = True
                    P.op("pe", lambda e, yb=yb, yc=yc, gd=gd, u_i=u_i, st_=st_: e.matmul(
                        yb[:, yc], lhsT=MATS[:, gd, 1, :], rhs=u_i, start=st_, stop=False, skip_group_check=True),
                        reads=[tMATS[gd], tUg[g]], writes=[tyb])
                    P.op("pe", lambda e, yb=yb, yc=yc, gd=gd, prev=prev, sp_=(cnt_i[i] == 1): e.matmul(
                        yb[:, yc], lhsT=MATS[:, gd, 2, :], rhs=prev, start=False, stop=sp_, skip_group_check=True),
                        reads=[tMATS[gd], tprev], writes=[tyb])
                    cnt_i[i] += 1
                    if n < 7:
                        P.op("pe", lambda e, bank=bank, r=r, gd=gd, u_i=u_i: e.matmul(
                            bank[:, r * 128:(r + 1) * 128], lhsT=MATS[:, gd, 0, :], rhs=u_i, start=(r == 0), stop=False,
                            skip_group_check=True), reads=[tMATS[gd], tUg[g]], writes=[tbank])
                        P.op("pe", lambda e, bank=bank, r=r, gd=gd, prev=prev: e.matmul(
                            bank[:, r * 128:(r + 1) * 128], lhsT=MATS[:, gd, 3, :], rhs=prev, start=False, stop=True,
                            skip_group_check=True), reads=[tMATS[gd], tprev], writes=[tbank])
                if n < 7:
                    P.op("act", lambda e, bank=bank, n=n: e.activation(
                        out=hq[n % 2][:, 0:2, :].rearrange("p a b -> p (a b)"), in_=bank[:, 0:256], func=AF.Copy),
                        reads=[tbank], writes=[thq[n % 2]])
            ys = ystg[g % 2]
            for b2 in range(2):
                P.op("act", lambda e, ys=ys, b2=b2, Yb=Yb: e.activation(
                    out=ys.rearrange("p (k i) -> p i k", i=8)[:, 4 * b2:4 * b2 + 4, :],
                    in_=Yb[b2].rearrange("p (i k) -> p i k", i=4), func=AF.Copy),
                    reads=[tYb[b2]], writes=[tystg[g % 2]])
            P.op("dve", lambda e, ys=ys: e.tensor_tensor(out=yx, in0=ys, in1=ys, op=ALU.mult), reads=[tystg[g % 2]],
                 writes=[tyx])
            P.op("dve", lambda e: e.tensor_scalar(out=yx, in0=yx, scalar1=0.044715, scalar2=1.0, op0=ALU.mult,
                                                  op1=ALU.add), reads=[tyx], writes=[tyx])
            P.op("dve", lambda e, ys=ys: e.tensor_tensor(out=yx, in0=yx, in1=ys, op=ALU.mult), reads=[tyx, tystg[g % 2]],
                 writes=[tyx])
            P.op("act", lambda e: e.activation(out=yx, in_=yx, func=AF.Sigmoid, scale=1.5957691216057308),
                 reads=[tyx], writes=[tyx])
            P.op("dve", lambda e, ys=ys, g=g: e.tensor_tensor(out=yo[g % 2], in0=yx, in1=ys, op=ALU.mult),
                 reads=[tyx, tystg[g % 2]], writes=[tyo[g % 2]])
            P.dma("act", Yd[g], yo[g % 2], tyo[g % 2], reads=[tyo[g % 2]], writes=[tYd])
        P.barrier()
    stM.close()
    if debug and "stop4" in debug:
        if "only4" in debug:
            P.emit([tYd] + dbg_tiles)
        else:
            P.emit([tYd, tATT, tQT, tKT, tVd, tZA, tUd, tZB, tGA, tGB] + dbg_tiles)
        return nc, dbg_outs

    with ExitStack() as st:
        wstf = [sbs(st, "wstf%d" % i, [128, 1024], F32) for i in range(2)]
        twstf = [Tk("wstf%d" % i) for i in range(2)]
        wa = sbs(st, "wa", [128, 4, 1024], BF16)
        wb = sbs(st, "wb", [128, 4, 1024], BF16)
        wgl = sbs(st, "wgl", [128, 4, 512], BF16)
        wo = sbs(st, "wo", [128, 8, 1024], BF16)
        twa, twb, twgl, two = Tk("wa"), Tk("wb"), Tk("wgl"), Tk("wo")
        wc = 0
        for (dst, tdst, src, nk, ncol) in ((wa, twa, wa_in, 4, 1024), (wb, twb, wb_in, 4, 1024),
                                            (wgl, twgl, wglu_in, 4, 512), (wo, two, wout_in, 8, 1024)):
            for k in range(nk):
                b = wc % 2
                wc += 1
                P.dma("sp", wstf[b][:, 0:ncol], src[k * 128:(k + 1) * 128, :], twstf[b], writes=[twstf[b]])
                P.op("act", lambda e, dst=dst, k=k, b=b, ncol=ncol: e.activation(out=dst[:, k, :], in_=wstf[b][:, 0:ncol],
                                                                              func=AF.Copy),
                     reads=[twstf[b]], writes=[tdst])
        fgb = sbs(st, "fgb", [128, 1024], F32)
        tfgb = Tk("fgb")
        P.dma("sp", fgb, fg_in.partition_broadcast(128), tfgb, writes=[tfgb])
        sYall = sbs(st, "sYall", [128, 4, 8, 1024], BF16)
        tsY = Tk("sYall")
        for g in range(32):
            P.dma("sp", sYall[(g % 8) * 16:(g % 8 + 1) * 16, g // 8, :, :],
                  Yd[g].rearrange("(t co) c -> co t c", co=16), tsY, reads=[tYd], writes=[tsY])
        sl_ = sbs(st, "sl_", [128, 4, 512], BF16)
        tsl_ = Tk("sl_")
        att = [sbs(st, "att%d" % i, [128, 4, 512], BF16) for i in range(2)]
        tatt = [Tk("att%d" % i) for i in range(2)]
        zb = [sbs(st, "zb%d" % i, [128, 4, 512], BF16) for i in range(2)]
        tzb = [Tk("zb%d" % i) for i in range(2)]
        gab = [sbs(st, "gab%d" % i, [128, 8, 512], BF16) for i in range(2)]
        tgab = [Tk("gab%d" % i) for i in range(2)]
        gbb = [sbs(st, "gbb%d" % i, [128, 8, 512], BF16) for i in range(2)]
        tgbb = [Tk("gbb%d" % i) for i in range(2)]
        gl = sbs(st, "gl", [128, 4, 512], BF16)
        tgl = Tk("gl")
        s2 = sbs(st, "s2", [128, 4, 512], BF16)
        ts2 = Tk("s2")
        ma = sbs(st, "ma", [128, 512], F32)
        mb = sbs(st, "mb", [128, 512], F32)
        tma, tmb = Tk("ma"), Tk("mb")
        mg = sbs(st, "mg", [128, 8, 512], BF16)
        tmg = Tk("mg")
        xr = [sbs(st, "xr%d" % i, [128, 1024], F32) for i in range(2)]
        txr = [Tk("xr%d" % i) for i in range(2)]
        yt = [sbs(st, "yt%d" % i, [128, 1024], F32) for i in range(2)]
        tyt = [Tk("yt%d" % i) for i in range(2)]
        junk5 = sbs(st, "junk5", [128, 1024], BF16)
        tjunk5 = Tk("junk5")
        ss5 = [sbs(st, "ss5_%d" % i, [128, 1], F32) for i in range(2)]
        tss5 = [Tk("ss5_%d" % i) for i in range(2)]
        ty_ = Tk("y")
        pcn = 0
        xc = 0
        for tb in range(NB):
            bb = tb % 2
            tsl = slice(tb * 512, (tb + 1) * 512)
            P.dma("sp", att[bb], ATT[:, tsl].rearrange("(k p) t -> p k t", p=128), tatt[bb], reads=[tATT],
                  writes=[tatt[bb]])
            P.dma("sp", zb[bb], ZB[:, tsl].rearrange("(k p) t -> p k t", p=128), tzb[bb], reads=[tZB], writes=[tzb[bb]])
            P.dma("sp", gab[bb], GA[:, tsl].rearrange("(k p) t -> p k t", p=128), tgab[bb], reads=[tGA],
                  writes=[tgab[bb]])
            P.dma("sp", gbb[bb], GB[:, tsl].rearrange("(k p) t -> p k t", p=128), tgbb[bb], reads=[tGB],
                  writes=[tgbb[bb]])
            for kc in range(4):
                P.op("dve", lambda e, kc=kc, tb=tb: e.tensor_copy(
                    out=sl_[:, kc, :].rearrange("p (c t) -> p c t", t=8),
                    in_=sYall[:, kc, :, tb * 64:(tb + 1) * 64].rearrange("p t c -> p c t")),
                    reads=[tsY], writes=[tsl_])
            for ct in range(4):
                pi = pcn % 6
                pcn += 1
                for kc in range(4):
                    P.op("pe", lambda e, pi=pi, kc=kc, ct=ct: e.matmul(
                        psb[pi], lhsT=wgl[:, kc, ct * 128:(ct + 1) * 128], rhs=sl_[:, kc, :], start=(kc == 0),
                        stop=(kc == 3)), reads=[twgl, tsl_], writes=[tps[pi]])
                P.op("act", lambda e, pi=pi, ct=ct: e.activation(out=gl[:, ct, :], in_=psb[pi], func=AF.Sigmoid),
                     reads=[tps[pi]], writes=[tgl])
            P.op("dve", lambda e: e.tensor_tensor(out=s2, in0=sl_, in1=gl, op=ALU.mult), reads=[tsl_, tgl], writes=[ts2])
            P.op("dve", lambda e, bb=bb: e.tensor_tensor(out=s2, in0=s2, in1=zb[bb], op=ALU.mult), reads=[ts2, tzb[bb]],
                 writes=[ts2])
            for ct in range(8):
                pa_i, pb_i = pcn % 6, (pcn + 1) % 6
                pcn += 2
                for kc in range(4):
                    P.op("pe", lambda e, pa_i=pa_i, kc=kc, ct=ct, bb=bb: e.matmul(
                        psb[pa_i], lhsT=wa[:, kc, ct * 128:(ct + 1) * 128], rhs=att[bb][:, kc, :], start=(kc == 0),
                        stop=(kc == 3)), reads=[twa, tatt[bb]], writes=[tps[pa_i]])
                for kc in range(4):
                    P.op("pe", lambda e, pb_i=pb_i, kc=kc, ct=ct: e.matmul(
                        psb[pb_i], lhsT=wb[:, kc, ct * 128:(ct + 1) * 128], rhs=s2[:, kc, :], start=(kc == 0),
                        stop=(kc == 3)), reads=[twb, ts2], writes=[tps[pb_i]])
                P.op("dve", lambda e, pa_i=pa_i, ct=ct, bb=bb: e.tensor_tensor(out=ma, in0=psb[pa_i], in1=gab[bb][:, ct, :],
                                                                              op=ALU.mult),
                     reads=[tps[pa_i], tgab[bb]], writes=[tma])
                P.op("dve", lambda e, pb_i=pb_i, ct=ct, bb=bb: e.tensor_tensor(out=mb, in0=psb[pb_i], in1=gbb[bb][:, ct, :],
                                                                              op=ALU.mult),
                     reads=[tps[pb_i], tgbb[bb]], writes=[tmb])
                P.op("dve", lambda e, ct=ct: e.tensor_tensor(out=mg[:, ct, :], in0=ma, in1=mb, op=ALU.add),
                     reads=[tma, tmb], writes=[tmg])
            for t4 in range(4):
                tt = tb * 4 + t4
                xb = xc % 2
                xc += 1
                P.dma("sp", xr[xb], x[tt * 128:(tt + 1) * 128, :], txr[xb], writes=[txr[xb]])
                for hf in range(2):
                    pi = pcn % 6
                    pcn += 1
                    for kc in range(8):
                        P.op("pe", lambda e, pi=pi, kc=kc, t4=t4, hf=hf: e.matmul(
                            psb[pi], lhsT=mg[:, kc, t4 * 128:(t4 + 1) * 128], rhs=wo[:, kc, hf * 512:(hf + 1) * 512],
                            start=(kc == 0), stop=(kc == 7)), reads=[tmg, two], writes=[tps[pi]])
                    P.op("dve", lambda e, pi=pi, hf=hf, xb=xb: e.tensor_tensor(
                        out=yt[xb][:, hf * 512:(hf + 1) * 512], in0=psb[pi], in1=xr[xb][:, hf * 512:(hf + 1) * 512],
                        op=ALU.add), reads=[tps[pi], txr[xb]], writes=[tyt[xb]])
                P.op("act", lambda e, xb=xb: e.activation(out=junk5, in_=yt[xb], func=AF.Square, accum_out=ss5[xb]),
                     reads=[tyt[xb]], writes=[tjunk5, tss5[xb]])
                P.op("dve", lambda e, xb=xb: e.tensor_scalar(out=ss5[xb], in0=ss5[xb], scalar1=1.0 / D, scalar2=EPS,
                                                             op0=ALU.mult, op1=ALU.add), reads=[tss5[xb]],
                     writes=[tss5[xb]])
                P.op("act", lambda e, xb=xb: e.activation(out=ss5[xb], in_=ss5[xb], func=AF.Ln),
                     reads=[tss5[xb]], writes=[tss5[xb]])
                P.op("act", lambda e, xb=xb: e.activation(out=ss5[xb], in_=ss5[xb], func=AF.Exp, scale=-0.5),
                     reads=[tss5[xb]], writes=[tss5[xb]])
                P.op("act", lambda e, xb=xb: e.activation(out=yt[xb], in_=yt[xb], func=AF.Copy, scale=ss5[xb]),
                     reads=[tyt[xb], tss5[xb]], writes=[tyt[xb]])
                P.op("dve", lambda e, xb=xb: e.tensor_tensor(out=xr[xb], in0=yt[xb], in1=fgb, op=ALU.mult),
                     reads=[tyt[xb], tfgb], writes=[txr[xb]])
                P.dma("act", y[tt * 128:(tt + 1) * 128, :], xr[xb], txr[xb], reads=[txr[xb]], writes=[ty_])
        P.barrier()
        P.emit([ty_])
        return nc, dbg_outs


def rope_tables():
    half = 32
    inv = (np.float32(10000.0) ** (-np.arange(half, dtype=np.float32) / half)).astype(np.float32)
    pos = np.arange(L, dtype=np.float32)
    ang = (pos[None, :] * inv[:, None]).astype(np.float32)
    c = np.cos(ang.astype(np.float64)).astype(np.float32)
    s = np.sin(ang.astype(np.float64)).astype(np.float32)
    return np.tile(c, (4, 1)), np.tile(s, (4, 1))


def band_mask():
    p = np.arange(128)[:, None]
    q = np.arange(256)[None, :]
    return (((q - p) >= 0) & ((q - p) <= 128)).astype(np.float32)


def ssm_layouts(inputs):
    f = np.float32
    def q2(a):
        t = a.reshape(64, 64).T
        return np.ascontiguousarray(np.concatenate([t, t], 0).astype(f))
    lamr = q2(inputs["lam_re"][0]); lami = q2(inputs["lam_im"][0])
    lstep = np.ascontiguousarray(np.broadcast_to(inputs["log_step"][0].reshape(1, 64), (128, 64)).astype(f))
    br = inputs["b_re"][0].reshape(64, 64, 16).transpose(1, 0, 2)
    bi = inputs["b_im"][0].reshape(64, 64, 16).transpose(1, 0, 2)
    cr = inputs["c_re"][0].reshape(64, 16, 64).transpose(2, 0, 1)
    ci = inputs["c_im"][0].reshape(64, 16, 64).transpose(2, 0, 1)
    cat = lambda a, b: np.ascontiguousarray(np.concatenate([a, b], 0).astype(f))
    ex = np.zeros((128, 2, 4, 8), f)
    t = np.arange(8, dtype=f)
    for r in range(2):
        tp = t if r == 0 else 7 - t
        ex[:, r, 0, :] = 7 - tp
        ex[:, r, 1, :] = -(7 - tp)
        ex[:, r, 2, :] = tp + 1
        ex[:, r, 3, 0:3] = (1, 8, 64)
    sg = np.ones((128, 2), f); sg[64:, 0] = -1; sg[:64, 1] = -1
    jp = np.zeros((128, 128), f); jm = np.zeros((128, 128), f)
    for n in range(64):
        jp[n, n + 64] = 1; jp[n + 64, n] = 1
        jm[n + 64, n] = -1; jm[n, n + 64] = 1
    d = inputs["d_skip"][0].reshape(32, 16)
    dcol = np.ascontiguousarray(np.tile(d.T, (8, 1)).astype(f))
    cm = np.zeros((128, 2, 128), f)
    for s_ in range(8):
        for t_ in range(8):
            if t_ >= s_:
                cm[s_ * 16:(s_ + 1) * 16, 0, t_ * 16:(t_ + 1) * 16] = 1
            if t_ <= s_:
                cm[s_ * 16:(s_ + 1) * 16, 1, t_ * 16:(t_ + 1) * 16] = 1
    return {"lamr_q": lamr, "lami_q": lami, "lstep_q": lstep, "ex_tab": ex.reshape(128, 2, 32), "sg_tab": sg,
            "ba_q": cat(br, bi), "bb_q": cat(bi, br), "ca_q": cat(cr, ci), "cb_q": cat(ci, cr),
            "dcol": dcol, "cmask": cm, "jm": jm, "jp": jp}


def make_in_maps(inputs):
    xs = [inputs["x_prompt"][i] for i in range(4)] + [inputs["x_sample"][i] for i in range(2)]
    xs = xs + [xs[0], xs[1]]
    cosT, sinT = rope_tables()
    common = {
        "w_in": np.ascontiguousarray(inputs["w_in"][0]),
        "norm_g": np.ascontiguousarray(inputs["norm_g"][0].reshape(8, 128).T),
        "ident": np.eye(128, dtype=np.float32),
        "cosT": cosT, "sinT": sinT,
        "mask01": band_mask(),
        **ssm_layouts(inputs),
        "w_a": np.ascontiguousarray(inputs["w_branch_a"][0]), "w_b": np.ascontiguousarray(inputs["w_branch_b"][0]),
        "w_glu": np.ascontiguousarray(inputs["w_glu"][0]), "w_out": np.ascontiguousarray(inputs["w_out"][0]),
        "final_g": np.ascontiguousarray(inputs["final_g"]),
    }
    return [dict(common, x=np.ascontiguousarray(xs[c])) for c in range(NCORES)]


def kernel(**inputs):
    nc, _ = build()
    in_maps = make_in_maps(inputs)
    res = run_bass_kernel_spmd(nc, in_maps, core_ids=list(range(NCORES)))
    ys = [np.asarray(r["y"]).reshape(L, D) for r in res.results]
    return (np.stack(ys[0:4]).astype(np.float32), np.stack(ys[4:6]).astype(np.float32))
```

```python
import numpy as np
import ml_dtypes
import concourse.bass as bass
import concourse.mybir as mybir
from concourse.bass_utils import run_bass_kernel_spmd

F32 = mybir.dt.float32
BF16 = mybir.dt.bfloat16
ALU = mybir.AluOpType
AF = mybir.ActivationFunctionType

L = 8192
D = 1024
NCORES = 8
EPS = 1e-6
EPOCH = 30000


class Tk:
    __slots__ = ("w", "rs", "dsem", "dcnt", "name")

    def __init__(self, name=""):
        self.w = {}
        self.rs = {}
        self.dsem = None
        self.dcnt = 0
        self.name = name


class Prog:
    ENG = ("pe", "act", "dve", "pool", "sp")

    def __init__(self, nc):
        self.nc = nc
        self.ops = {e: [] for e in self.ENG}
        self.cnt = {e: 0 for e in self.ENG}
        self.esems = {e: [] for e in self.ENG}
        self.waited = {e: {} for e in self.ENG}
        self.nsem = 0
        self.dtiles = []

    _uid = [0]

    def newsem(self, name):
        self.nsem += 1
        Prog._uid[0] += 1
        return self.nc.alloc_semaphore("%s_%d" % (name, Prog._uid[0]))

    def _esem(self, e, idx):
        lst = self.esems[e]
        while len(lst) <= idx:
            lst.append(self.newsem("es_%s_%d" % (e, len(lst))))
        return lst[idx]

    def _collect(self, e, reads, writes):
        need = {}
        for t in reads:
            for s, v in t.w.items():
                if need.get(s, 0) < v:
                    need[s] = v
        for t in writes:
            for s, v in t.w.items():
                if need.get(s, 0) < v:
                    need[s] = v
            for s, v in t.rs.items():
                if need.get(s, 0) < v:
                    need[s] = v
        waits = []
        wd = self.waited[e]
        own = set(id(s) for s in self.esems[e]) if e == "pe" else ()
        for s, v in need.items():
            if id(s) in own:
                continue
            if wd.get(s, 0) >= v:
                continue
            wd[s] = v
            waits.append((s, v))
        return waits

    def op(self, e, fn, reads=(), writes=()):
        waits = self._collect(e, reads, writes)
        seq = self.cnt[e]
        self.cnt[e] += 1
        sem = self._esem(e, seq // EPOCH)
        val = seq % EPOCH + 1
        self.ops[e].append((waits, fn, sem, 1))
        for t in writes:
            t.w = {sem: val}
            t.rs = {}
        for t in reads:
            if t.rs.get(sem, 0) < val:
                t.rs[sem] = val

    def dma(self, e, out, in_, sb, reads=(), writes=()):
        waits = self._collect(e, reads, writes)
        if sb.dsem is None:
            sb.dsem = self.newsem("ds_" + sb.name)
            self.dtiles.append(sb)
        sb.dcnt += 16
        sem, val = sb.dsem, sb.dcnt
        self.ops[e].append((waits, lambda eng: eng.dma_start(out=out, in_=in_), sem, 16))
        for t in writes:
            if t is sb:
                t.w = {sem: val}
                t.rs = {}
            else:
                t.w[sem] = val
        for t in reads:
            if t.rs.get(sem, 0) < val:
                t.rs[sem] = val

    def barrier(self):
        ev = {}
        for e in self.ENG:
            n = self.cnt[e]
            if n > 0:
                ev[self.esems[e][(n - 1) // EPOCH]] = (n - 1) % EPOCH + 1
        for t in self.dtiles:
            ev[t.dsem] = t.dcnt
        for e in self.ENG:
            wd = self.waited[e]
            own = set(id(x) for x in self.esems[e])
            waits = []
            for sm, v in ev.items():
                if id(sm) in own or wd.get(sm, 0) >= v:
                    continue
                wd[sm] = v
                waits.append((sm, v))
            if waits:
                self.ops[e].append((waits, None, None, 0))

    def emit(self, final_tiles):
        nc = self.nc
        fin = {}
        for t in final_tiles:
            for s, v in list(t.w.items()) + list(t.rs.items()):
                if fin.get(s, 0) < v:
                    fin[s] = v
        with nc.Block() as block:
            def body(e):
                def run(eng):
                    for waits, fn, sem, inc in self.ops[e]:
                        for s, v in waits:
                            eng.wait_ge(s, v)
                        if fn is not None:
                            fn(eng).then_inc(sem, inc)
                    if e == "sp":
                        for s, v in fin.items():
                            eng.wait_ge(s, v)
                return run
            block.tensor(body("pe"))
            block.scalar(body("act"))
            block.vector(body("dve"))
            block.gpsimd(body("pool"))
            block.sync(body("sp"))


def ssl(start, count, step):
    return slice(start, start + (count - 1) * step + 1, step)


def build(debug=None):
    nc = bass.Bass("TRN2", target_bir_lowering=False)
    P = Prog(nc)

    def din(name, shape, dt=F32):
        return nc.dram_tensor(name, list(shape), dt, kind="ExternalInput").ap()

    dbg_outs = {}

    def dscr(name, shape, dt=BF16):
        kind = "ExternalOutput" if (debug and name in debug) else "Internal"
        if debug and ("in_" + name) in debug:
            kind = "ExternalInput"
        t = nc.dram_tensor(name, list(shape), dt, kind=kind).ap()
        if kind == "ExternalOutput":
            dbg_outs[name] = t
        return t

    dbg_tiles = []

    def dump(name, ap, tk, dt):
        if not (debug and "dumps" in debug):
            return
        t = nc.dram_tensor("dbg_" + name, list(ap.shape), dt, kind="ExternalOutput").ap()
        dbg_outs["dbg_" + name] = t
        dt_ = Tk("dbgd_" + name)
        dbg_tiles.append(dt_)
        P.dma("sp", t, ap, tk, reads=[tk], writes=[dt_])

    x = din("x", [L, D])
    w_in = din("w_in", [D, 8192])
    norm_g = din("norm_g", [128, 8])
    ident_in = din("ident", [128, 128])
    cos_in = din("cosT", [128, L])
    sin_in = din("sinT", [128, L])
    mask_in = din("mask01", [128, 256])
    jm_in = din("jm", [128, 128])
    jp_in = din("jp", [128, 128])
    lamr_in = din("lamr_q", [128, 64])
    lami_in = din("lami_q", [128, 64])
    lstep_in = din("lstep_q", [128, 64])
    ex_in = din("ex_tab", [128, 2, 32])
    sg_in = din("sg_tab", [128, 2])
    ba_in = din("ba_q", [128, 64, 16])
    bb_in = din("bb_q", [128, 64, 16])
    ca_in = din("ca_q", [128, 64, 16])
    cb_in = din("cb_q", [128, 64, 16])
    dcol_in = din("dcol", [128, 32])
    cmask_in = din("cmask", [128, 2, 128])
    wa_in = din("w_a", [512, 1024])
    wb_in = din("w_b", [512, 1024])
    wglu_in = din("w_glu", [512, 512])
    wout_in = din("w_out", [1024, 1024])
    fg_in = din("final_g", [1024])
    y = nc.dram_tensor("y", [L, D], F32, kind="ExternalOutput").ap()

    QT = dscr("QT", [2, 24, 32, L])
    KT = dscr("KT", [2, 24, 32, L])
    Vd = dscr("Vd", [L, 1536])
    ZA = dscr("ZA", [512, L])
    Ud = dscr("Ud", [4, 8, 128, 1024])
    ZB = dscr("ZB", [512, L])
    GA = dscr("GA", [1024, L])
    GB = dscr("GB", [1024, L])
    tQT, tKT, tVd, tZA, tUd, tZB, tGA, tGB = [Tk(n) for n in
                                              ("QT", "KT", "Vd", "ZA", "Ud", "ZB", "GA", "GB")]

    def sb(name, shape, dt):
        return nc.alloc_sbuf_tensor(name, list(shape), dt).ap()

    def ps(name, shape, dt=F32):
        return nc.alloc_psum_tensor(name, list(shape), dt).ap()

    ident_f = sb("ident_f", [128, 128], F32)
    ident_b = sb("ident_b", [128, 128], BF16)
    gcol = sb("gcol", [128, 8], F32)
    t_ident_f, t_ident_b, t_gcol = Tk("identf"), Tk("identb"), Tk("gcol")
    def load_consts():
        P.dma("sp", ident_f, ident_in, t_ident_f, writes=[t_ident_f])
        P.dma("sp", gcol, norm_g, t_gcol, writes=[t_gcol])
        P.op("dve", lambda e: e.tensor_copy(out=ident_b, in_=ident_f), reads=[t_ident_f], writes=[t_ident_b])
    load_consts()

    psb = [ps("psb%d" % i, [128, 512]) for i in range(7)]
    tps = [Tk("psb%d" % i) for i in range(7)]
    pst = ps("pst", [128, 1024], BF16)
    tpst = Tk("pst")

    from contextlib import ExitStack

    def sbs(st, name, shape, dt):
        h = st.enter_context(nc.sbuf_tensor(name, list(shape), dt))
        return h.ap() if hasattr(h, "ap") else h[:]

    NTT = L // 128
    NB = L // 512
    w_in_v = w_in.rearrange("(k p) c -> p k c", p=128)
    st12 = ExitStack()
    hT = sbs(st12, "hT", [128, 8, L], BF16)
    thT = [Tk("hT%d" % i) for i in range(NB)]

    with ExitStack() as st:
        xt = [sbs(st, "xt%d" % i, [128, D], F32) for i in range(2)]
        txt = [Tk("xt%d" % i) for i in range(2)]
        xn = [sbs(st, "xn%d" % i, [128, D], BF16) for i in range(2)]
        txn = [Tk("xn%d" % i) for i in range(2)]
        junk = sbs(st, "junk", [128, D], BF16)
        tjunk = Tk("junk")
        ss = [sbs(st, "ss%d" % i, [128, 1], F32) for i in range(2)]
        tss = [Tk("ss%d" % i) for i in range(2)]
        rs_ = [sbs(st, "rs%d" % i, [128, 1], F32) for i in range(2)]
        trs = [Tk("rs%d" % i) for i in range(2)]
        for tt in range(NTT):
            b = tt % 2
            P.dma("sp", xt[b], x[tt * 128:(tt + 1) * 128, :], txt[b], writes=[txt[b]])
            P.op("act", lambda e, b=b: e.activation(out=junk, in_=xt[b], func=AF.Square, accum_out=ss[b]),
                 reads=[txt[b]], writes=[tjunk, tss[b]])
            P.op("dve", lambda e, b=b: e.tensor_scalar(out=rs_[b], in0=ss[b], scalar1=1.0 / D, scalar2=EPS,
                                                       op0=ALU.mult, op1=ALU.add),
                 reads=[tss[b]], writes=[trs[b]])
            P.op("act", lambda e, b=b: e.activation(out=rs_[b], in_=rs_[b], func=AF.Ln),
                 reads=[trs[b]], writes=[trs[b]])
            P.op("act", lambda e, b=b: e.activation(out=rs_[b], in_=rs_[b], func=AF.Exp, scale=-0.5),
                 reads=[trs[b]], writes=[trs[b]])
            P.op("act", lambda e, b=b: e.activation(out=xn[b], in_=xt[b], func=AF.Copy, scale=rs_[b]),
                 reads=[txt[b], trs[b]], writes=[txn[b]])
            for k in range(8):
                P.op("pe", lambda e, b=b, k=k: e.transpose(out=pst[:, k * 128:(k + 1) * 128],
                                                          in_=xn[b][:, k * 128:(k + 1) * 128], identity=ident_b),
                     reads=[txn[b], t_ident_b], writes=[tpst])
            P.op("dve", lambda e, tt=tt: e.tensor_copy(out=hT[:, :, tt * 128:(tt + 1) * 128],
                                                       in_=pst.rearrange("p (k t) -> p k t", k=8)),
                 reads=[tpst], writes=[thT[tt // 4]])
        P.barrier()

    with ExitStack() as st:
        wst = [sbs(st, "wst%d" % i, [128, 8, 256], F32) for i in range(1)]
        twst = [Tk("wst%d" % i) for i in range(1)]
        wcnt = [0]

        def load_w(dst_ap, tdst, c0, n, perm=False):
            for j in range(n // 256):
                b = wcnt[0] % len(wst)
                wcnt[0] += 1
                P.dma("sp", wst[b], w_in_v[:, :, c0 + j * 256:c0 + (j + 1) * 256], twst[b], writes=[twst[b]])
                for k in range(8):
                    if perm:
                        o_ap = dst_ap[:, k, :].rearrange("p (two h i) -> p h two i", two=2, i=32)[:, 4 * j:4 * j + 4]
                        i_ap = wst[b][:, k, :].rearrange("p (h two i) -> p h two i", two=2, i=32)
                    else:
                        o_ap = dst_ap[:, k, j * 256:(j + 1) * 256]
                        i_ap = wst[b][:, k, :]
                    P.op("dve", lambda e, o_ap=o_ap, i_ap=i_ap, k=k: e.tensor_scalar(
                        out=o_ap, in0=i_ap, scalar1=gcol[:, k:k + 1], scalar2=None, op0=ALU.mult),
                        reads=[twst[b], t_gcol], writes=[tdst])

        wbig = sbs(st, "wbig", [128, 8, 1536], BF16)
        twbig = Tk("wbig")
        wg = [sbs(st, "wg%d" % i, [128, 8, 512], BF16) for i in range(2)]
        twg = [Tk("wg%d" % i) for i in range(2)]
        stg = [sbs(st, "stg%d" % i, [128, 512], BF16) for i in range(3)]
        tstg = [Tk("stg%d" % i) for i in range(3)]

        def zug_gen():
            jobs = [(4608, "za", 0), (5632, "zb", 0), (5120, "u", 0), (6144, "ga", 0), (6656, "ga", 512),
                    (7168, "gb", 0), (7680, "gb", 512)]
            ZBK = (4, 5, 6)
            sc = 0
            zc = 0
            for ji, (c0, kind, roff) in enumerate(jobs):
                wb_ = ji % 2
                load_w(wg[wb_], twg[wb_], c0, 512)
                for sub in range(4):
                    for tb in range(NB):
                        pi = ZBK[zc % 3]
                        zc += 1
                        tsl = slice(tb * 512, (tb + 1) * 512)
                        for k in range(8):
                            P.op("pe", lambda e, pi=pi, k=k, wb_=wb_, sub=sub, tsl=tsl: e.matmul(
                                psb[pi], lhsT=wg[wb_][:, k, sub * 128:(sub + 1) * 128], rhs=hT[:, k, tsl],
                                start=(k == 0), stop=(k == 7)),
                                reads=[twg[wb_], thT[tb]], writes=[tps[pi]])
                        s_ = sc % 3
                        sc += 1
                        rows = slice(roff + sub * 128, roff + (sub + 1) * 128)
                        if kind == "u":
                            P.op("act", lambda e, pi=pi, s_=s_: e.activation(
                                out=stg[s_].rearrange("p (t c) -> p t c", t=8),
                                in_=psb[pi].rearrange("p (c t) -> p t c", t=8), func=AF.Copy),
                                reads=[tps[pi]], writes=[tstg[s_]])
                            P.dma("act", Ud[sub, :, :, tb * 64:(tb + 1) * 64].rearrange("t p c -> p t c"),
                                  stg[s_].rearrange("p (t c) -> p t c", t=8),
                                  tstg[s_], reads=[tstg[s_]], writes=[tUd])
                        else:
                            fn = AF.Silu if kind in ("za", "zb") else AF.Sigmoid
                            dd, td = {"za": (ZA, tZA), "zb": (ZB, tZB), "ga": (GA, tGA), "gb": (GB, tGB)}[kind]
                            P.op("act", lambda e, pi=pi, s_=s_, fn=fn: e.activation(out=stg[s_], in_=psb[pi], func=fn),
                                 reads=[tps[pi]], writes=[tstg[s_]])
                            P.dma("act", dd[rows, tsl], stg[s_], tstg[s_], reads=[tstg[s_]], writes=[td])
                        yield

        zg = zug_gen()
        st2a = ExitStack()
        cs = [sbs(st2a, "cs%d" % i, [128, 2, 512], F32) for i in range(2)]
        tcs = [Tk("cs%d" % i) for i in range(2)]
        rtmp = [sbs(st2a, "rtmp%d" % i, [128, 512], F32) for i in range(4)]
        trtmp = [Tk("rtmp%d" % i) for i in range(4)]
        ro = [sbs(st2a, "ro%d" % i, [128, 2, 512], BF16) for i in range(3)]
        tro = [Tk("ro%d" % i) for i in range(3)]
        rcnt = 0
        ccnt = 0
        for qk in range(2):
            load_w(wbig, twbig, qk * 1536, 1536, perm=True)
            dst, tdst = (QT, tQT) if qk == 0 else (KT, tKT)
            for tb in range(NB):
                cb = ccnt % 2
                ccnt += 1
                tsl = slice(tb * 512, (tb + 1) * 512)
                P.dma("sp", cs[cb][:, 0, :], cos_in[:, tsl], tcs[cb], writes=[tcs[cb]])
                P.dma("sp", cs[cb][:, 1, :], sin_in[:, tsl], tcs[cb], writes=[tcs[cb]])
                for ht in range(6):
                    ia = 2 * (ht % 2)
                    pa, pb = psb[ia], psb[ia + 1]
                    ta, tb_ = tps[ia], tps[ia + 1]
                    for half, (pp, tp) in enumerate(((pa, ta), (pb, tb_))):
                        for k in range(8):
                            P.op("pe", lambda e, pp=pp, k=k, ht=ht, half=half, tsl=tsl: e.matmul(
                                pp, lhsT=wbig[:, k, half * 768 + ht * 128:half * 768 + (ht + 1) * 128], rhs=hT[:, k, tsl],
                                start=(k == 0), stop=(k == 7)),
                                reads=[twbig, thT[tb]], writes=[tp])
                    r = rcnt % 3
                    rcnt += 1
                    C, S = cs[cb][:, 0, :], cs[cb][:, 1, :]
                    P.op("dve", lambda e, pa=pa, C=C: e.tensor_tensor(out=rtmp[0], in0=pa, in1=C, op=ALU.mult),
                         reads=[ta, tcs[cb]], writes=[trtmp[0]])
                    P.op("dve", lambda e, pb=pb, S=S: e.tensor_tensor(out=rtmp[1], in0=pb, in1=S, op=ALU.mult),
                         reads=[tb_, tcs[cb]], writes=[trtmp[1]])
                    P.op("dve", lambda e, r=r: e.tensor_tensor(out=ro[r][:, 0, :], in0=rtmp[0], in1=rtmp[1],
                                                               op=ALU.subtract),
                         reads=[trtmp[0], trtmp[1]], writes=[tro[r]])
                    P.op("dve", lambda e, pb=pb, C=C: e.tensor_tensor(out=rtmp[2], in0=pb, in1=C, op=ALU.mult),
                         reads=[tb_, tcs[cb]], writes=[trtmp[2]])
                    P.op("dve", lambda e, pa=pa, S=S: e.tensor_tensor(out=rtmp[3], in0=pa, in1=S, op=ALU.mult),
                         reads=[ta, tcs[cb]], writes=[trtmp[3]])
                    P.op("dve", lambda e, r=r: e.tensor_tensor(out=ro[r][:, 1, :], in0=rtmp[2], in1=rtmp[3],
                                                               op=ALU.add),
                         reads=[trtmp[2], trtmp[3]], writes=[tro[r]])
                    h0 = ht * 4
                    for half in range(2):
                        P.dma("sp", dst[half, h0:h0 + 4, :, tsl].rearrange("h i t -> (h i) t"),
                              ro[r][:, half, :], tro[r], reads=[tro[r]], writes=[tdst])
                    for _ in range(3):
                        next(zg, None)
        for _ in zg:
            pass
        P.barrier()
        st2a.close()
        load_w(wbig, twbig, 3072, 1536)
        vst = [sbs(st, "vst%d" % i, [128, 1536], BF16) for i in range(2)]
        tvst = [Tk("vst%d" % i) for i in range(2)]
        pc = 0
        for tt in range(NTT):
            vb = tt % 2
            for cb3 in range(3):
                pi = pc % 6
                pc += 1
                for k in range(8):
                    P.op("pe", lambda e, pi=pi, k=k, tt=tt, cb3=cb3: e.matmul(
                        psb[pi], lhsT=hT[:, k, tt * 128:(tt + 1) * 128], rhs=wbig[:, k, cb3 * 512:(cb3 + 1) * 512],
                        start=(k == 0), stop=(k == 7)),
                        reads=[twbig, thT[tt // 4]], writes=[tps[pi]])
                P.op("act", lambda e, pi=pi, vb=vb, cb3=cb3: e.activation(
                    out=vst[vb][:, cb3 * 512:(cb3 + 1) * 512], in_=psb[pi], func=AF.Copy),
                    reads=[tps[pi]], writes=[tvst[vb]])
            P.dma("act", Vd[tt * 128:(tt + 1) * 128, :], vst[vb], tvst[vb], reads=[tvst[vb]], writes=[tVd])
        P.barrier()
    st12.close()
    if debug and "stop2" in debug:
        P.emit([tQT, tKT, tVd, tZA, tUd, tZB, tGA, tGB])
        return nc, dbg_outs

    ATT = dscr("ATT", [512, L])
    tATT = Tk("ATT")
    PADK = 1024
    with ExitStack() as st:
        QTh = [sbs(st, "QTh%d" % i, [64, L], BF16) for i in range(2)]
        tQTh = [Tk("QTh%d" % i) for i in range(2)]
        KTh = [sbs(st, "KTh%d" % i, [64, L + 2 * PADK], BF16) for i in range(2)]
        tKTh = [Tk("KTh%d" % i) for i in range(2)]
        NVT = 80
        Vt = [sbs(st, "Vt%d" % i, [128, NVT, 64], BF16) for i in range(2)]
        tVt = [Tk("Vt%d" % i) for i in range(2)]
        ACC = sbs(st, "ACC", [64, 2, L], F32)
        tACC = Tk("ACC")
        pT = [sbs(st, "pT%d" % i, [128, 256], BF16) for i in range(3)]
        tpT = [Tk("pT%d" % i) for i in range(3)]
        onesk = sbs(st, "onesk", [128, 3, 64], BF16)
        tonesk = Tk("onesk")
        mask_f = sbs(st, "mask_f", [128, 256], F32)
        mask_b = sbs(st, "mask_b", [128, 256], BF16)
        tmask_f, tmask_b = Tk("maskf"), Tk("maskb")
        za_t = sbs(st, "za_t", [64, 2048], BF16)
        tza = Tk("za_t")
        dv = sbs(st, "dv", [64, 2048], F32)
        tdv = Tk("dv")
        ao = [sbs(st, "ao%d" % i, [64, 2048], BF16) for i in range(2)]
        tao = [Tk("ao%d" % i) for i in range(2)]

        P.dma("sp", mask_f, mask_in, tmask_f, writes=[tmask_f])
        P.op("dve", lambda e: e.tensor_copy(out=mask_b, in_=mask_f), reads=[tmask_f], writes=[tmask_b])
        P.op("dve", lambda e: e.memset(onesk, 1.0), writes=[tonesk])
        P.op("dve", lambda e: e.memset(onesk[0:64, 1, :], 0.0), writes=[tonesk])
        P.op("dve", lambda e: e.memset(onesk[64:128, 2, :], 0.0), writes=[tonesk])
        for i in range(2):
            P.op("dve", lambda e, i=i: e.memset(KTh[i][:, 0:PADK], 0.0), writes=[tKTh[i]])
            P.op("dve", lambda e, i=i: e.memset(KTh[i][:, PADK + L:], 0.0), writes=[tKTh[i]])

        DIL = (1, 4, 16)
        slot = 0
        sidx = 0
        oidx = 0
        pidx = 0
        def p3_setup(h, g, sl):
            d = DIL[g]
            hg = g * 8 + h
            Ls = L // d
            nt = Ls // 128 + 1
            for half in range(2):
                P.dma("sp", QTh[sl][half * 32:(half + 1) * 32, :], QT[half, hg, :, :], tQTh[sl],
                      writes=[tQTh[sl]])
                P.dma("sp", KTh[sl][half * 32:(half + 1) * 32, PADK:PADK + L], KT[half, hg, :, :], tKTh[sl],
                      writes=[tKTh[sl]])
            P.op("dve", lambda e, sl=sl: e.memset(Vt[sl], 0.0), writes=[tVt[sl]])
            vcols = slice(hg * 64, (hg + 1) * 64)
            for r in range(d):
                base = r * nt
                P.dma("sp", Vt[sl][64:128, base, :], Vd[ssl(r, 64, d), vcols], tVt[sl], reads=[tVd],
                      writes=[tVt[sl]])
                j0 = 64
                nmid = nt - 2
                srcv = Vd[ssl(r + d * j0, 128 * nmid, d), vcols].rearrange("(tau p) c -> p tau c", p=128)
                P.dma("sp", Vt[sl][:, base + 1:base + 1 + nmid, :], srcv, tVt[sl], reads=[tVd],
                      writes=[tVt[sl]])
                jl = Ls - 64
                P.dma("sp", Vt[sl][0:64, base + nt - 1, :], Vd[ssl(r + d * jl, 64, d), vcols], tVt[sl],
                      reads=[tVd], writes=[tVt[sl]])

        jobs3 = [(h_, g_) for h_ in range(8) for g_ in range(3)]
        p3_setup(jobs3[0][0], jobs3[0][1], 0)
        for h in range(8):
            for g in range(3):
                d = DIL[g]
                hg = g * 8 + h
                Ls = L // d
                nqb = Ls // 128
                nt = nqb + 1
                sl = slot % 2
                slot += 1
                if slot < len(jobs3):
                    p3_setup(jobs3[slot][0], jobs3[slot][1], slot % 2)
                SBK = (0, 1, 6)
                tl = [(r, tau) for r in range(d) for tau in range(nt)]
                info = {}
                ocur = {}

                def stS(ix):
                    r, tau = tl[ix]
                    qb_lo = max(tau - 1, 0)
                    qb_hi = min(tau, nqb - 1)
                    nq = (qb_hi - qb_lo + 1) * 128
                    mcol0 = (qb_lo - (tau - 1)) * 128
                    kstart = PADK + r + d * (128 * tau - 64)
                    kap = KTh[sl][:, ssl(kstart, 128, d)]
                    qstart = r + d * 128 * qb_lo
                    qap = QTh[sl][:, ssl(qstart, nq, d)]
                    sp_, tsp = psb[SBK[ix % 3]], tps[SBK[ix % 3]]
                    P.op("pe", lambda e, sp_=sp_, kap=kap, qap=qap, nq=nq: e.matmul(
                        sp_[:, 0:nq], lhsT=kap, rhs=qap, start=True, stop=True),
                        reads=[tKTh[sl], tQTh[sl]], writes=[tsp])
                    info[ix] = (r, tau, qb_lo, qb_hi, nq, mcol0, sp_, tsp)

                def stE(ix):
                    r, tau, qb_lo, qb_hi, nq, mcol0, sp_, tsp = info[ix]
                    pi_ = ix % 3
                    P.op("act", lambda e, sp_=sp_, pi_=pi_, nq=nq: e.activation(
                        out=pT[pi_][:, 0:nq], in_=sp_[:, 0:nq], func=AF.Exp, scale=0.125),
                        reads=[tsp], writes=[tpT[pi_]])
                    P.op("dve", lambda e, pi_=pi_, nq=nq, mcol0=mcol0: e.tensor_tensor(
                        out=pT[pi_][:, 0:nq], in0=pT[pi_][:, 0:nq], in1=mask_b[:, mcol0:mcol0 + nq],
                        op=ALU.mult),
                        reads=[tpT[pi_], tmask_b], writes=[tpT[pi_]])

                def stPV(ix, oidx_box):
                    r, tau, qb_lo, qb_hi, nq, mcol0, sp_, tsp = info.pop(ix)
                    pi_ = ix % 3
                    ok = 1 if tau == 0 else (2 if tau == nt - 1 else 0)
                    for qb in range(qb_lo, qb_hi + 1):
                        first = (qb == tau)
                        if first:
                            ocur[(r, qb)] = 2 + (oidx_box[0] % 4)
                            oidx_box[0] += 1
                        oi = ocur[(r, qb)]
                        op_ = psb[oi].rearrange("p (two q) -> p two q", two=2)
                        c0 = (qb - qb_lo) * 128
                        P.op("pe", lambda e, op_=op_, pi_=pi_, c0=c0, tile=r * nt + tau, first=first, sl=sl: e.matmul(
                            op_[0:64, 0, 0:128], lhsT=Vt[sl][:, tile, :], rhs=pT[pi_][:, c0:c0 + 128],
                            start=first, stop=False, skip_group_check=True),
                            reads=[tVt[sl], tpT[pi_]], writes=[tps[oi]])
                        P.op("pe", lambda e, op_=op_, pi_=pi_, c0=c0, ok=ok, first=first: e.matmul(
                            op_[0:64, 1, 0:128], lhsT=onesk[:, ok, :], rhs=pT[pi_][:, c0:c0 + 128],
                            start=False, stop=(not first), skip_group_check=True),
                            reads=[tonesk, tpT[pi_]], writes=[tps[oi]])
                        if not first:
                            t0 = r + d * 128 * qb
                            acc_ap = ACC[:, :, ssl(t0, 128, d)]
                            src = op_[0:64, :, 0:128]
                            if g == 0:
                                P.op("dve", lambda e, acc_ap=acc_ap, src=src: e.tensor_copy(out=acc_ap, in_=src),
                                     reads=[tps[oi]], writes=[tACC])
                            else:
                                P.op("dve", lambda e, acc_ap=acc_ap, src=src: e.tensor_tensor(
                                    out=acc_ap, in0=src, in1=acc_ap, op=ALU.add),
                                    reads=[tps[oi], tACC], writes=[tACC])

                n_t = len(tl)
                obox = [oidx]
                for i0 in range(min(3, n_t)):
                    stS(i0)
                stE(0)
                for ix in range(n_t):
                    if ix + 1 < n_t:
                        stE(ix + 1)
                    if ix + 3 < n_t:
                        stS(ix + 3)
                    stPV(ix, obox)
                oidx = obox[0]
            for c4 in range(4):
                csl = slice(c4 * 2048, (c4 + 1) * 2048)
                a_ = (h * 4 + c4) % 2
                P.dma("sp", za_t, ZA[h * 64:(h + 1) * 64, csl], tza, reads=[tZA], writes=[tza])
                P.op("act", lambda e, csl=csl: e.activation(out=dv, in_=ACC[:, 1, csl], func=AF.Ln), reads=[tACC],
                     writes=[tdv])
                P.op("act", lambda e: e.activation(out=dv, in_=dv, func=AF.Exp, scale=-1.0), reads=[tdv], writes=[tdv])
                P.op("dve", lambda e, csl=csl: e.tensor_tensor(out=dv, in0=dv, in1=ACC[:, 0, csl], op=ALU.mult),
                     reads=[tACC, tdv], writes=[tdv])
                P.op("dve", lambda e, a_=a_: e.tensor_tensor(out=ao[a_], in0=dv, in1=za_t, op=ALU.mult),
                     reads=[tdv, tza], writes=[tao[a_]])
                P.dma("act", ATT[h * 64:(h + 1) * 64, csl], ao[a_], tao[a_], reads=[tao[a_]], writes=[tATT])
        P.barrier()
    if debug and "stop3" in debug:
        P.emit([tATT, tQT, tKT, tVd, tZA, tUd, tZB, tGA, tGB])
        return nc, dbg_outs

    if debug and "only4" in debug:
        P = Prog(nc)
        for t_ in (t_ident_f, t_ident_b, t_gcol, tUd, tATT) + tuple(tps) + (tpst,):
            t_.w, t_.rs, t_.dsem, t_.dcnt = {}, {}, None, 0
        load_consts()
    Yd = dscr("Yd", [32, 128, 1024])
    tYd = Tk("Yd")
    PI = float(np.pi)
    stM = ExitStack()
    MATS = sbs(stM, "MATS", [128, 64, 4, 128], BF16)
    tMATS = [Tk("MATS%d" % i) for i in range(64)]
    a64 = sbs(stM, "a64", [128, 64], F32)
    b64 = sbs(stM, "b64", [128, 64], F32)
    nb64 = sbs(stM, "nb64", [128, 64], F32)
    tab64 = Tk("ab64")
    Jm_f = sbs(stM, "Jm_f", [128, 128], F32)
    tJm = Tk("Jm")
    P.dma("sp", Jm_f, jm_in, tJm, writes=[tJm])
    with ExitStack() as st:
        def small(name, shape, dt=F32):
            return sbs(st, "s4_" + name, shape, dt), Tk(name)
        lamr, tlamr = small("lamr", [128, 64])
        lami, tlami = small("lami", [128, 64])
        stp, tstp = small("stp", [128, 64])
        ar, tar = small("ar", [128, 64])
        ai, tai = small("ai", [128, 64])
        EX, tEX = small("EX", [128, 2, 32])
        sg, tsg = small("sg", [128, 2])
        BA, tBA = small("BA", [128, 64, 16])
        BB, tBB = small("BB", [128, 64, 16])
        CA, tCA = small("CA", [128, 64, 16])
        CB, tCB = small("CB", [128, 64, 16])
        dcol, tdcol = small("dcol", [128, 32])
        Jp, tJp = small("Jp", [128, 128])
        cmask, tcmask = small("cmask", [128, 2, 128])
        for dst, t_, src in ((lamr, tlamr, lamr_in), (lami, tlami, lami_in), (stp, tstp, lstep_in), (EX, tEX, ex_in),
                             (sg, tsg, sg_in), (BA, tBA, ba_in), (BB, tBB, bb_in), (CA, tCA, ca_in), (CB, tCB, cb_in),
                             (dcol, tdcol, dcol_in), (Jp, tJp, jp_in), (cmask, tcmask, cmask_in)):
            P.dma("sp", dst, src, t_, writes=[t_])
        P.op("act", lambda e: e.activation(out=stp, in_=stp, func=AF.Exp), reads=[tstp], writes=[tstp])
        P.op("dve", lambda e: e.tensor_tensor(out=ar, in0=lamr, in1=stp, op=ALU.mult), reads=[tlamr, tstp], writes=[tar])
        P.op("dve", lambda e: e.tensor_tensor(out=ai, in0=lami, in1=stp, op=ALU.mult), reads=[tlami, tstp], writes=[tai])
        ang, tang = small("ang", [128, 2, 32, 32])
        ang2, tang2 = small("ang2", [128, 2, 32, 32])
        mgl, tmgl = small("mgl", [128, 2, 32, 32])
        mc, tmc = small("mc", [128, 2, 32, 32])
        ms, tms = small("ms", [128, 2, 32, 32])
        mc1, tmc1 = small("mc1", [128, 2, 32, 32])
        ms2, tms2 = small("ms2", [128, 2, 32, 32])
        for r in range(2):
            aib = ai[:, r * 32:(r + 1) * 32].unsqueeze(2).to_broadcast([128, 32, 32])
            arb = ar[:, r * 32:(r + 1) * 32].unsqueeze(2).to_broadcast([128, 32, 32])
            exb = EX[:, r, :].unsqueeze(1).to_broadcast([128, 32, 32])
            P.op("dve", lambda e, r=r, aib=aib, exb=exb: e.tensor_tensor(out=ang[:, r], in0=aib, in1=exb, op=ALU.mult),
                 reads=[tai, tEX], writes=[tang])
            P.op("dve", lambda e, r=r, arb=arb, exb=exb: e.tensor_tensor(out=mgl[:, r], in0=arb, in1=exb, op=ALU.mult),
                 reads=[tar, tEX], writes=[tmgl])
        angf = ang.rearrange("p a b c -> p (a b c)")
        ang2f = ang2.rearrange("p a b c -> p (a b c)")
        mglf = mgl.rearrange("p a b c -> p (a b c)")
        mcf = mc.rearrange("p a b c -> p (a b c)")
        msf = ms.rearrange("p a b c -> p (a b c)")
        mc1f = mc1.rearrange("p a b c -> p (a b c)")
        ms2f = ms2.rearrange("p a b c -> p (a b c)")
        OFF = 64.0 * PI
        INV2PI = 1.0 / (2 * PI)
        kint, tkint = small("kint", [128, 2048], mybir.dt.int32)
        kf, tkf = small("kf", [128, 2048])
        P.op("dve", lambda e: e.tensor_scalar(out=ang2f, in0=angf, scalar1=OFF + PI / 2, scalar2=INV2PI, op0=ALU.add,
                                              op1=ALU.mult), reads=[tang], writes=[tang2])
        P.op("dve", lambda e: e.tensor_scalar(out=angf, in0=angf, scalar1=OFF, scalar2=INV2PI, op0=ALU.add,
                                              op1=ALU.mult), reads=[tang], writes=[tang])
        for af_, taf_ in ((ang2f, tang2), (angf, tang)):
            P.op("dve", lambda e, af_=af_: e.tensor_copy(out=kint, in_=af_), reads=[taf_], writes=[tkint])
            P.op("dve", lambda e: e.tensor_copy(out=kf, in_=kint), reads=[tkint], writes=[tkf])
            P.op("dve", lambda e, af_=af_: e.tensor_tensor(out=af_, in0=af_, in1=kf, op=ALU.subtract), reads=[taf_, tkf],
                 writes=[taf_])
            P.op("dve", lambda e, af_=af_: e.tensor_scalar(out=kf, in0=af_, scalar1=0.5, scalar2=None, op0=ALU.is_gt),
                 reads=[taf_], writes=[tkf])
            P.op("dve", lambda e, af_=af_: e.tensor_tensor(out=af_, in0=af_, in1=kf, op=ALU.subtract), reads=[taf_, tkf],
                 writes=[taf_])
            P.op("dve", lambda e, af_=af_: e.tensor_scalar(out=kf, in0=af_, scalar1=-0.5, scalar2=None, op0=ALU.is_lt),
                 reads=[taf_], writes=[tkf])
            P.op("dve", lambda e, af_=af_: e.tensor_tensor(out=af_, in0=af_, in1=kf, op=ALU.add), reads=[taf_, tkf],
                 writes=[taf_])
        P.op("act", lambda e: e.activation(out=ang2f, in_=ang2f, func=AF.Sin, scale=2 * PI), reads=[tang2], writes=[tang2])
        P.op("act", lambda e: e.activation(out=angf, in_=angf, func=AF.Sin, scale=2 * PI), reads=[tang], writes=[tang])
        P.op("act", lambda e: e.activation(out=mglf, in_=mglf, func=AF.Exp), reads=[tmgl], writes=[tmgl])
        P.op("dve", lambda e: e.tensor_tensor(out=mcf, in0=mglf, in1=ang2f, op=ALU.mult), reads=[tmgl, tang2], writes=[tmc])
        P.op("dve", lambda e: e.tensor_tensor(out=msf, in0=mglf, in1=angf, op=ALU.mult), reads=[tmgl, tang], writes=[tms])
        P.op("dve", lambda e: e.tensor_scalar(out=mc1f, in0=mcf, scalar1=sg[:, 0:1], scalar2=None, op0=ALU.mult),
             reads=[tmc, tsg], writes=[tmc1])
        P.op("dve", lambda e: e.tensor_scalar(out=ms2f, in0=msf, scalar1=sg[:, 1:2], scalar2=None, op0=ALU.mult),
             reads=[tms, tsg], writes=[tms2])
        def pw(tab, j):
            return tab[:, :, :, 24 + j]
        l1r, tl1r = small("l1r", [128, 2, 32])
        nr_, tnr = small("nr_", [128, 2, 32])
        den, tden = small("den", [128, 2, 32])
        tmpa, ttmpa = small("tmpa", [128, 2, 32])
        tmpb, ttmpb = small("tmpb", [128, 2, 32])
        wr, twr = small("wr", [128, 2, 32])
        wi, twi = small("wi", [128, 2, 32])
        wi1, twi1 = small("wi1", [128, 2, 32])
        wi2, twi2 = small("wi2", [128, 2, 32])
        lr3 = lamr.rearrange("p (r g) -> p r g", r=2)
        li3 = lami.rearrange("p (r g) -> p r g", r=2)
        P.op("dve", lambda e: e.tensor_scalar(out=nr_, in0=pw(mc, 0), scalar1=-1.0, scalar2=None, op0=ALU.add),
             reads=[tmc], writes=[tnr])
        P.op("dve", lambda e: e.tensor_tensor(out=den, in0=lr3, in1=lr3, op=ALU.mult), reads=[tlamr], writes=[tden])
        P.op("dve", lambda e: e.tensor_tensor(out=tmpa, in0=li3, in1=li3, op=ALU.mult), reads=[tlami], writes=[ttmpa])
        P.op("dve", lambda e: e.tensor_tensor(out=den, in0=den, in1=tmpa, op=ALU.add), reads=[tden, ttmpa], writes=[tden])
        P.op("dve", lambda e: e.tensor_tensor(out=tmpa, in0=nr_, in1=lr3, op=ALU.mult), reads=[tnr, tlamr], writes=[ttmpa])
        P.op("dve", lambda e: e.tensor_tensor(out=tmpb, in0=pw(ms, 0), in1=li3, op=ALU.mult), reads=[tms, tlami],
             writes=[ttmpb])
        P.op("dve", lambda e: e.tensor_tensor(out=tmpa, in0=tmpa, in1=tmpb, op=ALU.add), reads=[ttmpa, ttmpb],
             writes=[ttmpa])
        P.op("dve", lambda e: e.reciprocal(out=den, in_=den), reads=[tden], writes=[tden])
        P.op("dve", lambda e: e.tensor_tensor(out=wr, in0=tmpa, in1=den, op=ALU.mult), reads=[ttmpa, tden], writes=[twr])
        P.op("dve", lambda e: e.tensor_tensor(out=tmpa, in0=pw(ms, 0), in1=lr3, op=ALU.mult), reads=[tms, tlamr],
             writes=[ttmpa])
        P.op("dve", lambda e: e.tensor_tensor(out=tmpb, in0=nr_, in1=li3, op=ALU.mult), reads=[tnr, tlami], writes=[ttmpb])
        P.op("dve", lambda e: e.tensor_tensor(out=tmpa, in0=tmpa, in1=tmpb, op=ALU.subtract), reads=[ttmpa, ttmpb],
             writes=[ttmpa])
        P.op("dve", lambda e: e.tensor_tensor(out=wi, in0=tmpa, in1=den, op=ALU.mult), reads=[ttmpa, tden], writes=[twi])
        P.op("dve", lambda e: e.tensor_scalar(out=wi1, in0=wi, scalar1=sg[:, 0:1], scalar2=None, op0=ALU.mult),
             reads=[twi, tsg], writes=[twi1])
        P.op("dve", lambda e: e.tensor_scalar(out=wi2, in0=wi, scalar1=sg[:, 1:2], scalar2=None, op0=ALU.mult),
             reads=[twi, tsg], writes=[twi2])
        bA, tbA = small("bA", [128, 64, 16])
        bB, tbB = small("bB", [128, 64, 16])
        tmp16, ttmp16 = small("tmp16", [128, 64, 16])

        def bc16(t):
            return t.rearrange("p r g -> p (r g)").unsqueeze(2).to_broadcast([128, 64, 16])
        P.op("dve", lambda e: e.tensor_tensor(out=bA, in0=BA, in1=bc16(wr), op=ALU.mult), reads=[tBA, twr], writes=[tbA])
        P.op("dve", lambda e: e.tensor_tensor(out=tmp16, in0=BB, in1=bc16(wi2), op=ALU.mult), reads=[tBB, twi2],
             writes=[ttmp16])
        P.op("dve", lambda e: e.tensor_tensor(out=bA, in0=bA, in1=tmp16, op=ALU.add), reads=[tbA, ttmp16], writes=[tbA])
        P.op("dve", lambda e: e.tensor_tensor(out=bB, in0=BB, in1=bc16(wr), op=ALU.mult), reads=[tBB, twr], writes=[tbB])
        P.op("dve", lambda e: e.tensor_tensor(out=tmp16, in0=BA, in1=bc16(wi1), op=ALU.mult), reads=[tBA, twi1],
             writes=[ttmp16])
        P.op("dve", lambda e: e.tensor_tensor(out=bB, in0=bB, in1=tmp16, op=ALU.add), reads=[tbB, ttmp16], writes=[tbB])
        P.op("dve", lambda e: e.tensor_copy(out=a64.rearrange("p (r g) -> p r g", r=2), in_=pw(mc, 2)), reads=[tmc],
             writes=[tab64])
        P.op("dve", lambda e: e.tensor_copy(out=b64.rearrange("p (r g) -> p r g", r=2), in_=pw(ms, 2)), reads=[tms],
             writes=[tab64])
        P.op("dve", lambda e: e.tensor_scalar(out=nb64, in0=b64, scalar1=-1.0, scalar2=None, op0=ALU.mult),
             reads=[tab64], writes=[tab64])
        l8a, tl8a = small("l8a", [128, 2, 32])
        l8b, tl8b = small("l8b", [128, 2, 32])
        P.op("dve", lambda e: e.tensor_copy(out=l8a, in_=pw(mc, 1)), reads=[tmc], writes=[tl8a])
        P.op("dve", lambda e: e.tensor_copy(out=l8b, in_=pw(mc1, 1)), reads=[tmc1], writes=[tl8b])
        P.op("dve", lambda e: e.tensor_scalar(out=l8b, in0=pw(ms, 1), scalar1=sg[:, 0:1], scalar2=None, op0=ALU.mult),
             reads=[tms, tsg], writes=[tl8b])
        Pt, tPt = small("Pt", [128, 8, 8, 16])
        Qt, tQt = small("Qt", [128, 8, 8, 16])
        M4t, tM4t = small("M4t", [128, 8, 8, 16])
        tq, ttq = small("tq", [128, 8, 8, 16])
        m1tmp, tm1tmp = small("m1tmp", [128, 128])
        ddiag, tddiag = small("ddiag", [128, 128])
        l8tmp, tl8tmp = small("l8tmp", [128, 128])
        pcn = 0
        for r in range(2):
            for gb in range(4):
                gsl = slice(gb * 8, (gb + 1) * 8)
                gdsl = slice(r * 32 + gb * 8, r * 32 + (gb + 1) * 8)

                def wtab(tab, w):
                    return tab[:, r, gsl, w * 8:(w + 1) * 8].unsqueeze(3).to_broadcast([128, 8, 8, 16])

                def ctab(tab):
                    return tab[:, gdsl, :].unsqueeze(2).to_broadcast([128, 8, 8, 16])
                for (dst, tdst, w, A, tA, Bm, tB, kind) in ((Pt, tPt, 0, bA, tbA, bB, tbB, "p"),
                                                            (Qt, tQt, 1, CA, tCA, CB, tCB, "q"),
                                                            (M4t, tM4t, 2, CA, tCA, CB, tCB, "q")):
                    if kind == "p":
                        m_a, tm_a, m_b, tm_b, op2 = mc, tmc, ms2, tms2, ALU.add
                    else:
                        m_a, tm_a, m_b, tm_b, op2 = mc1, tmc1, ms, tms, ALU.subtract
                    i0a, i1a = wtab(m_a, w), ctab(A)
                    i0b, i1b = wtab(m_b, w), ctab(Bm)
                    P.op("dve", lambda e, dst=dst, i0a=i0a, i1a=i1a: e.tensor_tensor(
                        out=dst, in0=i0a, in1=i1a, op=ALU.mult), reads=[tm_a, tA], writes=[tdst])
                    P.op("dve", lambda e, i0b=i0b, i1b=i1b: e.tensor_tensor(
                        out=tq, in0=i0b, in1=i1b, op=ALU.mult), reads=[tm_b, tB], writes=[ttq])
                    P.op("dve", lambda e, dst=dst, op2=op2: e.tensor_tensor(out=dst, in0=dst, in1=tq, op=op2),
                         reads=[tdst, ttq], writes=[tdst])
                for gi in range(8):
                    g = gb * 8 + gi
                    gd = r * 32 + g
                    Pg = Pt[:, gi].rearrange("p s c -> p (s c)")
                    Qg = Qt[:, gi].rearrange("p s c -> p (s c)")
                    M4g = M4t[:, gi].rearrange("p s c -> p (s c)")
                    pa_ = psb[pcn % 6]
                    tpa_ = tps[pcn % 6]
                    pcn += 1
                    P.op("pe", lambda e, pa_=pa_, Pg=Pg, Qg=Qg: e.matmul(pa_[:, 0:128], lhsT=Pg, rhs=Qg, start=True,
                                                                        stop=True),
                         reads=[tPt, tQt], writes=[tpa_])
                    P.op("dve", lambda e, pa_=pa_, r=r: e.tensor_tensor(out=m1tmp, in0=pa_[:, 0:128], in1=cmask[:, r, :],
                                                                       op=ALU.mult),
                         reads=[tpa_, tcmask], writes=[tm1tmp])
                    if r == 0:
                        P.op("dve", lambda e, g=g, gd=gd: e.scalar_tensor_tensor(
                            out=MATS[:, gd, 1, :], in0=ident_f, scalar=dcol[:, g:g + 1], in1=m1tmp, op0=ALU.mult,
                            op1=ALU.add), reads=[t_ident_f, tdcol, tm1tmp], writes=[tMATS[gd]])
                    else:
                        P.op("dve", lambda e, gd=gd: e.tensor_copy(out=MATS[:, gd, 1, :], in_=m1tmp),
                             reads=[tm1tmp], writes=[tMATS[gd]])
                    pb_ = psb[pcn % 6]
                    tpb_ = tps[pcn % 6]
                    pcn += 1
                    P.op("pe", lambda e, pb_=pb_, Pg=Pg: e.transpose(out=pb_[:, 0:128], in_=Pg, identity=ident_f),
                         reads=[tPt, t_ident_f], writes=[tpb_])
                    P.op("act", lambda e, pb_=pb_, gd=gd: e.activation(out=MATS[:, gd, 0, :], in_=pb_[:, 0:128],
                                                                      func=AF.Copy),
                         reads=[tpb_], writes=[tMATS[gd]])
                    P.op("act", lambda e, M4g=M4g, gd=gd: e.activation(out=MATS[:, gd, 2, :], in_=M4g, func=AF.Copy),
                         reads=[tM4t], writes=[tMATS[gd]])
                    P.op("dve", lambda e, r=r, g=g: e.tensor_scalar(out=l8tmp, in0=Jp, scalar1=l8b[:, r, g:g + 1],
                                                                   scalar2=None, op0=ALU.mult),
                         reads=[tJp, tl8b], writes=[tl8tmp])
                    P.op("dve", lambda e, r=r, g=g, gd=gd: e.scalar_tensor_tensor(
                        out=MATS[:, gd, 3, :], in0=ident_f, scalar=l8a[:, r, g:g + 1], in1=l8tmp, op0=ALU.mult,
                        op1=ALU.add), reads=[t_ident_f, tl8a, tl8tmp], writes=[tMATS[gd]])
        dump("mc", mc, tmc, F32)
        dump("ms", ms, tms, F32)
        dump("bA", bA, tbA, F32)
        dump("wr", wr, twr, F32)
        dump("wi", wi, twi, F32)
        dump("Pt", Pt, tPt, F32)
        dump("Qt", Qt, tQt, F32)
        tall = Tk("matsall")
        P.barrier()
        for i8 in range(8):
            dump("MATS%d" % i8, MATS[:, i8 * 8:(i8 + 1) * 8], tall, BF16)
        P.barrier()

    with ExitStack() as st:
        Ug = sbs(st, "Ug", [128, 32, 1024], BF16)
        tUg = [Tk("Ug%d" % i) for i in range(32)]
        for sub in range(4):
            for t8 in range(8):
                P.dma("sp", Ug[t8 * 16:(t8 + 1) * 16, sub * 8:(sub + 1) * 8, :],
                      Ud[sub, t8, :, :].rearrange("(g ci) c -> ci g c", ci=16), tUg[sub * 8],
                      reads=[tUd], writes=[tUg[sub * 8 + gg] for gg in range(8)])
        SS = sbs(st, "SS", [128, 2, 32, 128], F32)
        tSS = Tk("SS")
        G0 = sbs(st, "G0", [128, 2, 32, 128], BF16)
        tG0 = [Tk("G0_0"), Tk("G0_1")]
        hq = [sbs(st, "hq%d" % i, [128, 4, 128], BF16) for i in range(2)]
        thq = [Tk("hq%d" % i) for i in range(2)]
        t1b = sbs(st, "t1b", [128, 2, 32], F32)
        t2b = sbs(st, "t2b", [128, 2, 32], F32)
        tt1b, tt2b = Tk("t1b"), Tk("t2b")
        ystg = [sbs(st, "ystg%d" % i, [128, 1024], F32) for i in range(2)]
        tystg = [Tk("ystg%d" % i) for i in range(2)]
        yx = sbs(st, "yx", [128, 1024], F32)
        tyx = Tk("yx")
        yo = [sbs(st, "yo%d" % i, [128, 1024], BF16) for i in range(2)]
        tyo = [Tk("yo%d" % i) for i in range(2)]
        P.op("dve", lambda e: e.memset(G0, 0.0), writes=tG0)
        hb = 0
        for r in range(2):
            for quad in range(8):
                for n in range(8):
                    i = n if r == 0 else 7 - n
                    bank, tbank = psb[hb % 2], tps[hb % 2]
                    for j in range(4):
                        g = quad * 4 + j
                        gd = r * 32 + g
                        P.op("pe", lambda e, bank=bank, j=j, gd=gd, g=g, i=i, n=n: e.matmul(
                            bank[:, j * 128:(j + 1) * 128], lhsT=MATS[:, gd, 0, :], rhs=Ug[:, g, i:1024:8],
                            start=(j == 0), stop=(n == 0), skip_group_check=True), reads=[tMATS[gd], tUg[g]],
                            writes=[tbank])
                        if n > 0:
                            P.op("pe", lambda e, bank=bank, j=j, gd=gd, n=n: e.matmul(
                                bank[:, j * 128:(j + 1) * 128], lhsT=MATS[:, gd, 3, :], rhs=hq[(n - 1) % 2][:, j, :],
                                start=False, stop=True, skip_group_check=True), reads=[tMATS[gd], thq[(n - 1) % 2]],
                                writes=[tbank])
                    if n < 7:
                        P.op("act", lambda e, bank=bank, n=n: e.activation(
                            out=hq[n % 2].rearrange("p a b -> p (a b)"), in_=bank, func=AF.Copy),
                            reads=[tbank], writes=[thq[n % 2]])
                    else:
                        P.op("act", lambda e, bank=bank, quad=quad: e.activation(
                            out=SS[:, 0, quad * 4:(quad + 1) * 4, :].rearrange("p a b -> p (a b)"), in_=bank,
                            func=AF.Copy), reads=[tbank], writes=[tSS])
                    hb += 1
            for blk in range(8):
                bank, tbank = psb[2 + blk % 2], tps[2 + blk % 2]
                P.op("pe", lambda e, bank=bank, blk=blk: e.matmul(
                    bank, lhsT=Jm_f, rhs=SS[:, 0, blk * 4:(blk + 1) * 4, :].rearrange("p a b -> p (a b)"),
                    start=True, stop=True), reads=[tJm, tSS], writes=[tbank])
                P.op("dve", lambda e, bank=bank, blk=blk: e.tensor_copy(
                    out=SS[:, 1, blk * 4:(blk + 1) * 4, :].rearrange("p a b -> p (a b)"), in_=bank),
                    reads=[tbank], writes=[tSS])
            ks = list(range(128)) if r == 0 else list(range(127, -1, -1))
            arow = a64[:, r * 32:(r + 1) * 32]
            brow = b64[:, r * 32:(r + 1) * 32]
            nbrow = nb64[:, r * 32:(r + 1) * 32]
            a2 = arow.unsqueeze(1).to_broadcast([128, 2, 32])
            for kk in range(1, 128):
                kp, k = ks[kk - 1], ks[kk]
                P.op("dve", lambda e, kp=kp, a2=a2: e.tensor_tensor(out=t1b, in0=SS[:, :, :, kp], in1=a2, op=ALU.mult),
                     reads=[tSS, tab64], writes=[tt1b])
                P.op("dve", lambda e, kp=kp, brow=brow: e.tensor_tensor(out=t2b[:, 0, :], in0=SS[:, 1, :, kp], in1=brow,
                                                                       op=ALU.mult),
                     reads=[tSS, tab64], writes=[tt2b])
                P.op("dve", lambda e, kp=kp, nbrow=nbrow: e.tensor_tensor(out=t2b[:, 1, :], in0=SS[:, 0, :, kp],
                                                                         in1=nbrow, op=ALU.mult),
                     reads=[tSS, tab64], writes=[tt2b])
                P.op("dve", lambda e: e.tensor_tensor(out=t1b, in0=t1b, in1=t2b, op=ALU.add), reads=[tt1b, tt2b],
                     writes=[tt1b])
                P.op("dve", lambda e, k=k: e.tensor_tensor(out=SS[:, :, :, k], in0=SS[:, :, :, k], in1=t1b, op=ALU.add),
                     reads=[tSS, tt1b], writes=[tSS])
            dump("SS%d_0" % r, SS[:, 0], tSS, F32)
            dump("SS%d_1" % r, SS[:, 1], tSS, F32)
            if r == 0:
                P.op("dve", lambda e: e.tensor_copy(out=G0[:, 0, :, 1:128], in_=SS[:, 0, :, 0:127]), reads=[tSS],
                     writes=[tG0[0]])
            else:
                P.op("dve", lambda e: e.tensor_copy(out=G0[:, 1, :, 0:127], in_=SS[:, 0, :, 1:128]), reads=[tSS],
                     writes=[tG0[1]])
        hq2 = [hq[0].rearrange("p (a b) c -> p a b c", a=2), hq[1].rearrange("p (a b) c -> p a b c", a=2)]
        for g in range(32):
            Yb = [psb[2 + 2 * (g % 2)], psb[3 + 2 * (g % 2)]]
            tYb = [tps[2 + 2 * (g % 2)], tps[3 + 2 * (g % 2)]]
            cnt_i = [0] * 8
            ybank_started = [False, False]
            for n in range(8):
                bank, tbank = psb[hb % 2], tps[hb % 2]
                hb += 1
                for r in range(2):
                    i = n if r == 0 else 7 - n
                    gd = r * 32 + g
                    if n == 0:
                        prev, tprev = G0[:, r, g, :], tG0[r]
                    else:
                        prev, tprev = hq2[(n - 1) % 2][:, 0, r, :], thq[(n - 1) % 2]
                    yb, tyb = Yb[i // 4], tYb[i // 4]
                    yc = slice((i % 4) * 128, (i % 4 + 1) * 128)
                    u_i = Ug[:, g, i:1024:8]
                    st_ = not ybank_started[i // 4]
                    ybank_started[i // 4] = True
                    P.op("pe", lambda e, yb=yb, yc=yc, gd=gd, u_i=u_i, st_=st_: e.matmul(
                        yb[:, yc], lhsT=MATS[:, gd, 1, :], rhs=u_i, start=st_, stop=False, skip_group_check=True),
                        reads=[tMATS[gd], tUg[g]], writes=[tyb])
                    P.op("pe", lambda e, yb=yb, yc=yc, gd=gd, prev=prev, sp_=(cnt_i[i] == 1): e.matmul(
                        yb[:, yc], lhsT=MATS[:, gd, 2, :], rhs=prev, start=False, stop=sp_, skip_group_check=True),
                        reads=[tMATS[gd], tprev], writes=[tyb])
                    cnt_i[i] += 1
                    if n < 7:
                        P.op("pe", lambda e, bank=bank, r=r, gd=gd, u_i=u_i: e.matmul(
                            bank[:, r * 128:(r + 1) * 128], lhsT=MATS[:, gd, 0, :], rhs=u_i, start=(r == 0), stop=False,
                            skip_group_check=True), reads=[tMATS[gd], tUg[g]], writes=[tbank])
                        P.op("pe", lambda e, bank=bank, r=r, gd=gd, prev=prev: e.matmul(
                            bank[:, r * 128:(r + 1) * 128], lhsT=MATS[:, gd, 3, :], rhs=prev, start=False, stop=True,
                            skip_group_check=True), reads=[tMATS[gd], tprev], writes=[tbank])
                if n < 7:
                    P.op("act", lambda e, bank=bank, n=n: e.activation(
                        out=hq[n % 2][:, 0:2, :].rearrange("p a b -> p (a b)"), in_=bank[:, 0:256], func=AF.Copy),
                        reads=[tbank], writes=[thq[n % 2]])
            ys = ystg[g % 2]
            for b2 in range(2):
                P.op("act", lambda e, ys=ys, b2=b2, Yb=Yb: e.activation(
                    out=ys.rearrange("p (k i) -> p i k", i=8)[:, 4 * b2:4 * b2 + 4, :],
                    in_=Yb[b2].rearrange("p (i k) -> p i k", i=4), func=AF.Copy),
                    reads=[tYb[b2]], writes=[tystg[g % 2]])
            P.op("dve", lambda e, ys=ys: e.tensor_tensor(out=yx, in0=ys, in1=ys, op=ALU.mult), reads=[tystg[g % 2]],
                 writes=[tyx])
            P.op("dve", lambda e: e.tensor_scalar(out=yx, in0=yx, scalar1=0.044715, scalar2=1.0, op0=ALU.mult,
                                                  op1=ALU.add), reads=[tyx], writes=[tyx])
            P.op("dve", lambda e, ys=ys: e.tensor_tensor(out=yx, in0=yx, in1=ys, op=ALU.mult), reads=[tyx, tystg[g % 2]],
                 writes=[tyx])
            P.op("act", lambda e: e.activation(out=yx, in_=yx, func=AF.Sigmoid, scale=1.5957691216057308),
                 reads=[tyx], writes=[tyx])
            P.op("dve", lambda e, ys=ys, g=g: e.tensor_tensor(out=yo[g % 2], in0=yx, in1=ys, op=ALU.mult),
                 reads=[tyx, tystg[g % 2]], writes=[tyo[g % 2]])
            P.dma("act", Yd[g], yo[g % 2], tyo[g % 2], reads=[tyo[g % 2]], writes=[tYd])
        P.barrier()
    stM.close()
    if debug and "stop4" in debug:
        if "only4" in debug:
            P.emit([tYd] + dbg_tiles)
        else:
            P.emit([tYd, tATT, tQT, tKT, tVd, tZA, tUd, tZB, tGA, tGB] + dbg_tiles)
        return nc, dbg_outs

    with ExitStack() as st:
        wstf = [sbs(st, "wstf%d" % i, [128, 1024], F32) for i in range(2)]
        twstf = [Tk("wstf%d" % i) for i in range(2)]
        wa = sbs(st, "wa", [128, 4, 1024], BF16)
        wb = sbs(st, "wb", [128, 4, 1024], BF16)
        wgl = sbs(st, "wgl", [128, 4, 512], BF16)
        wo = sbs(st, "wo", [128, 8, 1024], BF16)
        twa, twb, twgl, two = Tk("wa"), Tk("wb"), Tk("wgl"), Tk("wo")
        wc = 0
        for (dst, tdst, src, nk, ncol) in ((wa, twa, wa_in, 4, 1024), (wb, twb, wb_in, 4, 1024),
                                            (wgl, twgl, wglu_in, 4, 512), (wo, two, wout_in, 8, 1024)):
            for k in range(nk):
                b = wc % 2
                wc += 1
                P.dma("sp", wstf[b][:, 0:ncol], src[k * 128:(k + 1) * 128, :], twstf[b], writes=[twstf[b]])
                P.op("act", lambda e, dst=dst, k=k, b=b, ncol=ncol: e.activation(out=dst[:, k, :], in_=wstf[b][:, 0:ncol],
                                                                              func=AF.Copy),
                     reads=[twstf[b]], writes=[tdst])
        fgb = sbs(st, "fgb", [128, 1024], F32)
        tfgb = Tk("fgb")
        P.dma("sp", fgb, fg_in.partition_broadcast(128), tfgb, writes=[tfgb])
        sYall = sbs(st, "sYall", [128, 4, 8, 1024], BF16)
        tsY = Tk("sYall")
        for g in range(32):
            P.dma("sp", sYall[(g % 8) * 16:(g % 8 + 1) * 16, g // 8, :, :],
                  Yd[g].rearrange("(t co) c -> co t c", co=16), tsY, reads=[tYd], writes=[tsY])
        sl_ = sbs(st, "sl_", [128, 4, 512], BF16)
        tsl_ = Tk("sl_")
        att = [sbs(st, "att%d" % i, [128, 4, 512], BF16) for i in range(2)]
        tatt = [Tk("att%d" % i) for i in range(2)]
        zb = [sbs(st, "zb%d" % i, [128, 4, 512], BF16) for i in range(2)]
        tzb = [Tk("zb%d" % i) for i in range(2)]
        gab = [sbs(st, "gab%d" % i, [128, 8, 512], BF16) for i in range(2)]
        tgab = [Tk("gab%d" % i) for i in range(2)]
        gbb = [sbs(st, "gbb%d" % i, [128, 8, 512], BF16) for i in range(2)]
        tgbb = [Tk("gbb%d" % i) for i in range(2)]
        gl = sbs(st, "gl", [128, 4, 512], BF16)
        tgl = Tk("gl")
        s2 = sbs(st, "s2", [128, 4, 512], BF16)
        ts2 = Tk("s2")
        ma = sbs(st, "ma", [128, 512], F32)
        mb = sbs(st, "mb", [128, 512], F32)
        tma, tmb = Tk("ma"), Tk("mb")
        mg = sbs(st, "mg", [128, 8, 512], BF16)
        tmg = Tk("mg")
        xr = [sbs(st, "xr%d" % i, [128, 1024], F32) for i in range(2)]
        txr = [Tk("xr%d" % i) for i in range(2)]
        yt = [sbs(st, "yt%d" % i, [128, 1024], F32) for i in range(2)]
        tyt = [Tk("yt%d" % i) for i in range(2)]
        junk5 = sbs(st, "junk5", [128, 1024], BF16)
        tjunk5 = Tk("junk5")
        ss5 = [sbs(st, "ss5_%d" % i, [128, 1], F32) for i in range(2)]
        tss5 = [Tk("ss5_%d" % i) for i in range(2)]
        ty_ = Tk("y")
        pcn = 0
        xc = 0
        for tb in range(NB):
            bb = tb % 2
            tsl = slice(tb * 512, (tb + 1) * 512)
            P.dma("sp", att[bb], ATT[:, tsl].rearrange("(k p) t -> p k t", p=128), tatt[bb], reads=[tATT],
                  writes=[tatt[bb]])
            P.dma("sp", zb[bb], ZB[:, tsl].rearrange("(k p) t -> p k t", p=128), tzb[bb], reads=[tZB], writes=[tzb[bb]])
            P.dma("sp", gab[bb], GA[:, tsl].rearrange("(k p) t -> p k t", p=128), tgab[bb], reads=[tGA],
                  writes=[tgab[bb]])
            P.dma("sp", gbb[bb], GB[:, tsl].rearrange("(k p) t -> p k t", p=128), tgbb[bb], reads=[tGB],
                  writes=[tgbb[bb]])
            for kc in range(4):
                P.op("dve", lambda e, kc=kc, tb=tb: e.tensor_copy(
                    out=sl_[:, kc, :].rearrange("p (c t) -> p c t", t=8),
                    in_=sYall[:, kc, :, tb * 64:(tb + 1) * 64].rearrange("p t c -> p c t")),
                    reads=[tsY], writes=[tsl_])
            for ct in range(4):
                pi = pcn % 6
                pcn += 1
                for kc in range(4):
                    P.op("pe", lambda e, pi=pi, kc=kc, ct=ct: e.matmul(
                        psb[pi], lhsT=wgl[:, kc, ct * 128:(ct + 1) * 128], rhs=sl_[:, kc, :], start=(kc == 0),
                        stop=(kc == 3)), reads=[twgl, tsl_], writes=[tps[pi]])
                P.op("act", lambda e, pi=pi, ct=ct: e.activation(out=gl[:, ct, :], in_=psb[pi], func=AF.Sigmoid),
                     reads=[tps[pi]], writes=[tgl])
            P.op("dve", lambda e: e.tensor_tensor(out=s2, in0=sl_, in1=gl, op=ALU.mult), reads=[tsl_, tgl], writes=[ts2])
            P.op("dve", lambda e, bb=bb: e.tensor_tensor(out=s2, in0=s2, in1=zb[bb], op=ALU.mult), reads=[ts2, tzb[bb]],
                 writes=[ts2])
            for ct in range(8):
                pa_i, pb_i = pcn % 6, (pcn + 1) % 6
                pcn += 2
                for kc in range(4):
                    P.op("pe", lambda e, pa_i=pa_i, kc=kc, ct=ct, bb=bb: e.matmul(
                        psb[pa_i], lhsT=wa[:, kc, ct * 128:(ct + 1) * 128], rhs=att[bb][:, kc, :], start=(kc == 0),
                        stop=(kc == 3)), reads=[twa, tatt[bb]], writes=[tps[pa_i]])
                for kc in range(4):
                    P.op("pe", lambda e, pb_i=pb_i, kc=kc, ct=ct: e.matmul(
                        psb[pb_i], lhsT=wb[:, kc, ct * 128:(ct + 1) * 128], rhs=s2[:, kc, :], start=(kc == 0),
                        stop=(kc == 3)), reads=[twb, ts2], writes=[tps[pb_i]])
                P.op("dve", lambda e, pa_i=pa_i, ct=ct, bb=bb: e.tensor_tensor(out=ma, in0=psb[pa_i], in1=gab[bb][:, ct, :],
                                                                              op=ALU.mult),
                     reads=[tps[pa_i], tgab[bb]], writes=[tma])
                P.op("dve", lambda e, pb_i=pb_i, ct=ct, bb=bb: e.tensor_tensor(out=mb, in0=psb[pb_i], in1=gbb[bb][:, ct, :],
                                                                              op=ALU.mult),
                     reads=[tps[pb_i], tgbb[bb]], writes=[tmb])
                P.op("dve", lambda e, ct=ct: e.tensor_tensor(out=mg[:, ct, :], in0=ma, in1=mb, op=ALU.add),
                     reads=[tma, tmb], writes=[tmg])
            for t4 in range(4):
                tt = tb * 4 + t4
                xb = xc % 2
                xc += 1
                P.dma("sp", xr[xb], x[tt * 128:(tt + 1) * 128, :], txr[xb], writes=[txr[xb]])
                for hf in range(2):
                    pi = pcn % 6
                    pcn += 1
                    for kc in range(8):
                        P.op("pe", lambda e, pi=pi, kc=kc, t4=t4, hf=hf: e.matmul(
                            psb[pi], lhsT=mg[:, kc, t4 * 128:(t4 + 1) * 128], rhs=wo[:, kc, hf * 512:(hf + 1) * 512],
                            start=(kc == 0), stop=(kc == 7)), reads=[tmg, two], writes=[tps[pi]])
                    P.op("dve", lambda e, pi=pi, hf=hf, xb=xb: e.tensor_tensor(
                        out=yt[xb][:, hf * 512:(hf + 1) * 512], in0=psb[pi], in1=xr[xb][:, hf * 512:(hf + 1) * 512],
                        op=ALU.add), reads=[tps[pi], txr[xb]], writes=[tyt[xb]])
                P.op("act", lambda e, xb=xb: e.activation(out=junk5, in_=yt[xb], func=AF.Square, accum_out=ss5[xb]),
                     reads=[tyt[xb]], writes=[tjunk5, tss5[xb]])
                P.op("dve", lambda e, xb=xb: e.tensor_scalar(out=ss5[xb], in0=ss5[xb], scalar1=1.0 / D, scalar2=EPS,
                                                             op0=ALU.mult, op1=ALU.add), reads=[tss5[xb]],
                     writes=[tss5[xb]])
                P.op("act", lambda e, xb=xb: e.activation(out=ss5[xb], in_=ss5[xb], func=AF.Ln),
                     reads=[tss5[xb]], writes=[tss5[xb]])
                P.op("act", lambda e, xb=xb: e.activation(out=ss5[xb], in_=ss5[xb], func=AF.Exp, scale=-0.5),
                     reads=[tss5[xb]], writes=[tss5[xb]])
                P.op("act", lambda e, xb=xb: e.activation(out=yt[xb], in_=yt[xb], func=AF.Copy, scale=ss5[xb]),
                     reads=[tyt[xb], tss5[xb]], writes=[tyt[xb]])
                P.op("dve", lambda e, xb=xb: e.tensor_tensor(out=xr[xb], in0=yt[xb], in1=fgb, op=ALU.mult),
                     reads=[tyt[xb], tfgb], writes=[txr[xb]])
                P.dma("act", y[tt * 128:(tt + 1) * 128, :], xr[xb], txr[xb], reads=[txr[xb]], writes=[ty_])
        P.barrier()
        P.emit([ty_])
        return nc, dbg_outs


def rope_tables():
    half = 32
    inv = (np.float32(10000.0) ** (-np.arange(half, dtype=np.float32) / half)).astype(np.float32)
    pos = np.arange(L, dtype=np.float32)
    ang = (pos[None, :] * inv[:, None]).astype(np.float32)
    c = np.cos(ang.astype(np.float64)).astype(np.float32)
    s = np.sin(ang.astype(np.float64)).astype(np.float32)
    return np.tile(c, (4, 1)), np.tile(s, (4, 1))


def band_mask():
    p = np.arange(128)[:, None]
    q = np.arange(256)[None, :]
    return (((q - p) >= 0) & ((q - p) <= 128)).astype(np.float32)


def ssm_layouts(inputs):
    f = np.float32
    def q2(a):
        t = a.reshape(64, 64).T
        return np.ascontiguousarray(np.concatenate([t, t], 0).astype(f))
    lamr = q2(inputs["lam_re"][0]); lami = q2(inputs["lam_im"][0])
    lstep = np.ascontiguousarray(np.broadcast_to(inputs["log_step"][0].reshape(1, 64), (128, 64)).astype(f))
    br = inputs["b_re"][0].reshape(64, 64, 16).transpose(1, 0, 2)
    bi = inputs["b_im"][0].reshape(64, 64, 16).transpose(1, 0, 2)
    cr = inputs["c_re"][0].reshape(64, 16, 64).transpose(2, 0, 1)
    ci = inputs["c_im"][0].reshape(64, 16, 64).transpose(2, 0, 1)
    cat = lambda a, b: np.ascontiguousarray(np.concatenate([a, b], 0).astype(f))
    ex = np.zeros((128, 2, 4, 8), f)
    t = np.arange(8, dtype=f)
    for r in range(2):
        tp = t if r == 0 else 7 - t
        ex[:, r, 0, :] = 7 - tp
        ex[:, r, 1, :] = -(7 - tp)
        ex[:, r, 2, :] = tp + 1
        ex[:, r, 3, 0:3] = (1, 8, 64)
    sg = np.ones((128, 2), f); sg[64:, 0] = -1; sg[:64, 1] = -1
    jp = np.zeros((128, 128), f); jm = np.zeros((128, 128), f)
    for n in range(64):
        jp[n, n + 64] = 1; jp[n + 64, n] = 1
        jm[n + 64, n] = -1; jm[n, n + 64] = 1
    d = inputs["d_skip"][0].reshape(32, 16)
    dcol = np.ascontiguousarray(np.tile(d.T, (8, 1)).astype(f))
    cm = np.zeros((128, 2, 128), f)
    for s_ in range(8):
        for t_ in range(8):
            if t_ >= s_:
                cm[s_ * 16:(s_ + 1) * 16, 0, t_ * 16:(t_ + 1) * 16] = 1
            if t_ <= s_:
                cm[s_ * 16:(s_ + 1) * 16, 1, t_ * 16:(t_ + 1) * 16] = 1
    return {"lamr_q": lamr, "lami_q": lami, "lstep_q": lstep, "ex_tab": ex.reshape(128, 2, 32), "sg_tab": sg,
            "ba_q": cat(br, bi), "bb_q": cat(bi, br), "ca_q": cat(cr, ci), "cb_q": cat(ci, cr),
            "dcol": dcol, "cmask": cm, "jm": jm, "jp": jp}


def make_in_maps(inputs):
    xs = [inputs["x_prompt"][i] for i in range(4)] + [inputs["x_sample"][i] for i in range(2)]
    xs = xs + [xs[0], xs[1]]
    cosT, sinT = rope_tables()
    common = {
        "w_in": np.ascontiguousarray(inputs["w_in"][0]),
        "norm_g": np.ascontiguousarray(inputs["norm_g"][0].reshape(8, 128).T),
        "ident": np.eye(128, dtype=np.float32),
        "cosT": cosT, "sinT": sinT,
        "mask01": band_mask(),
        **ssm_layouts(inputs),
        "w_a": np.ascontiguousarray(inputs["w_branch_a"][0]), "w_b": np.ascontiguousarray(inputs["w_branch_b"][0]),
        "w_glu": np.ascontiguousarray(inputs["w_glu"][0]), "w_out": np.ascontiguousarray(inputs["w_out"][0]),
        "final_g": np.ascontiguousarray(inputs["final_g"]),
    }
    return [dict(common, x=np.ascontiguousarray(xs[c])) for c in range(NCORES)]


def kernel(**inputs):
    nc, _ = build()
    in_maps = make_in_maps(inputs)
    res = run_bass_kernel_spmd(nc, in_maps, core_ids=list(range(NCORES)))
    ys = [np.asarray(r["y"]).reshape(L, D) for r in res.results]
    return (np.stack(ys[0:4]).astype(np.float32), np.stack(ys[4:6]).astype(np.float32))
```
